# Optimizing a Trainium2 kernel written in Bass

```python
import math
import jax, jax.numpy as jnp
from jax import lax
import numpy as np

D_MODEL = 1024
BATCH = 8
SEQ = 2048
DEPTH = 1
DEC_BATCH = 128
DEC_SEQ = 4
PAST_LEN = 16384
PAGE_SIZE = 128

D_POOL = D_MODEL // 4
POOL_WINDOWS = (2, 4, 8, 16)
N_POOL_GROUPS = len(POOL_WINDOWS)
POOL_GW = D_POOL // N_POOL_GROUPS
POOL_CTX = max(POOL_WINDOWS) - 1
D_RNN = D_MODEL // 2
RG_BLOCKS = 8
RG_BW = D_RNN // RG_BLOCKS
RNN_CONV = 4
RG_C = 8.0
XA_HEADS = 4
XA_HEAD_DIM = 64
D_XA = XA_HEADS * XA_HEAD_DIM
D_MIX = D_POOL + D_RNN + D_XA
D_IN = D_POOL + 2 * D_RNN + D_XA
N_MEM = 256
D_FF = 3 * D_MODEL
FFN_CONV = 3
EPS = 1e-6

kernel_name = "hybrid_pool_rglru_xattn_convffn_step"


def rms_norm(x, g):
    xf = x.astype(jnp.float32)
    y = xf * lax.rsqrt(jnp.mean(xf * xf, axis=-1, keepdims=True) + EPS)
    return (y * g.astype(jnp.float32)).astype(x.dtype)


def group_rms(x):
    xf = x.astype(jnp.float32)
    return xf * lax.rsqrt(jnp.mean(xf * xf, axis=-1, keepdims=True) + EPS)


def causal_dwconv(ctx, u, w, b):
    k = w.shape[0]
    t = u.shape[1]
    full = jnp.concatenate([ctx.astype(u.dtype), u], axis=1)
    out = b + sum(full[:, i:i + t] * w[i] for i in range(k))
    return out, full[:, t:]


def pool_mixer(ctx, u, start, w_pool, pool_scale):
    b, t, _ = u.shape
    full = jnp.concatenate([ctx.astype(u.dtype), u], axis=1)
    c = jnp.cumsum(full.astype(jnp.float32), axis=1)
    c = jnp.concatenate([jnp.zeros((b, 1, D_POOL), jnp.float32), c], axis=1)
    pos = start + jnp.arange(t)
    outs = []
    for g, w in enumerate(POOL_WINDOWS):
        cg = c[..., g * POOL_GW:(g + 1) * POOL_GW]
        s = cg[:, POOL_CTX + 1:POOL_CTX + 1 + t] - cg[:, POOL_CTX + 1 - w:POOL_CTX + 1 - w + t]
        cnt = jnp.minimum(pos + 1, w).astype(jnp.float32)[None, :, None]
        diff = s / cnt - u[..., g * POOL_GW:(g + 1) * POOL_GW].astype(jnp.float32)
        outs.append(jnp.einsum('btc,cd->btd', diff.astype(u.dtype), w_pool[g]))
    out = jnp.concatenate(outs, axis=-1) * pool_scale
    return out, full[:, t:]


def rg_lru(xc, h0, w_a, b_a, w_x, b_x, lam):
    b, t, _ = xc.shape
    xb = xc.reshape(b, t, RG_BLOCKS, RG_BW)
    r = jax.nn.sigmoid((jnp.einsum('btnc,ncd->btnd', xb, w_a).reshape(b, t, D_RNN) + b_a).astype(jnp.float32))
    i = jax.nn.sigmoid((jnp.einsum('btnc,ncd->btnd', xb, w_x).reshape(b, t, D_RNN) + b_x).astype(jnp.float32))
    log_a = -RG_C * r * jax.nn.softplus(-lam.astype(jnp.float32))
    a = jnp.exp(log_a)
    mult = jnp.sqrt(-jnp.expm1(2.0 * log_a))
    bt = mult * i * xc.astype(jnp.float32)
    bt = bt.at[:, 0].add(a[:, 0] * h0.astype(jnp.float32))

    def combine(l, rr):
        a1, b1 = l
        a2, b2 = rr
        return a1 * a2, a2 * b1 + b2

    _, h = lax.associative_scan(combine, (a, bt), axis=1)
    return h.astype(xc.dtype), h[:, -1].astype(xc.dtype)


def mem_kv(mem, g_mem, w_mk, w_mv):
    b, m, _ = mem.shape
    mn = rms_norm(mem, g_mem)
    k = (mn @ w_mk).reshape(b, m, XA_HEADS, XA_HEAD_DIM)
    v = (mn @ w_mv).reshape(b, m, XA_HEADS, XA_HEAD_DIM)
    return k, v


def cross_attn(q, k, v):
    b, t, _ = q.shape
    qh = q.reshape(b, t, XA_HEADS, XA_HEAD_DIM)
    s = jnp.einsum('bthd,bmhd->bhtm', qh, k).astype(jnp.float32) / math.sqrt(XA_HEAD_DIM)
    p = jax.nn.softmax(s, axis=-1).astype(q.dtype)
    o = jnp.einsum('bhtm,bmhd->bthd', p, v)
    return o.reshape(b, t, D_XA)


def layer(x, ctx_pool, ctx_conv, h0, ctx_ffn, mk, mv, start, p):
    xn = rms_norm(x, p['g_mix_norm'])
    proj = xn @ p['w_in']
    u_pool = proj[..., :D_POOL]
    x_rnn = proj[..., D_POOL:D_POOL + D_RNN]
    g_rnn = proj[..., D_POOL + D_RNN:D_POOL + 2 * D_RNN]
    q = proj[..., D_POOL + 2 * D_RNN:]
    o_pool, new_pool = pool_mixer(ctx_pool, u_pool, start, p['w_pool'], p['pool_scale'])
    xc, new_conv = causal_dwconv(ctx_conv, x_rnn, p['rnn_conv_w'], p['rnn_conv_b'])
    h, h_last = rg_lru(xc, h0, p['w_rg_a'], p['b_rg_a'], p['w_rg_x'], p['b_rg_x'], p['rg_lambda'])
    o_rnn = jax.nn.gelu(g_rnn, approximate=True) * h
    o_xa = cross_attn(q, mk, mv)
    mix = jnp.concatenate([group_rms(o_pool), group_rms(o_rnn), group_rms(o_xa)], axis=-1)
    mix = (mix * p['g_mix_out'].astype(jnp.float32)).astype(x.dtype)
    x = x + mix @ p['w_out']
    xn = rms_norm(x, p['g_ffn_norm'])
    gate = xn @ p['w_ff_gate']
    up = xn @ p['w_ff_up']
    gate_c, new_ffn = causal_dwconv(ctx_ffn, gate, p['ffn_conv_w'], p['ffn_conv_b'])
    x = x + (jax.nn.gelu(gate_c, approximate=True) * up) @ p['w_ff_down']
    return x, new_pool, new_conv, h_last, new_ffn


def setup_inputs(seed: int = 0) -> dict:
    key = jax.random.key(seed)
    ks = iter(jax.random.split(key, 40))
    f32 = jnp.float32

    def nrm(shape, scale):
        return jax.random.normal(next(ks), shape, f32) * scale

    def gain(shape):
        return 1.0 + nrm(shape, 0.02)

    a0 = jax.random.uniform(next(ks), (DEPTH, D_RNN), f32, 0.9, 0.999)
    a_root = a0 ** (1.0 / RG_C)
    rg_lambda = jnp.log(a_root) - jnp.log1p(-a_root)
    return {
        'x_prompt': nrm((BATCH, SEQ, D_MODEL), 1.0),
        'x_sample': nrm((DEC_BATCH, DEC_SEQ, D_MODEL), 1.0),
        'mem_prompt': nrm((BATCH, N_MEM, D_MODEL), 1.0),
        'state_pool': nrm((DEPTH, DEC_BATCH, POOL_CTX, D_POOL), 1.0),
        'state_rnn_conv': nrm((DEPTH, DEC_BATCH, RNN_CONV - 1, D_RNN), 1.0),
        'state_rnn_h': nrm((DEPTH, DEC_BATCH, D_RNN), 0.5),
        'state_ffn_conv': nrm((DEPTH, DEC_BATCH, FFN_CONV - 1, D_FF), 1.0),
        'cache_mem_k': nrm((DEPTH, DEC_BATCH, N_MEM, XA_HEADS, XA_HEAD_DIM), 1.0),
        'cache_mem_v': nrm((DEPTH, DEC_BATCH, N_MEM, XA_HEADS, XA_HEAD_DIM), 1.0),
        'g_mix_norm': gain((DEPTH, D_MODEL)),
        'w_in': nrm((DEPTH, D_MODEL, D_IN), D_MODEL ** -0.5),
        'w_pool': nrm((DEPTH, N_POOL_GROUPS, POOL_GW, POOL_GW), POOL_GW ** -0.5),
        'pool_scale': gain((DEPTH, D_POOL)),
        'rnn_conv_w': nrm((DEPTH, RNN_CONV, D_RNN), RNN_CONV ** -0.5),
        'rnn_conv_b': nrm((DEPTH, D_RNN), 0.01),
        'w_rg_a': nrm((DEPTH, RG_BLOCKS, RG_BW, RG_BW), RG_BW ** -0.5),
        'b_rg_a': nrm((DEPTH, D_RNN), 0.01),
        'w_rg_x': nrm((DEPTH, RG_BLOCKS, RG_BW, RG_BW), RG_BW ** -0.5),
        'b_rg_x': nrm((DEPTH, D_RNN), 0.01),
        'rg_lambda': rg_lambda,
        'g_mem_norm': gain((DEPTH, D_MODEL)),
        'w_mem_k': nrm((DEPTH, D_MODEL, D_XA), D_MODEL ** -0.5),
        'w_mem_v': nrm((DEPTH, D_MODEL, D_XA), D_MODEL ** -0.5),
        'g_mix_out': gain((DEPTH, D_MIX)),
        'w_out': nrm((DEPTH, D_MIX, D_MODEL), D_MIX ** -0.5),
        'g_ffn_norm': gain((DEPTH, D_MODEL)),
        'w_ff_gate': nrm((DEPTH, D_MODEL, D_FF), D_MODEL ** -0.5),
        'w_ff_up': nrm((DEPTH, D_MODEL, D_FF), D_MODEL ** -0.5),
        'ffn_conv_w': nrm((DEPTH, FFN_CONV, D_FF), FFN_CONV ** -0.5),
        'ffn_conv_b': nrm((DEPTH, D_FF), 0.01),
        'w_ff_down': nrm((DEPTH, D_FF, D_MODEL), D_FF ** -0.5),
        'g_final': gain((D_MODEL,)),
    }


def reference(x_prompt, x_sample, mem_prompt, state_pool, state_rnn_conv, state_rnn_h,
              state_ffn_conv, cache_mem_k, cache_mem_v, g_mix_norm, w_in, w_pool, pool_scale,
              rnn_conv_w, rnn_conv_b, w_rg_a, b_rg_a, w_rg_x, b_rg_x, rg_lambda, g_mem_norm,
              w_mem_k, w_mem_v, g_mix_out, w_out, g_ffn_norm, w_ff_gate, w_ff_up, ffn_conv_w,
              ffn_conv_b, w_ff_down, g_final):
    bp = x_prompt.shape[0]
    dt = x_prompt.dtype
    yp, ys = x_prompt, x_sample
    pp, pc, ph, pf, pk, pv = [], [], [], [], [], []
    sp, sc, sh, sf = [], [], [], []
    for l in range(DEPTH):
        p = {
            'g_mix_norm': g_mix_norm[l], 'w_in': w_in[l], 'w_pool': w_pool[l],
            'pool_scale': pool_scale[l], 'rnn_conv_w': rnn_conv_w[l], 'rnn_conv_b': rnn_conv_b[l],
            'w_rg_a': w_rg_a[l], 'b_rg_a': b_rg_a[l], 'w_rg_x': w_rg_x[l], 'b_rg_x': b_rg_x[l],
            'rg_lambda': rg_lambda[l], 'g_mix_out': g_mix_out[l], 'w_out': w_out[l],
            'g_ffn_norm': g_ffn_norm[l], 'w_ff_gate': w_ff_gate[l], 'w_ff_up': w_ff_up[l],
            'ffn_conv_w': ffn_conv_w[l], 'ffn_conv_b': ffn_conv_b[l], 'w_ff_down': w_ff_down[l],
        }
        mk, mv = mem_kv(mem_prompt, g_mem_norm[l], w_mem_k[l], w_mem_v[l])
        yp, n_pool, n_conv, n_h, n_ffn = layer(
            yp, jnp.zeros((bp, POOL_CTX, D_POOL), dt), jnp.zeros((bp, RNN_CONV - 1, D_RNN), dt),
            jnp.zeros((bp, D_RNN), dt), jnp.zeros((bp, FFN_CONV - 1, D_FF), dt), mk, mv, 0, p)
        pp.append(n_pool); pc.append(n_conv); ph.append(n_h); pf.append(n_ffn)
        pk.append(mk); pv.append(mv)
        ys, s_pool, s_conv, s_h, s_ffn = layer(
            ys, state_pool[l], state_rnn_conv[l], state_rnn_h[l], state_ffn_conv[l],
            cache_mem_k[l], cache_mem_v[l], PAST_LEN, p)
        sp.append(s_pool); sc.append(s_conv); sh.append(s_h); sf.append(s_ffn)
    y_prompt = rms_norm(yp, g_final)
    y_sample = rms_norm(ys, g_final)
    return (y_prompt, y_sample,
            jnp.stack(pp), jnp.stack(pc), jnp.stack(ph), jnp.stack(pf), jnp.stack(pk), jnp.stack(pv),
            jnp.stack(sp), jnp.stack(sc), jnp.stack(sh), jnp.stack(sf))
```

```python
import numpy as np
from contextlib import ExitStack
import concourse.bass as bass
import concourse.mybir as mybir
from concourse.bass_utils import run_bass_kernel_spmd

F32 = mybir.dt.float32
BF16 = mybir.dt.bfloat16
ALU = mybir.AluOpType
AF = mybir.ActivationFunctionType

ENGS = ['pe', 'act', 'dve', 'pool', 'sp']
N_DMA_SEMS = 32
EPS = 1e-6
NCORES = 8
KSTOP = 10 ** 9


class Prog:
    def __init__(self, nc):
        self.nc = nc
        self.ops = {e: [] for e in ENGS}
        self.key_w = {}
        self.key_r = {}
        self.waited = {e: {} for e in ENGS}
        self.dma_val = [0] * N_DMA_SEMS
        self.dma_rr = {'sp': 0, 'pool': 0, 'act': 0}
        self.signal = {e: set() for e in ENGS}

    def _deps(self, eng, reads, writes, is_dma):
        deps = []
        for k in reads:
            t = self.key_w.get(k)
            if t is not None:
                deps.append((t, 'raw'))
        for k in writes:
            t = self.key_w.get(k)
            if t is not None:
                deps.append((t, 'waw'))
            for t in self.key_r.get(k, {}).values():
                deps.append((t, 'war'))
        out = []
        w = self.waited[eng]
        for t, kind in deps:
            if t[0] == 'e':
                _, e2, idx = t
                if e2 == eng and not is_dma:
                    if eng == 'pe':
                        continue
                if w.get(('e', e2), -1) >= idx:
                    continue
                w[('e', e2)] = idx
                self.signal[e2].add(idx)
                out.append(t)
            else:
                _, s, val = t
                if w.get(('d', s), 0) >= val:
                    continue
                w[('d', s)] = val
                out.append(t)
        return out

    def _register(self, tok, reads, writes):
        for k in writes:
            self.key_w[k] = tok
            self.key_r[k] = {}
        for k in reads:
            d = self.key_r.setdefault(k, {})
            d[(tok[0], tok[1])] = tok

    def op(self, eng, fn, reads=(), writes=()):
        self.nrec = getattr(self, 'nrec', 0) + 1
        if self.nrec > KSTOP:
            return
        deps = self._deps(eng, reads, writes, False)
        idx = len(self.ops[eng])
        self.ops[eng].append(dict(fn=fn, deps=deps, dma=None))
        self._register(('e', eng, idx), reads, writes)

    def dma(self, eng, out, in_, reads=(), writes=(), after=()):
        self.nrec = getattr(self, 'nrec', 0) + 1
        if self.nrec > KSTOP:
            return
        half = N_DMA_SEMS // 2
        s = self.dma_rr[eng] + (half if eng == 'pool' else 0)
        self.dma_rr[eng] = (self.dma_rr[eng] + 1) % half
        deps = self._deps(eng, list(reads) + list(after), writes, True)
        prev = self.dma_val[s]
        w = self.waited[eng]
        if prev > 0 and w.get(('d', s), 0) < prev:
            w[('d', s)] = prev
            deps.append(('d', s, prev))
        val = prev + 16
        self.dma_val[s] = val
        self.ops[eng].append(dict(fn=lambda e: e.dma_start(out=out, in_=in_), deps=deps, dma=(s, val)))
        self._register(('d', s, val), reads, writes)

    def emit(self, stack):
        nc = self.nc
        esem = {e: stack.enter_context(nc.semaphore("s_" + e)) for e in ENGS}
        dsem = [stack.enter_context(nc.semaphore("d_%d" % i)) for i in range(N_DMA_SEMS)]
        rank = {}
        for e in ENGS:
            for r, idx in enumerate(sorted(self.signal[e])):
                rank[(e, idx)] = r + 1
        fin = [(s, self.dma_val[s]) for s in range(N_DMA_SEMS) if self.dma_val[s] > 0]

        def replay(e, engobj):
            for i, o in enumerate(self.ops[e]):
                for t in o['deps']:
                    if t[0] == 'e':
                        engobj.wait_ge(esem[t[1]], rank[(t[1], t[2])])
                    else:
                        engobj.wait_ge(dsem[t[1]], t[2])
                ins = o['fn'](engobj)
                if o['dma'] is not None:
                    ins.then_inc(dsem[o['dma'][0]], 16)
                elif i in self.signal[e]:
                    ins.then_inc(esem[e], 1)
            if e == 'sp':
                for s, v in fin:
                    engobj.wait_ge(dsem[s], v)

        with nc.Block() as block:
            @block.tensor
            def _(pe):
                replay('pe', pe)

            @block.scalar
            def _(act):
                replay('act', act)

            @block.vector
            def _(dve):
                replay('dve', dve)

            @block.gpsimd
            def _(pool):
                replay('pool', pool)

            @block.sync
            def _(sp):
                replay('sp', sp)


PV = {}
_r = 0
for _n, _rows in [('g_mix_norm', 8), ('pool_scale', 2), ('rnn_conv_w', 16), ('rnn_conv_b', 4), ('b_rg_a', 4),
                  ('b_rg_x', 4), ('rg_lambda', 4), ('g_mem_norm', 8), ('g_mix_out', 8), ('g_ffn_norm', 8),
                  ('ffn_conv_b', 24)]:
    PV[_n] = _r
    _r += _rows
PV_A_ROWS = _r
PV['ffn_conv_w'] = 128
PV_COLS = 128 + 72
PV_HBA, PV_HBX, PV_HC, PV_C2 = 200, 204, 208, 212
PV_TOT = 216


def build():
    nc = bass.Bass("TRN2", target_bir_lowering=False)

    def din(name, shape):
        return nc.dram_tensor(name, list(shape), F32, kind="ExternalInput").ap()

    def dout(name, shape):
        return nc.dram_tensor(name, list(shape), F32, kind="ExternalOutput").ap()

    def dint(name, shape):
        return nc.dram_tensor(name, list(shape), BF16, kind="Internal").ap()

    I = {}
    for n, s in [('xp', (2048, 1024)), ('xs', (64, 1024)), ('mem', (256, 1024)), ('st_pool', (240, 256)),
                 ('st_conv', (48, 512)), ('st_h', (16, 512)), ('st_ffn', (32, 3072)), ('ck', (16, 256, 256)),
                 ('cv', (16, 256, 256)),
                 ('g_mix_norm', (8, 128)), ('w_in', (1024, 1536)), ('w_pool', (4, 64, 64)), ('pool_scale', (2, 128)),
                 ('rnn_conv_w', (16, 128)), ('rnn_conv_b', (4, 128)), ('w_rg_a', (8, 64, 64)), ('b_rg_a', (4, 128)),
                 ('w_rg_x', (8, 64, 64)), ('b_rg_x', (4, 128)), ('rg_lambda', (4, 128)), ('g_mem_norm', (8, 128)),
                 ('w_mem_k', (1024, 256)), ('w_mem_v', (1024, 256)), ('g_mix_out', (8, 128)), ('w_out', (1024, 1024)),
                 ('g_ffn_norm', (8, 128)), ('w_ff_gate', (1024, 3072)), ('w_ff_up', (1024, 3072)),
                 ('ffn_conv_w', (72, 128)), ('ffn_conv_b', (24, 128)), ('w_ff_down', (3072, 1024)),
                 ('g_final', (1, 1024))]:
        I[n] = din(n, s)
    O = {}
    for n, s in [('y_p', (2048, 1024)), ('y_s', (64, 1024)), ('pool_p', (15, 256)), ('conv_p', (3, 512)),
                 ('h_p', (1, 512)), ('ffn_p', (2, 3072)), ('mk_p', (256, 256)), ('mv_p', (256, 256)),
                 ('pool_s', (16, 15, 256)), ('conv_s', (16, 3, 512)), ('h_s', (16, 512)), ('ffn_s', (16, 2, 3072))]:
        O[n] = dout(n, s)
    WB = {n: dint(n, s) for n, s in [('w_in_b', (1024, 1536)), ('w_out_b', (1024, 1024)), ('w_gate_b', (1024, 3072)),
                                     ('w_up_b', (1024, 3072)), ('w_down_b', (3072, 1024)), ('w_mk_b', (1024, 256)),
                                     ('w_mv_b', (1024, 256))]}

    with ExitStack() as st:
        def sb(name, shape, dt=F32):
            return st.enter_context(nc.sbuf_tensor(name, list(shape), dt))

        def pst(name, shape, dt=F32):
            return st.enter_context(nc.psum_tensor(name, list(shape), dt))

        P = Prog(nc)
        xtok = sb("xtok", [128, 4, 1024])
        xnb = [sb("xnb%d" % i, [128, 1024], BF16) for i in range(2)]
        xnT = sb("xnT", [128, 8, 512], BF16)
        upool = sb("upool", [128, 2, 527])
        xrnn = sb("xrnn", [128, 4, 515])
        gg = sb("gg", [128, 4, 512])
        qb = sb("qb", [128, 2, 512], BF16)
        S = [dict(xc=sb("S%d_xc" % i, [128, 512]), xcb=sb("S%d_xcb" % i, [128, 512], BF16),
                  thr=sb("S%d_thr" % i, [128, 512]), thi=sb("S%d_thi" % i, [128, 512]),
                  a2=sb("S%d_a2" % i, [128, 512])) for i in range(2)]
        o_pool = sb("o_pool", [128, 2, 512])
        o_xa = sb("o_xa", [128, 2, 512])
        sq = [sb("sq%d" % i, [128, 512], BF16) for i in range(2)]
        rbc = sb("rbc", [128, 3, 512])
        mixT = sb("mixT", [128, 8, 512], BF16)
        sA = sb("sA", [128, 2, 527]); sB = sb("sB", [128, 2, 527]); sC = sb("sC", [128, 527])
        diff = sb("diff", [128, 2, 512], BF16)
        KT = sb("KT", [128, 2, 256], BF16)
        Vz = sb("Vz", [128, 4, 2, 128], BF16)
        PT = [sb("PT%d" % i, [128, 2, 512], BF16) for i in range(2)]
        rd = sb("rd", [128, 512])
        kvf = [sb("kvf%d" % i, [128, 2, 256]) for i in range(2)]
        KTs = [sb("KTs%d" % i, [128, 2, 256], BF16) for i in range(2)]
        Vzs = [sb("Vzs%d" % i, [128, 4, 2, 128], BF16) for i in range(2)]
        gbuf = [sb("gbuf%d" % i, [128, 514]) for i in range(2)]
        ftmp = [sb("ftmp%d" % i, [128, 512]) for i in range(2)]
        hT = [sb("hT%d" % i, [128, 4, 512], BF16) for i in range(2)]
        gcar = sb("gcar", [128, 24, 2])
        gcar_s = sb("gcar_s", [128, 24, 16, 2])
        gfin = sb("gfin", [128, 1024])
        pv = sb("pv", [128, PV_TOT])
        ident = sb("ident", [128, 128], BF16); identf = sb("identf", [128, 128])
        ones_b = sb("ones_b", [128, 128], BF16)
        onesz = sb("onesz", [128, 2, 128], BF16)
        bdw = sb("bdw", [128, 10, 128], BF16)
        rc15 = sb("rc15", [128, 2, 15])
        hcar = sb("hcar", [128, 4])
        h0T = sb("h0T", [128, 4, 16])
        hls = sb("hls", [128, 4, 16])
        stg = sb("stg", [128, 128]); stg2 = [sb("stg2_%d" % i, [128, 128]) for i in range(2)]
        ss = sb("ss", [128, 8]); rstd = sb("rstd", [128, 8])
        t16 = sb("t16", [128, 16])
        gu = [sb("gu%d" % i, [128, 8192], BF16) for i in range(2)]
        dn = [sb("dn%d" % i, [128, 4096], BF16) for i in range(2)]
        psb = [pst("ps%d" % i, [128, 512]) for i in range(6)]
        ptr = pst("ptr", [128, 1024], BF16)
        ptr2 = pst("ptr2", [128, 1024], BF16)
        ptrs = [ptr, ptr2]
        bank_rr = [0]
        held = set()

        def bank(exclude=()):
            b = bank_rr[0]
            while b in exclude or b in held:
                b = (b + 1) % 6
            bank_rr[0] = (b + 1) % 6
            return b

        def act(out, in_, func, reads, writes, scale=None, bias=None, accum=None):
            kw = {}
            if scale is not None:
                kw['scale'] = scale
            if bias is not None:
                kw['bias'] = bias
            if accum is not None:
                kw['accum_out'] = accum
            P.op('act', lambda e: e.activation(out=out, in_=in_, func=func, **kw), reads, writes)

        def tsc(eng, out, in0, s1, s2, op0, op1, reads, writes):
            if s2 is None:
                P.op(eng, lambda e: e.tensor_scalar(out=out, in0=in0, scalar1=s1, scalar2=None, op0=op0), reads, writes)
            else:
                P.op(eng, lambda e: e.tensor_scalar(out=out, in0=in0, scalar1=s1, scalar2=s2, op0=op0, op1=op1), reads, writes)

        def stt(eng, out, in0, sc, in1, op0, op1, reads, writes):
            P.op(eng, lambda e: e.scalar_tensor_tensor(out=out, in0=in0, scalar=sc, in1=in1, op0=op0, op1=op1), reads, writes)

        def tt(eng, out, in0, in1, op, reads, writes):
            P.op(eng, lambda e: e.tensor_tensor(out=out, in0=in0, in1=in1, op=op), reads, writes)

        def cp(eng, out, in_, reads, writes):
            P.op(eng, lambda e: e.tensor_copy(out=out, in_=in_), reads, writes)

        def mset(eng, ap, v, writes):
            P.op(eng, lambda e: e.memset(ap, v), (), writes)

        def mm(out, lhsT, rhs, start, stop, reads, writes):
            P.op('pe', lambda e: e.matmul(out, lhsT=lhsT, rhs=rhs, start=start, stop=stop), reads, writes)

        def tr(out, in_, idn, reads, writes):
            P.op('pe', lambda e: e.transpose(out=out, in_=in_, identity=idn), reads, writes)

        def pvc(name, j=0):
            c = PV[name] + j
            return pv[:, c:c + 1]

        def rsqrt_inplace(ap, key):
            act(ap, ap, AF.Ln, [key], [key])
            act(ap, ap, AF.Exp, [key], [key], scale=-0.5)

        stg2_rr = [0]

        def rows_out(src_ap, ncols, src_keys, dmas):
            i = stg2_rr[0]
            stg2_rr[0] = 1 - i
            rb = bank()
            tr(psb[rb][0:ncols, 0:128], src_ap, identf[:], list(src_keys) + ['identf'], ['ps:%d' % rb])
            cp('dve', stg2[i][0:ncols, :], psb[rb][0:ncols, 0:128], ['ps:%d' % rb], ['stg2_%d' % i])
            for (dap, p0, step, n) in dmas:
                if step == 1:
                    sap = stg2[i][p0:p0 + n, :]
                else:
                    sap = stg2[i][p0:p0 + (n - 1) * step + 1:step, :]
                P.dma('sp', dap, sap, ['stg2_%d' % i], ())

        def cast(dst, src, key):
            P.dma('pool', dst, src, (), [key])

        NPASS = 5
        gu_seq = []
        d_seq = []
        for p in range(NPASS):
            gu_seq += [(p, 'in0'), (p, 'in1'), (p, 'out')] + [(p, 'gu%d' % sg) for sg in range(6)]
            d_seq += [(p, sg) for sg in range(6)]
        gu_next = [0]
        d_next = [0]
        gu_slot_of = {}
        d_slot_of = {}

        NDIRECT = 4

        def ring_load(p, dst, src32, src16, wbkey, wkeys, cast_pass):
            if p < NDIRECT:
                P.dma('pool', dst, src32.rearrange("(k p) n -> p k n", p=128), (), wkeys)
                if p == cast_pass:
                    P.dma('pool', src16, src32, (), [wbkey])
            else:
                P.dma('sp', dst, src16.rearrange("(k p) n -> p k n", p=128), [wbkey], wkeys)

        def gu_prefetch():
            i = gu_next[0]
            if i >= len(gu_seq):
                return
            gu_next[0] += 1
            s = i % 2
            item = gu_seq[i]
            gu_slot_of[item] = s
            p, k = item
            key = 'gu:%d' % s
            keyb = 'gub:%d' % s
            if k == 'in0' or k == 'in1':
                h = 0 if k == 'in0' else 1
                ring_load(p, gu[s][:, 0:6144].rearrange("p (k n) -> p k n", n=768), I['w_in'][:, h * 768:(h + 1) * 768],
                          WB['w_in_b'][:, h * 768:(h + 1) * 768], 'wb:' + k, [key, keyb], 2 + h)
            elif k == 'out':
                ring_load(p, gu[s][:, 0:8192].rearrange("p (k n) -> p k n", n=1024), I['w_out'], WB['w_out_b'], 'wb:out',
                          [key, keyb], 2)
            else:
                sg = int(k[2:])
                cs = slice(sg * 512, (sg + 1) * 512)
                ring_load(p, gu[s][:, 0:4096].rearrange("p (k n) -> p k n", n=512), I['w_ff_gate'][:, cs], WB['w_gate_b'][:, cs],
                          'wb:g%d' % sg, [key], sg % 4)
                ring_load(p, gu[s][:, 4096:8192].rearrange("p (k n) -> p k n", n=512), I['w_ff_up'][:, cs], WB['w_up_b'][:, cs],
                          'wb:u%d' % sg, [keyb], sg % 4)

        def d_prefetch():
            i = d_next[0]
            if i >= len(d_seq):
                return
            d_next[0] += 1
            s = i % 2
            item = d_seq[i]
            d_slot_of[item] = s
            p, sg = item
            rs = slice(sg * 512, (sg + 1) * 512)
            ring_load(p, dn[s][:, :].rearrange("p (k n) -> p k n", n=1024), I['w_ff_down'][rs, :], WB['w_down_b'][rs, :],
                      'wb:d%d' % sg, ['dn:%d' % s], sg % 4)

        gu_prefetch(); gu_prefetch()

        mset('pool', identf[:], 1.0, ['identf'])
        P.op('pool', lambda e: e.affine_select(out=identf[:], in_=identf[:], pattern=[[-1, 128]],
                                               compare_op=ALU.is_equal, fill=0.0, base=0, channel_multiplier=1),
             ['identf'], ['identf'])
        cp('pool', ident[:], identf[:], ['identf'], ['ident'])
        mset('pool', ones_b[:], 1.0, ['ones_b'])
        mset('pool', onesz[:], 0.0, ['onesz'])
        mset('pool', onesz[:, 0, 0:64], 1.0, ['onesz'])
        mset('pool', onesz[:, 1, 64:128], 1.0, ['onesz'])
        mset('pool', Vz[:], 0.0, ['Vz'])
        mset('pool', gcar[:], 0.0, ['gcar:%d' % c for c in range(24)])
        mset('pool', hcar[:], 0.0, ['hcar'])
        for c in range(2):
            for hh in range(2):
                w = (2, 4, 8, 16)[2 * c + hh]
                pr = slice(hh * 64, hh * 64 + 64)
                mset('pool', rc15[pr, c, :], 1.0 / w, ['rc15'])
                for t in range(w - 1):
                    mset('pool', rc15[pr, c, t:t + 1], 1.0 / (t + 1), ['rc15'])
        for j_ in range(4):
            st_ap, st_keys = [(gg[:, 0:2, :].rearrange("p a b -> p (a b)"), ['gg:0', 'gg:1']),
                              (gg[:, 2:4, :].rearrange("p a b -> p (a b)"), ['gg:2', 'gg:3']),
                              (o_pool[:, :, :].rearrange("p a b -> p (a b)"), ['o_pool:0', 'o_pool:1']),
                              (o_xa[:, :, :].rearrange("p a b -> p (a b)"), ['o_xa:0', 'o_xa:1'])][j_]
            P.dma('sp', st_ap, I['xp'][j_ * 128:(j_ + 1) * 128, :], (), st_keys)
        P.dma('sp', gfin[:], I['g_final'][0].partition_broadcast(128), (), ['gfin'])
        stA = xtok[:, 1, 512:640]
        stB = xtok[:, 1, 640:768]
        mset('dve', xtok[:, 1, 512:768], 0.0, ['xtok:1'])
        pst_keys = []
        for n in ['g_mix_norm', 'pool_scale', 'rnn_conv_w', 'rnn_conv_b', 'b_rg_a', 'b_rg_x', 'rg_lambda',
                  'g_mem_norm', 'g_mix_out', 'g_ffn_norm', 'ffn_conv_b']:
            r0 = PV[n]
            nr = I[n].shape[0]
            P.dma('sp', xtok[r0:r0 + nr, 1, 512:640], I[n], (), ['pst:' + n], after=['xtok:1'])
            pst_keys.append('pst:' + n)
        P.dma('sp', xtok[0:72, 1, 640:768], I['ffn_conv_w'], (), ['pst:fcw'], after=['xtok:1'])
        pst_keys.append('pst:fcw')
        tr(psb[0][:, 0:128], stA, identf[:], ['xtok:1', 'identf'] + pst_keys, ['ps:0'])
        tr(psb[0][:, 128:256], stB, identf[:], ['xtok:1', 'identf'] + pst_keys, ['ps:0'])
        cp('dve', pv[:, 0:256 - 56], psb[0][:, 0:200], ['ps:0'], ['pv'])
        tsc('dve', pv[:, PV_HBA:PV_HBA + 4], pv[:, PV['b_rg_a']:PV['b_rg_a'] + 4], 0.5, None, ALU.mult, None, ['pv'], ['pv'])
        tsc('dve', pv[:, PV_HBX:PV_HBX + 4], pv[:, PV['b_rg_x']:PV['b_rg_x'] + 4], 0.5, None, ALU.mult, None, ['pv'], ['pv'])
        act(pv[:, PV_HC:PV_HC + 4], pv[:, PV['rg_lambda']:PV['rg_lambda'] + 4], AF.Exp, ['pv'], ['pv'], scale=-1.0)
        act(pv[:, PV_HC:PV_HC + 4], pv[:, PV_HC:PV_HC + 4], AF.Ln, ['pv'], ['pv'], bias=1.0)
        tsc('dve', pv[:, PV_C2:PV_C2 + 4], pv[:, PV_HC:PV_HC + 4], -8.0, None, ALU.mult, None, ['pv'], ['pv'])
        tsc('dve', pv[:, PV_HC:PV_HC + 4], pv[:, PV_HC:PV_HC + 4], -4.0, None, ALU.mult, None, ['pv'], ['pv'])
        bst0 = xtok[:, 0, :]
        bst1 = xtok[:, 1, 0:256]
        mset('dve', xtok[:, 0, :], 0.0, ['xtok:0'])
        mset('dve', xtok[:, 1, 0:256], 0.0, ['xtok:1'])

        bds_keys = []

        def bd_dst(bi, hh):
            base = bst0 if bi < 8 else bst1
            b = bi if bi < 8 else bi - 8
            return base[hh * 64:hh * 64 + 64, b * 128 + hh * 64: b * 128 + hh * 64 + 64]

        for c in range(2):
            for hh in range(2):
                P.dma('sp', bd_dst(c, hh), I['w_pool'][2 * c + hh], (), ['bds:p%d%d' % (c, hh)], after=['xtok:0', 'xtok:1'])
                bds_keys.append('bds:p%d%d' % (c, hh))
        for c in range(4):
            for hh in range(2):
                P.dma('sp', bd_dst(2 + c, hh), I['w_rg_a'][2 * c + hh], (), ['bds:a%d%d' % (c, hh)], after=['xtok:0', 'xtok:1'])
                P.dma('sp', bd_dst(6 + c, hh), I['w_rg_x'][2 * c + hh], (), ['bds:x%d%d' % (c, hh)], after=['xtok:0', 'xtok:1'])
                bds_keys += ['bds:a%d%d' % (c, hh), 'bds:x%d%d' % (c, hh)]
        cp('dve', bdw[:, 0:8, :], xtok[:, 0, :].rearrange("p (b n) -> p b n", n=128), ['xtok:0'] + bds_keys, ['bdw'])
        cp('dve', bdw[:, 8:10, :], xtok[:, 1, 0:256].rearrange("p (b n) -> p b n", n=128), ['xtok:1'] + bds_keys, ['bdw'])


        def norm_units(ntt, tp, gname, srcs=None, dst=None, dkey='xnT'):
            if dst is None:
                dst = xnT
            if srcs is None:
                srcs = [(xtok[:, j, :], ['xtok:%d' % j]) for j in range(ntt)]
            for j in range(ntt):
                act(xnb[j % 2][:tp, :], srcs[j][0][:tp, :], AF.Square, srcs[j][1], ['xnb:%d' % (j % 2), 'ss'], accum=ss[:tp, j:j + 1])
            tsc('dve', rstd[:tp, 0:ntt], ss[:tp, 0:ntt], 1.0 / 1024, EPS, ALU.mult, ALU.add, ['ss'], ['rstd'])
            rsqrt_inplace(rstd[:tp, 0:ntt], 'rstd')
            yield

            def scale_(j):
                tsc('dve', xnb[j % 2][:tp, :], srcs[j][0][:tp, :], rstd[:tp, j:j + 1], None, ALU.mult, None,
                    list(srcs[j][1]) + ['rstd'], ['xnb:%d' % (j % 2)])

            def tr_(j):
                for k in range(8):
                    tr(ptrs[j % 2][:, k * 128:k * 128 + tp], xnb[j % 2][:tp, k * 128:(k + 1) * 128], ident[:tp, :tp],
                       ['xnb:%d' % (j % 2), 'ident'], ['ptr%d' % (j % 2)])

            def evac_(j):
                for k in range(8):
                    o = dst[:, k, j * 128:(j + 1) * 128]
                    i_ = ptrs[j % 2][:, k * 128:(k + 1) * 128]
                    if j % 2 == 0:
                        act(o, i_, AF.Identity, ['ptr0', 'pv'], ['%s:%d' % (dkey, k)], scale=pvc(gname, k))
                    else:
                        tsc('dve', o, i_, pvc(gname, k), None, ALU.mult, None, ['ptr1', 'pv'], ['%s:%d' % (dkey, k)])

            for step in range(ntt + 2):
                if step - 2 >= 0:
                    evac_(step - 2)
                if 0 <= step - 1 < ntt:
                    tr_(step - 1)
                if step < ntt:
                    scale_(step)
                yield

        def norm_and_transpose(ntt, tp, gname, srcs=None):
            for _ in norm_units(ntt, tp, gname, srcs):
                pass

        norm1_done = set()

        wmk = hT[0][:, :, :].rearrange("p a b -> p (a b)")[:, 0:2048].rearrange("p (k n) -> p k n", n=256)
        wmv = hT[1][:, :, :].rearrange("p a b -> p (a b)")[:, 0:2048].rearrange("p (k n) -> p k n", n=256)
        P.dma('pool', wmk, I['w_mem_k'].rearrange("(k p) n -> p k n", p=128), (), ['hT:0'])
        P.dma('pool', wmv, I['w_mem_v'].rearrange("(k p) n -> p k n", p=128), (), ['hT:1'])
        for j in range(2):
            P.dma('sp', xtok[:, 2 + j, :], I['mem'][j * 128:(j + 1) * 128, :], (), ['xtok:%d' % (2 + j)])
        memT = xnT
        for j in range(2):
            jj = 2 + j
            xb = xnb[j]
            xk = 'xnb:%d' % j
            act(xb[:, :], xtok[:, jj, :], AF.Square, ['xtok:%d' % jj], [xk, 'ss'], accum=ss[:, j:j + 1])
            tsc('dve', rstd[:, j:j + 1], ss[:, j:j + 1], 1.0 / 1024, EPS, ALU.mult, ALU.add, ['ss'], ['rstd'])
            rsqrt_inplace(rstd[:, j:j + 1], 'rstd')
            tsc('dve', xb[:, :], xtok[:, jj, :], rstd[:, j:j + 1], None, ALU.mult, None, ['xtok:%d' % jj, 'rstd'], [xk])
            for k in range(8):
                tr(ptr[:, k * 128:(k + 1) * 128], xb[:, k * 128:(k + 1) * 128], ident[:], [xk, 'ident'], ['ptr0'])
            for k in range(8):
                tsc('dve', memT[:, k, j * 128:(j + 1) * 128], ptr[:, k * 128:(k + 1) * 128], pvc('g_mem_norm', k), None,
                    ALU.mult, None, ['ptr0', 'pv'], ['xnT:%d' % k])
        def ffn_cast():
            pass

        for ch in range(2):
            b = bank()
            for k in range(8):
                mm(psb[b][:, 0:256], wmk[:, k, ch * 128:(ch + 1) * 128], memT[:, k, 0:256], k == 0, k == 7,
                   ['hT:0', 'xnT:%d' % k], ['ps:%d' % b])
            cp('dve', KT[:, ch, :], psb[b][:, 0:256], ['ps:%d' % b], ['KT'])
        for (wsrc, wkey, oname) in [(wmk, 'hT:0', 'mk_p'), (wmv, 'hT:1', 'mv_p')]:
            for mt in range(2):
                b = bank()
                for k in range(8):
                    mm(psb[b][:, 0:256], memT[:, k, mt * 128:(mt + 1) * 128], wsrc[:, k, :], k == 0, k == 7,
                       [wkey, 'xnT:%d' % k], ['ps:%d' % b])
                skey = 'S%d_xc' % mt
                cp('dve', S[mt]['xc'][:, 0:256], psb[b][:, 0:256], ['ps:%d' % b], [skey])
                P.dma('sp', O[oname][mt * 128:(mt + 1) * 128, :], S[mt]['xc'][:, 0:256], [skey], ())
                if oname == 'mv_p':
                    for h in range(4):
                        cp('pool', Vz[:, h, mt, (h % 2) * 64:(h % 2) * 64 + 64], S[mt]['xc'][:, h * 64:(h + 1) * 64],
                           [skey], ['Vz'])

        xstage = [(gg[:, 0:2, :].rearrange("p a b -> p (a b)"), ['gg:0', 'gg:1']),
                  (gg[:, 2:4, :].rearrange("p a b -> p (a b)"), ['gg:2', 'gg:3']),
                  (o_pool[:, :, :].rearrange("p a b -> p (a b)"), ['o_pool:0', 'o_pool:1']),
                  (o_xa[:, :, :].rearrange("p a b -> p (a b)"), ['o_xa:0', 'o_xa:1'])]

        def run_pass(pidx, kind, sti):
            if kind == 's':
                nseq, T, N, ntt, tp = 16, 4, 64, 1, 64
                xsrc = [I['xs']]
                ydst = [O['y_s']]
            else:
                nseq, T, N, ntt, tp = 1, 512, 512, 4, 128
                xsrc = [I['xp'][sti * 512 + j * 128: sti * 512 + (j + 1) * 128, :] for j in range(4)]
                ydst = [O['y_p'][sti * 512 + j * 128: sti * 512 + (j + 1) * 128, :] for j in range(4)]
            last = (kind == 'p' and sti == 3)
            WU = 15 + T
            WX = 3 + T
            WG = 2 + T

            def fv(ap2d):
                return ap2d.rearrange("p (s t) -> p s t", t=T)

            def hv(buf, c, W, lo, hi, pr=slice(0, 128)):
                b2 = buf[pr, c, 0:nseq * W] if c is not None else buf[pr, 0:nseq * W]
                return b2.rearrange("p (s w) -> p s w", w=W)[:, :, lo:hi]

            if kind == 'p' and sti == 0:
                mset('pool', upool[:, :, 0:15], 0.0, ['upool:0', 'upool:1'])
                mset('pool', xrnn[:, :, 0:3], 0.0, ['xrnn:%d' % c for c in range(4)])
            for j in range(ntt):
                P.dma('sp', xtok[:tp, j, :], xsrc[j], (), ['xtok:%d' % j])
            if kind == 's':
                for n_ in range(8):
                    if n_ < 2:
                        ap_, wk_ = kvf[n_][:, :, :], ['kvf:%d' % n_]
                    else:
                        t_ = 1 + (n_ - 2) // 2
                        h_ = (n_ - 2) % 2
                        ap_ = xtok[:, t_, h_ * 512:(h_ + 1) * 512].rearrange("p (a b) -> p a b", b=256)
                        wk_ = ['kvf:%d' % n_, 'xtok:%d' % t_]
                    P.dma('sp', ap_, I['ck'][n_].rearrange("(mt p) n -> p mt n", p=128), (), wk_)
            if pidx in norm1_done:
                xin_buf, xin_key = mixT, 'mixT'
            else:
                xin_buf, xin_key = xnT, 'xnT'
                if kind == 'p':
                    norm_and_transpose(ntt, tp, 'g_mix_norm', srcs=xstage)
                else:
                    norm_and_transpose(ntt, tp, 'g_mix_norm')
            for half in range(2):
                s = gu_slot_of[(pidx, 'in%d' % half)]
                wv = gu[s][:, 0:6144].rearrange("p (k n) -> p k n", n=768)
                for ml in range(6):
                    m = half * 6 + ml
                    b = bank()
                    for k in range(8):
                        mm(psb[b][:, 0:N], wv[:, k, ml * 128:(ml + 1) * 128], xin_buf[:, k, 0:N], k == 0, k == 7,
                           ['gu:%d' % s, 'gub:%d' % s, '%s:%d' % (xin_key, k)], ['ps:%d' % b])
                    pk = ['ps:%d' % b]
                    if m < 2:
                        act(hv(upool, m, WU, 15, 15 + T), fv(psb[b][:, 0:N]), AF.Copy, pk, ['upool:%d' % m])
                    elif m < 6:
                        act(hv(xrnn, m - 2, WX, 3, 3 + T), fv(psb[b][:, 0:N]), AF.Copy, pk, ['xrnn:%d' % (m - 2)])
                    elif m < 10:
                        act(gg[:, m - 6, 0:N], psb[b][:, 0:N], AF.Gelu_apprx_tanh, pk, ['gg:%d' % (m - 6)])
                    else:
                        cp('dve', qb[:, m - 10, 0:N], psb[b][:, 0:N], pk, ['qb:%d' % (m - 10)])
                if half == 0:
                    ffn_cast()
                gu_prefetch()

            groups = {0: ([(o_pool, 0, 'o_pool:0'), (o_pool, 1, 'o_pool:1')], 256, 0),
                      1: ([(gg, c, 'gg:%d' % c) for c in range(4)], 512, 2),
                      2: ([(o_xa, 0, 'o_xa:0'), (o_xa, 1, 'o_xa:1')], 256, 6)}
            sqi_ = [0]

            def rms_group(g):
                chunks, nfe, mi = groups[g]
                b = bank()
                for i, (buf, c, key) in enumerate(chunks):
                    sqb = sq[sqi_[0] % 2]
                    sqk = 'sq:%d' % (sqi_[0] % 2)
                    sqi_[0] += 1
                    act(sqb[:, 0:N], buf[:, c, 0:N], AF.Square, [key], [sqk])
                    mm(psb[b][:, 0:N], ones_b[:, :], sqb[:, 0:N], i == 0, i == len(chunks) - 1, ['ones_b', sqk], ['ps:%d' % b])
                tsc('dve', rbc[:, g, 0:N], psb[b][:, 0:N], 1.0 / nfe, EPS, ALU.mult, ALU.add, ['ps:%d' % b], ['rbc:%d' % g])
                rsqrt_inplace(rbc[:, g, 0:N], 'rbc:%d' % g)
                for (buf, c, key) in chunks:
                    stt('dve', mixT[:, mi, 0:N], buf[:, c, 0:N], pvc('g_mix_out', mi), rbc[:, g, 0:N], ALU.mult, ALU.mult,
                        [key, 'pv', 'rbc:%d' % g], ['mixT:%d' % mi])
                    mi += 1

            if kind == 'p':
                att_pending = []
                for ch in range(2):
                    bo = bank()
                    held.add(bo)
                    bd_ = bank()
                    held.add(bd_)
                    att_pending.append((ch, bo, bd_))
                    for e_ in range(2):
                        h = 2 * ch + e_
                        pr = slice(e_ * 64, e_ * 64 + 64)
                        pt = PT[h % 2]
                        ptk = 'PT:%d' % (h % 2)
                        for mt in range(2):
                            b = bank()
                            mm(psb[b][:, 0:N], KT[pr, ch, mt * 128:(mt + 1) * 128], qb[pr, ch, 0:N], True, True,
                               ['KT', 'qb:%d' % ch], ['ps:%d' % b])
                            act(pt[:, mt, 0:N], psb[b][:, 0:N], AF.Exp, ['ps:%d' % b], [ptk], scale=0.125)
                        for mt in range(2):
                            first = (e_ == 0 and mt == 0)
                            lastm = (e_ == 1 and mt == 1)
                            mm(psb[bo][:, 0:N], Vz[:, h, mt, :], pt[:, mt, 0:N], first, lastm, ['Vz', ptk], ['ps:%d' % bo])
                            mm(psb[bd_][:, 0:N], onesz[:, e_, :], pt[:, mt, 0:N], first, lastm, ['onesz', ptk], ['ps:%d' % bd_])

                def att_partB():
                    for (ch, bo, bd_) in att_pending:
                        P.op('dve', (lambda bd_=bd_: lambda e: e.reciprocal(out=rd[:, 0:N], in_=psb[bd_][:, 0:N]))(), ['ps:%d' % bd_], ['rd'])
                        tt('dve', o_xa[:, ch, 0:N], psb[bo][:, 0:N], rd[:, 0:N], ALU.mult, ['ps:%d' % bo, 'rd'], ['o_xa:%d' % ch])
                        held.discard(bo)
                        held.discard(bd_)
                    rms_group(2)
            else:
                bs = [bank(), bank()]
                NSLOT = 8

                def kvslot(n):
                    i = n % NSLOT
                    if i < 2:
                        return kvf[i][:, :, :], 'kvf:%d' % i, None
                    t_ = 1 + (i - 2) // 2
                    h_ = (i - 2) % 2
                    return (xtok[:, t_, h_ * 512:(h_ + 1) * 512].rearrange("p (a b) -> p a b", b=256), 'kvf:%d' % i, 'xtok:%d' % t_)

                def kvload(n):
                    ap, key, xk_ = kvslot(n)
                    src = I['ck'][n] if n < 16 else I['cv'][n - 16]
                    wk = [key] + ([xk_] if (xk_ is not None and n < NSLOT) else [])
                    P.dma('sp', ap, src.rearrange("(mt p) n -> p mt n", p=128), (), wk)

                def kprep(bb):
                    ap, kk, _ = kvslot(bb)
                    b = bank(exclude=bs)
                    for ch in range(2):
                        for mt in range(2):
                            tr(psb[b][:, (ch * 2 + mt) * 128:(ch * 2 + mt + 1) * 128], ap[:, mt, ch * 128:(ch + 1) * 128],
                               identf[:], [kk, 'identf'], ['ps:%d' % b])
                    dst = KTs[bb % 2][:, :, :].rearrange("p a b -> p (a b)")
                    if bb % 2 == 0:
                        cp('dve', dst, psb[b][:, 0:512], ['ps:%d' % b], ['KTs:%d' % (bb % 2)])
                    else:
                        act(dst, psb[b][:, 0:512], AF.Copy, ['ps:%d' % b], ['KTs:%d' % (bb % 2)])
                    kvload(bb + NSLOT)

                def kscores(bb):
                    cols = slice(bb * 4, bb * 4 + 4)
                    for e_ in range(2):
                        pr = slice(e_ * 64, e_ * 64 + 64)
                        for ch in range(2):
                            for mt in range(2):
                                c0 = (mt * 2 + ch) * 64 + bb * 4
                                mm(psb[bs[e_]][:, c0:c0 + 4], KTs[bb % 2][pr, ch, mt * 128:(mt + 1) * 128], qb[pr, ch, cols], True, True,
                                   ['KTs:%d' % (bb % 2), 'qb:%d' % ch], ['ps:%d' % bs[e_]])

                kprep(0)
                for bb in range(16):
                    if bb + 1 < 16:
                        kprep(bb + 1)
                    kscores(bb)
                for e_ in range(2):
                    act(PT[1][:, e_, 0:256], psb[bs[e_]][:, 0:256], AF.Exp, ['ps:%d' % bs[e_]], ['PT:1'], scale=0.125)
                bo = [bank(), bank()]
                bdn = [bank(), bank()]

                def vb16(bb):
                    return Vzs[bb % 2][:, :, :, :].rearrange("p a b c -> p (a b c)")[:, 0:512].rearrange("p (a b) -> p a b", b=256)

                def vprep(bb):
                    ap, kk, _ = kvslot(16 + bb)
                    if bb % 2 == 0:
                        act(vb16(bb), ap, AF.Copy, [kk], ['Vzs:%d' % (bb % 2)])
                    else:
                        cp('dve', vb16(bb), ap, [kk], ['Vzs:%d' % (bb % 2)])
                    if 16 + bb + NSLOT < 32:
                        kvload(16 + bb + NSLOT)

                def pvmm(bb):
                    vk = 'Vzs:%d' % (bb % 2)
                    vb = vb16(bb)
                    for ch in range(2):
                        for e_ in range(2):
                            oc = slice(e_ * 64 + bb * 4, e_ * 64 + bb * 4 + 4)
                            for mt in range(2):
                                c0 = (mt * 2 + ch) * 64 + bb * 4
                                mm(psb[bo[ch]][:, oc], vb[:, mt, ch * 128:(ch + 1) * 128], PT[1][:, e_, c0:c0 + 4], mt == 0, mt == 1,
                                   [vk, 'PT:1'], ['ps:%d' % bo[ch]])
                                mm(psb[bdn[ch]][:, oc], ones_b[:, :], PT[1][:, e_, c0:c0 + 4], mt == 0, mt == 1,
                                   ['ones_b', 'PT:1'], ['ps:%d' % bdn[ch]])

                vprep(0)
                for bb in range(16):
                    if bb + 1 < 16:
                        vprep(bb + 1)
                    pvmm(bb)
                for ch in range(2):
                    for e_ in range(2):
                        pr = slice(e_ * 64, e_ * 64 + 64)
                        P.op('dve', (lambda ch=ch, e_=e_, pr=pr: lambda e: e.reciprocal(
                            out=rd[pr, e_ * 64:(e_ + 1) * 64], in_=psb[bdn[ch]][pr, e_ * 64:(e_ + 1) * 64]))(), ['ps:%d' % bdn[ch]], ['rd'])
                        tt('dve', o_xa[pr, ch, 0:N], psb[bo[ch]][pr, e_ * 64:(e_ + 1) * 64], rd[pr, e_ * 64:(e_ + 1) * 64], ALU.mult,
                           ['ps:%d' % bo[ch], 'rd'], ['o_xa:%d' % ch])

                def att_partB():
                    rms_group(2)

            for c in range(2):
                uk = 'upool:%d' % c
                tt('dve', hv(sA, c, WU, 1, WU), hv(upool, c, WU, 1, WU), hv(upool, c, WU, 0, WU - 1), ALU.add, [uk], ['sA:%d' % c])
            hi = slice(64, 128)
            lo = slice(0, 64)
            tt('dve', hv(sB, 0, WU, 3, WU, hi), hv(sA, 0, WU, 3, WU, hi), hv(sA, 0, WU, 1, WU - 2, hi), ALU.add, ['sA:0'], ['sB:0'])
            tt('dve', hv(sB, 1, WU, 3, WU), hv(sA, 1, WU, 3, WU), hv(sA, 1, WU, 1, WU - 2), ALU.add, ['sA:1'], ['sB:1'])
            tt('dve', hv(sC, None, WU, 7, WU), hv(sB, 1, WU, 7, WU), hv(sB, 1, WU, 3, WU - 4), ALU.add, ['sB:1'], ['sC'])
            tt('dve', hv(sA, 1, WU, 15, WU, hi), hv(sC, None, WU, 15, WU, hi), hv(sC, None, WU, 7, WU - 8, hi), ALU.add, ['sC'], ['sA:1'])
            fin_src = [(0, lo, sA, 0, 'sA:0', 2), (0, hi, sB, 0, 'sB:0', 4), (1, lo, sC, None, 'sC', 8), (1, hi, sA, 1, 'sA:1', 16)]
            for (c, pr, buf, bc, bkey, w) in fin_src:
                stt('dve', fv(diff[pr, c, 0:N]), hv(buf, bc, WU, 15, WU, pr), 1.0 / w, hv(upool, c, WU, 15, WU, pr),
                    ALU.mult, ALU.subtract, [bkey, 'upool:%d' % c], ['diff:%d' % c])
            if kind == 'p' and sti == 0:
                for (c, pr, buf, bc, bkey, w) in fin_src:
                    src = buf[pr, bc, 15:30] if bc is not None else buf[pr, 15:30]
                    tt('dve', t16[pr, 0:15], src, rc15[pr, c, :], ALU.mult, [bkey, 'rc15'], ['t16'])
                    tt('dve', diff[pr, c, 0:15], t16[pr, 0:15], upool[pr, c, 15:30], ALU.subtract, ['t16', 'upool:%d' % c], ['diff:%d' % c])
            for c in range(2):
                b = bank()
                mm(psb[b][:, 0:N], bdw[:, c, :], diff[:, c, 0:N], True, True, ['bdw', 'diff:%d' % c], ['ps:%d' % b])
                act(o_pool[:, c, 0:N], psb[b][:, 0:N], AF.Identity, ['ps:%d' % b, 'pv'], ['o_pool:%d' % c], scale=pvc('pool_scale', c))
            if kind == 's':
                for c in range(2):
                    cp('pool', stg[:, 0:64].rearrange("p (s t) -> p s t", t=4), hv(upool, c, WU, 15, 19), ['upool:%d' % c], ['stg'])
                    rows_out(stg[:, 0:64], 64, ['stg'],
                             [(O['pool_s'][:, 11 + t, c * 128:(c + 1) * 128], t, 4, 16) for t in range(4)])
            else:
                if last:
                    cp('pool', stg[:, 0:30].rearrange("p (c r) -> p c r", r=15), upool[:, :, 512:527], ['upool:0', 'upool:1'], ['stg'])
                    rows_out(stg[:, 0:30], 30, ['stg'],
                             [(O['pool_p'][:, c * 128:(c + 1) * 128], c * 15, 1, 15) for c in range(2)])
                else:
                    for c in range(2):
                        cp('pool', upool[:, c, 0:15], upool[:, c, 512:527], ['upool:%d' % c], ['upool:%d' % c])

            att_partB()
            rms_group(0)
            ffn_cast()
            RGB = {
                0: dict(xc=(S[0]['xc'][:, 0:512], 'S0_xc'), xcb=(S[0]['xcb'][:, 0:512], 'S0_xcb'), thr=(S[0]['thr'][:, 0:512], 'S0_thr'),
                        thi=(S[0]['thi'][:, 0:512], 'S0_thi'), a2=(S[0]['a2'][:, 0:512], 'S0_a2')),
                1: dict(xc=(S[1]['xc'][:, 0:512], 'S1_xc'), xcb=(S[1]['xcb'][:, 0:512], 'S1_xcb'), thr=(S[1]['thr'][:, 0:512], 'S1_thr'),
                        thi=(S[1]['thi'][:, 0:512], 'S1_thi'), a2=(S[1]['a2'][:, 0:512], 'S1_a2')),
                2: dict(xc=(rd[:, 0:512], 'rd'), xcb=(diff[:, 0, :], 'diff:0'), thr=(sA[:, 0, 0:512], 'sA:0'),
                        thi=(sA[:, 1, 0:512], 'sA:1'), a2=(o_pool[:, 0, :], 'o_pool:0')),
                3: dict(xc=(sC[:, 0:512], 'sC'), xcb=(diff[:, 1, :], 'diff:1'), thr=(sB[:, 0, 0:512], 'sB:0'),
                        thi=(sB[:, 1, 0:512], 'sB:1'), a2=(o_pool[:, 1, :], 'o_pool:1')),
            }

            def rb(c, n):
                return RGB[c][n][0][:, 0:N]

            def rk(c, n):
                return RGB[c][n][1]

            C4 = range(4)
            for c in C4:
                act(fv(rb(c, 'xc')), hv(xrnn, c, WX, 0, T), AF.Identity, ['xrnn:%d' % c, 'pv'], [rk(c, 'xc')],
                    scale=pvc('rnn_conv_w', 0 * 4 + c), bias=pvc('rnn_conv_b', c))
            for c in C4:
                xcv = fv(rb(c, 'xc'))
                for tap in range(1, 4):
                    stt('dve', xcv, hv(xrnn, c, WX, tap, tap + T), pvc('rnn_conv_w', tap * 4 + c), xcv, ALU.mult, ALU.add,
                        ['xrnn:%d' % c, 'pv', rk(c, 'xc')], [rk(c, 'xc')])
            for c in C4:
                cp('pool', rb(c, 'xcb'), rb(c, 'xc'), [rk(c, 'xc')], [rk(c, 'xcb')])
                ffn_cast()
                if kind == 'p' and not last:
                    cp('pool', xrnn[:, c, 0:3], xrnn[:, c, 512:515], ['xrnn:%d' % c], ['xrnn:%d' % c])
            for pair in range(2):
                cs = [2 * pair, 2 * pair + 1]
                gb = {}
                for c in cs:
                    ba = bank()
                    mm(psb[ba][:, 0:N], bdw[:, 2 + c, :], rb(c, 'xcb'), True, True, ['bdw', rk(c, 'xcb')], ['ps:%d' % ba])
                    bx = bank()
                    mm(psb[bx][:, 0:N], bdw[:, 6 + c, :], rb(c, 'xcb'), True, True, ['bdw', rk(c, 'xcb')], ['ps:%d' % bx])
                    gb[c] = (ba, bx)
                for c in cs:
                    ba, bx = gb[c]
                    act(rb(c, 'thr'), psb[ba][:, 0:N], AF.Tanh, ['ps:%d' % ba, 'pv'], [rk(c, 'thr')], scale=0.5, bias=pv[:, PV_HBA + c:PV_HBA + c + 1])
                    act(rb(c, 'thi'), psb[bx][:, 0:N], AF.Tanh, ['ps:%d' % bx, 'pv'], [rk(c, 'thi')], scale=0.5, bias=pv[:, PV_HBX + c:PV_HBX + c + 1])
            for c in C4:
                hc_ = pv[:, PV_HC + c:PV_HC + c + 1]
                c2_ = pv[:, PV_C2 + c:PV_C2 + c + 1]
                act(rb(c, 'a2'), rb(c, 'thr'), AF.Exp, [rk(c, 'thr'), 'pv'], [rk(c, 'a2')], scale=c2_, bias=c2_)
                act(rb(c, 'thr'), rb(c, 'thr'), AF.Exp, [rk(c, 'thr'), 'pv'], [rk(c, 'thr')], scale=hc_, bias=hc_)
            for c in C4:
                tsc('dve', rb(c, 'a2'), rb(c, 'a2'), -1.0, 1.0 + 1.2e-7, ALU.mult, ALU.add, [rk(c, 'a2')], [rk(c, 'a2')])
                stt('dve', rb(c, 'thi'), rb(c, 'thi'), 1.0, rb(c, 'xc'), ALU.add, ALU.mult, [rk(c, 'thi'), rk(c, 'xc')], [rk(c, 'thi')])
            for c in C4:
                act(rb(c, 'a2'), rb(c, 'a2'), AF.Ln, [rk(c, 'a2')], [rk(c, 'a2')])
                act(rb(c, 'a2'), rb(c, 'a2'), AF.Exp, [rk(c, 'a2')], [rk(c, 'a2')], scale=0.5)
            for c in C4:
                stt('dve', rb(c, 'thi'), rb(c, 'thi'), 0.5, rb(c, 'a2'), ALU.mult, ALU.mult, [rk(c, 'thi'), rk(c, 'a2')], [rk(c, 'thi')])
                if kind == 's':
                    a0 = fv(rb(c, 'thr'))[:, :, 0]
                    b0 = fv(rb(c, 'thi'))[:, :, 0]
                    tt('dve', t16[:, 0:16], a0, h0T[:, c, :], ALU.mult, [rk(c, 'thr'), 'h0T'], ['t16'])
                    tt('dve', b0, b0, t16[:, 0:16], ALU.add, [rk(c, 'thi'), 't16'], [rk(c, 'thi')])
                    mset('dve', a0, 0.0, [rk(c, 'thr')])
                    init = 0.0
                else:
                    init = hcar[:, c:c + 1]
                P.op('dve', (lambda c=c, init=init: lambda e: e.tensor_tensor_scan(
                    out=rb(c, 'a2'), data0=rb(c, 'thr'), data1=rb(c, 'thi'), initial=init,
                    op0=ALU.mult, op1=ALU.add))(), [rk(c, 'thr'), rk(c, 'thi'), 'hcar'], [rk(c, 'a2')])
                if kind == 'p':
                    cp('dve', hcar[:, c:c + 1], rb(c, 'a2')[:, N - 1:N], [rk(c, 'a2')], ['hcar'])
                else:
                    cp('dve', hls[:, c, :], fv(rb(c, 'a2'))[:, :, 3], [rk(c, 'a2')], ['hls'])
                tt('dve', gg[:, c, 0:N], gg[:, c, 0:N], rb(c, 'a2'), ALU.mult, ['gg:%d' % c, rk(c, 'a2')], ['gg:%d' % c])
            if kind == 's':
                for half in range(2):
                    for ci in range(2):
                        c = half * 2 + ci
                        cp('pool', stg[:, ci * 48:(ci + 1) * 48].rearrange("p (s r) -> p s r", r=3), hv(xrnn, c, WX, 4, 7), ['xrnn:%d' % c], ['stg'])
                    dl = []
                    for ci in range(2):
                        c = half * 2 + ci
                        for r in range(3):
                            dl.append((O['conv_s'][:, r, c * 128:(c + 1) * 128], ci * 48 + r, 3, 16))
                    rows_out(stg[:, 0:96], 96, ['stg'], dl)
                cp('pool', stg[:, 0:64].rearrange("p (c s) -> p c s", s=16), hls[:, :, :], ['hls'], ['stg'])
                rows_out(stg[:, 0:64], 64, ['stg'], [(O['h_s'][:, c * 128:(c + 1) * 128], c * 16, 1, 16) for c in range(4)])
            elif last:
                cp('pool', stg[:, 0:12].rearrange("p (c r) -> p c r", r=3), xrnn[:, :, 512:515], ['xrnn:%d' % c for c in range(4)], ['stg'])
                cp('pool', stg[:, 12:16], hcar[:, :], ['hcar'], ['stg'])
                rows_out(stg[:, 0:16], 16, ['stg'],
                         [(O['conv_p'][:, c * 128:(c + 1) * 128], c * 3, 1, 3) for c in range(4)] +
                         [(O['h_p'][:, c * 128:(c + 1) * 128], 12 + c, 1, 1) for c in range(4)])

            rms_group(1)

            s = gu_slot_of[(pidx, 'out')]
            wv = gu[s][:, 0:8192].rearrange("p (k n) -> p k n", n=1024)
            for j in range(ntt):
                for half in range(2):
                    b = bank()
                    for k in range(8):
                        mm(psb[b][:tp, 0:512], mixT[:, k, j * 128:j * 128 + tp], wv[:, k, half * 512:(half + 1) * 512], k == 0, k == 7,
                           ['gu:%d' % s, 'gub:%d' % s, 'mixT:%d' % k], ['ps:%d' % b])
                    xs_ = xtok[:tp, j, half * 512:(half + 1) * 512]
                    tt('dve', xs_, xs_, psb[b][:tp, 0:512], ALU.add, ['xtok:%d' % j, 'ps:%d' % b], ['xtok:%d' % j])
            gu_prefetch()
            if d_next[0] == 0:
                d_prefetch(); d_prefetch()

            if kind == 'p' and sti < 3:
                for j in range(4):
                    P.dma('sp', xstage[j][0], I['xp'][(sti + 1) * 512 + j * 128:(sti + 1) * 512 + (j + 1) * 128, :], (), xstage[j][1])
            if last:
                state_loads()
                st_gen[0] = state_units()
            nu = [None]
            if kind == 'p' and sti < 3:
                nu[0] = norm_units(4, 128, 'g_mix_norm', xstage, mixT, 'mixT')
                norm1_done.add(pidx + 1)
            norm_and_transpose(ntt, tp, 'g_ffn_norm')

            def down_part(sg):
                ds = d_slot_of[(pidx, sg)]
                dv = dn[ds][:, :].rearrange("p (k n) -> p k n", n=1024)
                hb = hT[sg % 2]
                for j in range(ntt):
                    for half in range(2):
                        b = bank()
                        for cc in range(4):
                            mm(psb[b][:tp, 0:512], hb[:, cc, j * 128:j * 128 + tp], dv[:, cc, half * 512:(half + 1) * 512], cc == 0, cc == 3,
                               ['dn:%d' % ds, 'hT:%d' % (sg % 2)], ['ps:%d' % b])
                        xs_ = xtok[:tp, j, half * 512:(half + 1) * 512]
                        tt('dve', xs_, xs_, psb[b][:tp, 0:512], ALU.add, ['xtok:%d' % j, 'ps:%d' % b], ['xtok:%d' % j])
                d_prefetch()

            for sg in range(6):
                s = gu_slot_of[(pidx, 'gu%d' % sg)]
                gv = gu[s][:, 0:4096].rearrange("p (k n) -> p k n", n=512)
                uv = gu[s][:, 4096:8192].rearrange("p (k n) -> p k n", n=512)
                for cc in range(4):
                    c = sg * 4 + cc
                    par = c % 2
                    gk = 'gbuf:%d' % par
                    fk = 'ftmp:%d' % par
                    bg = bank()
                    for k in range(8):
                        mm(psb[bg][:, 0:N], gv[:, k, cc * 128:(cc + 1) * 128], xnT[:, k, 0:N], k == 0, k == 7,
                           ['gu:%d' % s, 'gub:%d' % s, 'xnT:%d' % k], ['ps:%d' % bg])
                    bu = bank()
                    for k in range(8):
                        mm(psb[bu][:, 0:N], uv[:, k, cc * 128:(cc + 1) * 128], xnT[:, k, 0:N], k == 0, k == 7,
                           ['gu:%d' % s, 'gub:%d' % s, 'xnT:%d' % k], ['ps:%d' % bu])
                    gb3 = gbuf[par][:, 0:nseq * WG].rearrange("p (s w) -> p s w", w=WG)
                    if kind == 's':
                        cp('pool', gb3[:, :, 0:2], gcar_s[:, c, :, :], ['gcs:%d' % c], [gk])
                    else:
                        cp('pool', gb3[:, :, 0:2], gcar[:, c:c + 1, :], ['gcar:%d' % c], [gk])
                    act(gb3[:, :, 2:2 + T], fv(psb[bg][:, 0:N]), AF.Copy, ['ps:%d' % bg], [gk])
                    fw_ = PV['ffn_conv_w']
                    ft3 = fv(ftmp[par][:, 0:N])
                    act(ft3, fv(psb[bg][:, 0:N]), AF.Identity, ['ps:%d' % bg, 'pv'], [fk],
                        scale=pv[:, fw_ + 48 + c:fw_ + 48 + c + 1], bias=pvc('ffn_conv_b', c))
                    stt('dve', ft3, gb3[:, :, 1:1 + T], pv[:, fw_ + 24 + c:fw_ + 24 + c + 1], ft3, ALU.mult, ALU.add, [gk, 'pv', fk], [fk])
                    stt('dve', ft3, gb3[:, :, 0:T], pv[:, fw_ + c:fw_ + c + 1], ft3, ALU.mult, ALU.add, [gk, 'pv', fk], [fk])
                    if kind == 's':
                        cp('pool', gcar_s[:, c, :, :], gb3[:, :, T:T + 2], [gk], ['gcs:%d' % c])
                    else:
                        cp('pool', gcar[:, c:c + 1, :], gb3[:, :, T:T + 2], [gk], ['gcar:%d' % c])
                    act(ftmp[par][:, 0:N], ftmp[par][:, 0:N], AF.Gelu_apprx_tanh, [fk], [fk])
                    tt('dve', hT[sg % 2][:, cc, 0:N], ftmp[par][:, 0:N], psb[bu][:, 0:N], ALU.mult, [fk, 'ps:%d' % bu], ['hT:%d' % (sg % 2)])
                    if cc == 1 and sg > 0:
                        down_part(sg - 1)
                    if last and sg >= 1:
                        for _ in range(2):
                            next(st_gen[0], None)
                    if nu[0] is not None and ((sg == 3 and cc >= 1) or sg == 4):
                        next(nu[0], None)
                gu_prefetch()
                if kind == 's':
                    src = gcar_s[:, sg * 4:(sg + 1) * 4, :, :].rearrange("p c s r -> p (c s r)")
                    dl = []
                    for ci in range(4):
                        c = sg * 4 + ci
                        for r in range(2):
                            dl.append((O['ffn_s'][:, r, c * 128:(c + 1) * 128], ci * 32 + r, 2, 16))
                    rows_out(src, 128, ['gcs:%d' % (sg * 4 + ci) for ci in range(4)], dl)
            if nu[0] is not None:
                for _ in nu[0]:
                    pass
            if last:
                for _ in st_gen[0]:
                    pass
            down_part(5)

            if kind == 's':
                pass
            elif last:
                src = gcar[:, :, :].rearrange("p c r -> p (c r)")
                dl = []
                for c in range(24):
                    for r in range(2):
                        dl.append((O['ffn_p'][r:r + 1, c * 128:(c + 1) * 128], c * 2 + r, 1, 1))
                rows_out(src, 48, ['gcar:%d' % c for c in range(24)], dl)

            for j in range(ntt):
                act(xnb[j % 2][:tp, :], xtok[:tp, j, :], AF.Square, ['xtok:%d' % j], ['xnb:%d' % (j % 2), 'ss'], accum=ss[:tp, 4 + j:5 + j])
            tsc('dve', rstd[:tp, 4:4 + ntt], ss[:tp, 4:4 + ntt], 1.0 / 1024, EPS, ALU.mult, ALU.add, ['ss'], ['rstd'])
            rsqrt_inplace(rstd[:tp, 4:4 + ntt], 'rstd')
            for j in range(ntt):
                stt('dve', xtok[:tp, j, :], xtok[:tp, j, :], rstd[:tp, 4 + j:5 + j], gfin[:tp, :], ALU.mult, ALU.mult,
                    ['xtok:%d' % j, 'rstd', 'gfin'], ['xtok:%d' % j])
                P.dma('sp', ydst[j], xtok[:tp, j, :], ['xtok:%d' % j], ())

        st_gen = [None]

        def state_loads():
            stf_a = gg[0:32, :, :].rearrange("p a b -> p (a b)")
            stf_b = rbc[0:32, 0:2, :].rearrange("p a b -> p (a b)")
            P.dma('sp', stf_a, I['st_ffn'][:, 0:2048], (), ['gg:0', 'gg:1', 'gg:2', 'gg:3'])
            P.dma('sp', stf_b, I['st_ffn'][:, 2048:3072], (), ['rbc:0', 'rbc:1'])
            P.dma('sp', o_pool[:, 0, 0:256], I['st_pool'][0:128, :], (), ['o_pool:0'])
            P.dma('sp', o_pool[0:112, 1, 0:256], I['st_pool'][128:240, :], (), ['o_pool:1'])
            P.dma('sp', o_xa[0:48, 0, :], I['st_conv'], (), ['o_xa:0'])
            P.dma('sp', o_xa[0:16, 1, :], I['st_h'], (), ['o_xa:1'])
            P.dma('sp', O['pool_s'][:, 0:11, :], I['st_pool'].rearrange("(b r) c -> b r c", r=15)[:, 4:15, :], (), ())

        def state_units():
            stf_a = gg[0:32, :, :].rearrange("p a b -> p (a b)")
            stf_b = rbc[0:32, 0:2, :].rearrange("p a b -> p (a b)")

            def st_tr(src_ap, nr, src_keys):
                b = bank()
                tr(psb[b][:, 0:nr], src_ap, identf[0:nr, 0:nr], list(src_keys) + ['identf'], ['ps:%d' % b])
                return b

            for c in range(2):
                flat = upool[:, c, 0:16 * 19].rearrange("p (s w) -> p s w", w=19)
                b = st_tr(o_pool[0:128, 0, c * 128:(c + 1) * 128], 128, ['o_pool:0'])
                cp('dve', flat[:, 0:8, 0:15], psb[b][:, 0:120].rearrange("p (s r) -> p s r", r=15), ['ps:%d' % b], ['upool:%d' % c])
                cp('dve', flat[:, 8, 0:8], psb[b][:, 120:128], ['ps:%d' % b], ['upool:%d' % c])
                yield
                b = st_tr(o_pool[0:112, 1, c * 128:(c + 1) * 128], 112, ['o_pool:1'])
                cp('dve', flat[:, 8, 8:15], psb[b][:, 0:7], ['ps:%d' % b], ['upool:%d' % c])
                cp('dve', flat[:, 9:16, 0:15], psb[b][:, 7:112].rearrange("p (s r) -> p s r", r=15), ['ps:%d' % b], ['upool:%d' % c])
                yield
            for c in range(4):
                b = st_tr(o_xa[0:48, 0, c * 128:(c + 1) * 128], 48, ['o_xa:0'])
                cp('dve', xrnn[:, c, 0:16 * 7].rearrange("p (s w) -> p s w", w=7)[:, :, 0:3],
                   psb[b][:, 0:48].rearrange("p (s r) -> p s r", r=3), ['ps:%d' % b], ['xrnn:%d' % c])
                yield
            for c in range(4):
                b = st_tr(o_xa[0:16, 1, c * 128:(c + 1) * 128], 16, ['o_xa:1'])
                cp('dve', h0T[:, c, :], psb[b][:, 0:16], ['ps:%d' % b], ['h0T'])
                yield
            for c in range(24):
                if c < 16:
                    src, sk_ = stf_a[:, c * 128:(c + 1) * 128], ['gg:%d' % (c // 4)]
                else:
                    src, sk_ = stf_b[:, (c - 16) * 128:(c - 15) * 128], ['rbc:%d' % ((c - 16) // 4)]
                b = st_tr(src, 32, sk_)
                act(gcar_s[:, c, :, :], psb[b][:, 0:32].rearrange("p (s r) -> p s r", r=2), AF.Copy, ['ps:%d' % b], ['gcs:%d' % c])
                yield

        for sti in range(4):
            run_pass(sti, 'p', sti)
        run_pass(4, 's', 0)
        P.emit(st)
    return nc


_NC_CACHE = {}


def kernel(**inp):
    f = lambda a: np.ascontiguousarray(np.asarray(a, dtype=np.float32))
    shared = {
        'g_mix_norm': f(inp['g_mix_norm'][0]).reshape(8, 128),
        'w_in': f(inp['w_in'][0]),
        'w_pool': f(inp['w_pool'][0]),
        'pool_scale': f(inp['pool_scale'][0]).reshape(2, 128),
        'rnn_conv_w': f(inp['rnn_conv_w'][0]).reshape(16, 128),
        'rnn_conv_b': f(inp['rnn_conv_b'][0]).reshape(4, 128),
        'w_rg_a': f(inp['w_rg_a'][0]),
        'b_rg_a': f(inp['b_rg_a'][0]).reshape(4, 128),
        'w_rg_x': f(inp['w_rg_x'][0]),
        'b_rg_x': f(inp['b_rg_x'][0]).reshape(4, 128),
        'rg_lambda': f(inp['rg_lambda'][0]).reshape(4, 128),
        'g_mem_norm': f(inp['g_mem_norm'][0]).reshape(8, 128),
        'w_mem_k': f(inp['w_mem_k'][0]),
        'w_mem_v': f(inp['w_mem_v'][0]),
        'g_mix_out': f(inp['g_mix_out'][0]).reshape(8, 128),
        'w_out': f(inp['w_out'][0]),
        'g_ffn_norm': f(inp['g_ffn_norm'][0]).reshape(8, 128),
        'w_ff_gate': f(inp['w_ff_gate'][0]),
        'w_ff_up': f(inp['w_ff_up'][0]),
        'ffn_conv_w': f(inp['ffn_conv_w'][0]).reshape(72, 128),
        'ffn_conv_b': f(inp['ffn_conv_b'][0]).reshape(24, 128),
        'w_ff_down': f(inp['w_ff_down'][0]),
        'g_final': f(inp['g_final']).reshape(1, 1024),
    }
    in_maps = []
    for c in range(NCORES):
        b = slice(16 * c, 16 * c + 16)
        m = dict(shared)
        m['xp'] = f(inp['x_prompt'][c])
        m['xs'] = f(inp['x_sample'][b]).reshape(64, 1024)
        m['mem'] = f(inp['mem_prompt'][c])
        m['st_pool'] = f(inp['state_pool'][0, b]).reshape(240, 256)
        m['st_conv'] = f(inp['state_rnn_conv'][0, b]).reshape(48, 512)
        m['st_h'] = f(inp['state_rnn_h'][0, b])
        m['st_ffn'] = f(inp['state_ffn_conv'][0, b]).reshape(32, 3072)
        m['ck'] = f(inp['cache_mem_k'][0, b]).reshape(16, 256, 256)
        m['cv'] = f(inp['cache_mem_v'][0, b]).reshape(16, 256, 256)
        in_maps.append(m)
    if 'nc' not in _NC_CACHE:
        _NC_CACHE['nc'] = build()
    nc = _NC_CACHE['nc']
    res = run_bass_kernel_spmd(nc, in_maps, core_ids=list(range(NCORES)))
    R = res.results

    def cat(name, shape):
        return np.stack([np.asarray(R[c][name], dtype=np.float32).reshape(shape) for c in range(NCORES)], axis=0)

    y_p = cat('y_p', (2048, 1024))
    y_s = cat('y_s', (16, 4, 1024)).reshape(128, 4, 1024)
    pool_p = cat('pool_p', (15, 256))[None]
    conv_p = cat('conv_p', (3, 512))[None]
    h_p = cat('h_p', (512,))[None]
    ffn_p = cat('ffn_p', (2, 3072))[None]
    mk_p = cat('mk_p', (256, 4, 64))[None]
    mv_p = cat('mv_p', (256, 4, 64))[None]
    pool_s = cat('pool_s', (16, 15, 256)).reshape(1, 128, 15, 256)
    conv_s = cat('conv_s', (16, 3, 512)).reshape(1, 128, 3, 512)
    h_s = cat('h_s', (16, 512)).reshape(1, 128, 512)
    ffn_s = cat('ffn_s', (16, 2, 3072)).reshape(1, 128, 2, 3072)
    return (y_p, y_s, pool_p, conv_p, h_p, ffn_p, mk_p, mv_p, pool_s, conv_s, h_s, ffn_s)
```

```python
import numpy as np
from contextlib import ExitStack
import concourse.bass as bass
import concourse.mybir as mybir
from concourse.bass_utils import run_bass_kernel_spmd

F32 = mybir.dt.float32
BF16 = mybir.dt.bfloat16
ALU = mybir.AluOpType
AF = mybir.ActivationFunctionType

ENGS = ['pe', 'act', 'dve', 'pool', 'sp']
N_DMA_SEMS = 32
EPS = 1e-6
NCORES = 8
KSTOP = 10 ** 9


class Prog:
    def __init__(self, nc):
        self.nc = nc
        self.ops = {e: [] for e in ENGS}
        self.key_w = {}
        self.key_r = {}
        self.waited = {e: {} for e in ENGS}
        self.dma_val = [0] * N_DMA_SEMS
        self.dma_rr = {'sp': 0, 'pool': 0, 'act': 0}
        self.signal = {e: set() for e in ENGS}

    def _deps(self, eng, reads, writes, is_dma):
        deps = []
        for k in reads:
            t = self.key_w.get(k)
            if t is not None:
                deps.append((t, 'raw'))
        for k in writes:
            t = self.key_w.get(k)
            if t is not None:
                deps.append((t, 'waw'))
            for t in self.key_r.get(k, {}).values():
                deps.append((t, 'war'))
        out = []
        w = self.waited[eng]
        for t, kind in deps:
            if t[0] == 'e':
                _, e2, idx = t
                if e2 == eng and not is_dma:
                    if eng == 'pe':
                        continue
                if w.get(('e', e2), -1) >= idx:
                    continue
                w[('e', e2)] = idx
                self.signal[e2].add(idx)
                out.append(t)
            else:
                _, s, val = t
                if w.get(('d', s), 0) >= val:
                    continue
                w[('d', s)] = val
                out.append(t)
        return out

    def _register(self, tok, reads, writes):
        for k in writes:
            self.key_w[k] = tok
            self.key_r[k] = {}
        for k in reads:
            d = self.key_r.setdefault(k, {})
            d[(tok[0], tok[1])] = tok

    def op(self, eng, fn, reads=(), writes=()):
        self.nrec = getattr(self, 'nrec', 0) + 1
        if self.nrec > KSTOP:
            return
        deps = self._deps(eng, reads, writes, False)
        idx = len(self.ops[eng])
        self.ops[eng].append(dict(fn=fn, deps=deps, dma=None))
        self._register(('e', eng, idx), reads, writes)

    def dma(self, eng, out, in_, reads=(), writes=(), after=()):
        self.nrec = getattr(self, 'nrec', 0) + 1
        if self.nrec > KSTOP:
            return
        half = N_DMA_SEMS // 2
        s = self.dma_rr[eng] + (half if eng == 'pool' else 0)
        self.dma_rr[eng] = (self.dma_rr[eng] + 1) % half
        deps = self._deps(eng, list(reads) + list(after), writes, True)
        prev = self.dma_val[s]
        w = self.waited[eng]
        if prev > 0 and w.get(('d', s), 0) < prev:
            w[('d', s)] = prev
            deps.append(('d', s, prev))
        val = prev + 16
        self.dma_val[s] = val
        self.ops[eng].append(dict(fn=lambda e: e.dma_start(out=out, in_=in_), deps=deps, dma=(s, val)))
        self._register(('d', s, val), reads, writes)

    def emit(self, stack):
        nc = self.nc
        esem = {e: stack.enter_context(nc.semaphore("s_" + e)) for e in ENGS}
        dsem = [stack.enter_context(nc.semaphore("d_%d" % i)) for i in range(N_DMA_SEMS)]
        rank = {}
        for e in ENGS:
            for r, idx in enumerate(sorted(self.signal[e])):
                rank[(e, idx)] = r + 1
        fin = [(s, self.dma_val[s]) for s in range(N_DMA_SEMS) if self.dma_val[s] > 0]

        def replay(e, engobj):
            for i, o in enumerate(self.ops[e]):
                for t in o['deps']:
                    if t[0] == 'e':
                        engobj.wait_ge(esem[t[1]], rank[(t[1], t[2])])
                    else:
                        engobj.wait_ge(dsem[t[1]], t[2])
                ins = o['fn'](engobj)
                if o['dma'] is not None:
                    ins.then_inc(dsem[o['dma'][0]], 16)
                elif i in self.signal[e]:
                    ins.then_inc(esem[e], 1)
            if e == 'sp':
                for s, v in fin:
                    engobj.wait_ge(dsem[s], v)

        with nc.Block() as block:
            @block.tensor
            def _(pe):
                replay('pe', pe)

            @block.scalar
            def _(act):
                replay('act', act)

            @block.vector
            def _(dve):
                replay('dve', dve)

            @block.gpsimd
            def _(pool):
                replay('pool', pool)

            @block.sync
            def _(sp):
                replay('sp', sp)


PV = {}
_r = 0
for _n, _rows in [('g_mix_norm', 8), ('pool_scale', 2), ('rnn_conv_w', 16), ('rnn_conv_b', 4), ('b_rg_a', 4),
                  ('b_rg_x', 4), ('rg_lambda', 4), ('g_mem_norm', 8), ('g_mix_out', 8), ('g_ffn_norm', 8),
                  ('ffn_conv_b', 24)]:
    PV[_n] = _r
    _r += _rows
PV_A_ROWS = _r
PV['ffn_conv_w'] = 128
PV_COLS = 128 + 72
PV_HBA, PV_HBX, PV_HC, PV_C2 = 200, 204, 208, 212
PV_TOT = 216


def build():
    nc = bass.Bass("TRN2", target_bir_lowering=False)

    def din(name, shape):
        return nc.dram_tensor(name, list(shape), F32, kind="ExternalInput").ap()

    def dout(name, shape):
        return nc.dram_tensor(name, list(shape), F32, kind="ExternalOutput").ap()

    def dint(name, shape):
        return nc.dram_tensor(name, list(shape), BF16, kind="Internal").ap()

    I = {}
    for n, s in [('xp', (2048, 1024)), ('xs', (64, 1024)), ('mem', (256, 1024)), ('st_pool', (240, 256)),
                 ('st_conv', (48, 512)), ('st_h', (16, 512)), ('st_ffn', (32, 3072)), ('ck', (16, 256, 256)),
                 ('cv', (16, 256, 256)),
                 ('g_mix_norm', (8, 128)), ('w_in', (1024, 1536)), ('w_pool', (4, 64, 64)), ('pool_scale', (2, 128)),
                 ('rnn_conv_w', (16, 128)), ('rnn_conv_b', (4, 128)), ('w_rg_a', (8, 64, 64)), ('b_rg_a', (4, 128)),
                 ('w_rg_x', (8, 64, 64)), ('b_rg_x', (4, 128)), ('rg_lambda', (4, 128)), ('g_mem_norm', (8, 128)),
                 ('w_mem_k', (1024, 256)), ('w_mem_v', (1024, 256)), ('g_mix_out', (8, 128)), ('w_out', (1024, 1024)),
                 ('g_ffn_norm', (8, 128)), ('w_ff_gate', (1024, 3072)), ('w_ff_up', (1024, 3072)),
                 ('ffn_conv_w', (72, 128)), ('ffn_conv_b', (24, 128)), ('w_ff_down', (3072, 1024)),
                 ('g_final', (1, 1024))]:
        I[n] = din(n, s)
    O = {}
    for n, s in [('y_p', (2048, 1024)), ('y_s', (64, 1024)), ('pool_p', (15, 256)), ('conv_p', (3, 512)),
                 ('h_p', (1, 512)), ('ffn_p', (2, 3072)), ('mk_p', (256, 256)), ('mv_p', (256, 256)),
                 ('pool_s', (16, 15, 256)), ('conv_s', (16, 3, 512)), ('h_s', (16, 512)), ('ffn_s', (16, 2, 3072))]:
        O[n] = dout(n, s)
    WB = {n: dint(n, s) for n, s in [('w_in_b', (1024, 1536)), ('w_out_b', (1024, 1024)), ('w_gate_b', (1024, 3072)),
                                     ('w_up_b', (1024, 3072)), ('w_down_b', (3072, 1024)), ('w_mk_b', (1024, 256)),
                                     ('w_mv_b', (1024, 256))]}

    with ExitStack() as st:
        def sb(name, shape, dt=F32):
            return st.enter_context(nc.sbuf_tensor(name, list(shape), dt))

        def pst(name, shape, dt=F32):
            return st.enter_context(nc.psum_tensor(name, list(shape), dt))

        P = Prog(nc)
        xtok = sb("xtok", [128, 4, 1024])
        xnb = [sb("xnb%d" % i, [128, 1024], BF16) for i in range(2)]
        xnT = sb("xnT", [128, 8, 512], BF16)
        upool = sb("upool", [128, 2, 527])
        xrnn = sb("xrnn", [128, 4, 515])
        gg = sb("gg", [128, 4, 512])
        qb = sb("qb", [128, 2, 512], BF16)
        S = [dict(xc=sb("S%d_xc" % i, [128, 512]), xcb=sb("S%d_xcb" % i, [128, 512], BF16),
                  thr=sb("S%d_thr" % i, [128, 512]), thi=sb("S%d_thi" % i, [128, 512]),
                  a2=sb("S%d_a2" % i, [128, 512])) for i in range(2)]
        o_pool = sb("o_pool", [128, 2, 512])
        o_xa = sb("o_xa", [128, 2, 512])
        sq = [sb("sq%d" % i, [128, 512], BF16) for i in range(2)]
        rbc = sb("rbc", [128, 3, 512])
        mixT = sb("mixT", [128, 8, 512], BF16)
        sA = sb("sA", [128, 2, 527]); sB = sb("sB", [128, 2, 527]); sC = sb("sC", [128, 527])
        diff = sb("diff", [128, 2, 512], BF16)
        KT = sb("KT", [128, 2, 256], BF16)
        Vz = sb("Vz", [128, 4, 2, 128], BF16)
        PT = [sb("PT%d" % i, [128, 2, 512], BF16) for i in range(2)]
        rd = sb("rd", [128, 512])
        kvf = [sb("kvf%d" % i, [128, 2, 256]) for i in range(2)]
        KTs = [sb("KTs%d" % i, [128, 2, 256], BF16) for i in range(2)]
        Vzs = [sb("Vzs%d" % i, [128, 4, 2, 128], BF16) for i in range(2)]
        gbuf = [sb("gbuf%d" % i, [128, 514]) for i in range(2)]
        ftmp = [sb("ftmp%d" % i, [128, 512]) for i in range(2)]
        hT = [sb("hT%d" % i, [128, 4, 512], BF16) for i in range(2)]
        gcar = sb("gcar", [128, 24, 2])
        gcar_s = sb("gcar_s", [128, 24, 16, 2])
        gfin = sb("gfin", [128, 1024])
        pv = sb("pv", [128, PV_TOT])
        ident = sb("ident", [128, 128], BF16); identf = sb("identf", [128, 128])
        ones_b = sb("ones_b", [128, 128], BF16)
        onesz = sb("onesz", [128, 2, 128], BF16)
        bdw = sb("bdw", [128, 10, 128], BF16)
        rc15 = sb("rc15", [128, 2, 15])
        hcar = sb("hcar", [128, 4])
        h0T = sb("h0T", [128, 4, 16])
        hls = sb("hls", [128, 4, 16])
        stg = sb("stg", [128, 128]); stg2 = [sb("stg2_%d" % i, [128, 128]) for i in range(2)]
        ss = sb("ss", [128, 8]); rstd = sb("rstd", [128, 8])
        t16 = sb("t16", [128, 16])
        gu = [sb("gu%d" % i, [128, 8192], BF16) for i in range(2)]
        dn = [sb("dn%d" % i, [128, 4096], BF16) for i in range(2)]
        psb = [pst("ps%d" % i, [128, 512]) for i in range(6)]
        ptr = pst("ptr", [128, 1024], BF16)
        ptr2 = pst("ptr2", [128, 1024], BF16)
        ptrs = [ptr, ptr2]
        bank_rr = [0]
        held = set()

        def bank(exclude=()):
            b = bank_rr[0]
            while b in exclude or b in held:
                b = (b + 1) % 6
            bank_rr[0] = (b + 1) % 6
            return b

        def act(out, in_, func, reads, writes, scale=None, bias=None, accum=None):
            kw = {}
            if scale is not None:
                kw['scale'] = scale
            if bias is not None:
                kw['bias'] = bias
            if accum is not None:
                kw['accum_out'] = accum
            P.op('act', lambda e: e.activation(out=out, in_=in_, func=func, **kw), reads, writes)

        def tsc(eng, out, in0, s1, s2, op0, op1, reads, writes):
            if s2 is None:
                P.op(eng, lambda e: e.tensor_scalar(out=out, in0=in0, scalar1=s1, scalar2=None, op0=op0), reads, writes)
            else:
                P.op(eng, lambda e: e.tensor_scalar(out=out, in0=in0, scalar1=s1, scalar2=s2, op0=op0, op1=op1), reads, writes)

        def stt(eng, out, in0, sc, in1, op0, op1, reads, writes):
            P.op(eng, lambda e: e.scalar_tensor_tensor(out=out, in0=in0, scalar=sc, in1=in1, op0=op0, op1=op1), reads, writes)

        def tt(eng, out, in0, in1, op, reads, writes):
            P.op(eng, lambda e: e.tensor_tensor(out=out, in0=in0, in1=in1, op=op), reads, writes)

        def cp(eng, out, in_, reads, writes):
            P.op(eng, lambda e: e.tensor_copy(out=out, in_=in_), reads, writes)

        def mset(eng, ap, v, writes):
            P.op(eng, lambda e: e.memset(ap, v), (), writes)

        def mm(out, lhsT, rhs, start, stop, reads, writes):
            P.op('pe', lambda e: e.matmul(out, lhsT=lhsT, rhs=rhs, start=start, stop=stop), reads, writes)

        def tr(out, in_, idn, reads, writes):
            P.op('pe', lambda e: e.transpose(out=out, in_=in_, identity=idn), reads, writes)

        def pvc(name, j=0):
            c = PV[name] + j
            return pv[:, c:c + 1]

        def rsqrt_inplace(ap, key):
            act(ap, ap, AF.Ln, [key], [key])
            act(ap, ap, AF.Exp, [key], [key], scale=-0.5)

        stg2_rr = [0]

        def rows_out(src_ap, ncols, src_keys, dmas):
            i = stg2_rr[0]
            stg2_rr[0] = 1 - i
            rb = bank()
            tr(psb[rb][0:ncols, 0:128], src_ap, identf[:], list(src_keys) + ['identf'], ['ps:%d' % rb])
            cp('dve', stg2[i][0:ncols, :], psb[rb][0:ncols, 0:128], ['ps:%d' % rb], ['stg2_%d' % i])
            for (dap, p0, step, n) in dmas:
                if step == 1:
                    sap = stg2[i][p0:p0 + n, :]
                else:
                    sap = stg2[i][p0:p0 + (n - 1) * step + 1:step, :]
                P.dma('sp', dap, sap, ['stg2_%d' % i], ())

        def cast(dst, src, key):
            P.dma('pool', dst, src, (), [key])

        NPASS = 5
        gu_seq = []
        d_seq = []
        for p in range(NPASS):
            gu_seq += [(p, 'in0'), (p, 'in1'), (p, 'out')] + [(p, 'gu%d' % sg) for sg in range(6)]
            d_seq += [(p, sg) for sg in range(6)]
        gu_next = [0]
        d_next = [0]
        gu_slot_of = {}
        d_slot_of = {}

        NDIRECT = 1

        def ring_load(p, dst, src32, src16, wbkey, wkeys, cast_pass):
            if p < NDIRECT:
                P.dma('pool', dst, src32.rearrange("(k p) n -> p k n", p=128), (), wkeys)
                P.dma('sp', src16.rearrange("(k p) n -> p k n", p=128), dst, wkeys, [wbkey])
            else:
                P.dma('sp', dst, src16.rearrange("(k p) n -> p k n", p=128), [wbkey], wkeys)

        def gu_prefetch():
            i = gu_next[0]
            if i >= len(gu_seq):
                return
            gu_next[0] += 1
            s = i % 2
            item = gu_seq[i]
            gu_slot_of[item] = s
            p, k = item
            key = 'gu:%d' % s
            keyb = 'gub:%d' % s
            if k == 'in0' or k == 'in1':
                h = 0 if k == 'in0' else 1
                ring_load(p, gu[s][:, 0:6144].rearrange("p (k n) -> p k n", n=768), I['w_in'][:, h * 768:(h + 1) * 768],
                          WB['w_in_b'][:, h * 768:(h + 1) * 768], 'wb:' + k, [key, keyb], h)
            elif k == 'out':
                ring_load(p, gu[s][:, 0:8192].rearrange("p (k n) -> p k n", n=1024), I['w_out'], WB['w_out_b'], 'wb:out',
                          [key, keyb], 0)
            else:
                sg = int(k[2:])
                cs = slice(sg * 512, (sg + 1) * 512)
                ring_load(p, gu[s][:, 0:4096].rearrange("p (k n) -> p k n", n=512), I['w_ff_gate'][:, cs], WB['w_gate_b'][:, cs],
                          'wb:g%d' % sg, [key], sg % 2)
                ring_load(p, gu[s][:, 4096:8192].rearrange("p (k n) -> p k n", n=512), I['w_ff_up'][:, cs], WB['w_up_b'][:, cs],
                          'wb:u%d' % sg, [keyb], sg % 2)

        def d_prefetch():
            i = d_next[0]
            if i >= len(d_seq):
                return
            d_next[0] += 1
            s = i % 2
            item = d_seq[i]
            d_slot_of[item] = s
            p, sg = item
            rs = slice(sg * 512, (sg + 1) * 512)
            ring_load(p, dn[s][:, :].rearrange("p (k n) -> p k n", n=1024), I['w_ff_down'][rs, :], WB['w_down_b'][rs, :],
                      'wb:d%d' % sg, ['dn:%d' % s], sg % 2)

        gu_prefetch(); gu_prefetch()

        mset('pool', identf[:], 1.0, ['identf'])
        P.op('pool', lambda e: e.affine_select(out=identf[:], in_=identf[:], pattern=[[-1, 128]],
                                               compare_op=ALU.is_equal, fill=0.0, base=0, channel_multiplier=1),
             ['identf'], ['identf'])
        cp('pool', ident[:], identf[:], ['identf'], ['ident'])
        mset('pool', ones_b[:], 1.0, ['ones_b'])
        mset('pool', onesz[:], 0.0, ['onesz'])
        mset('pool', onesz[:, 0, 0:64], 1.0, ['onesz'])
        mset('pool', onesz[:, 1, 64:128], 1.0, ['onesz'])
        mset('pool', Vz[:], 0.0, ['Vz'])
        mset('pool', gcar[:], 0.0, ['gcar:%d' % c for c in range(24)])
        mset('pool', hcar[:], 0.0, ['hcar'])
        for c in range(2):
            for hh in range(2):
                w = (2, 4, 8, 16)[2 * c + hh]
                pr = slice(hh * 64, hh * 64 + 64)
                mset('pool', rc15[pr, c, :], 1.0 / w, ['rc15'])
                for t in range(w - 1):
                    mset('pool', rc15[pr, c, t:t + 1], 1.0 / (t + 1), ['rc15'])
        for j_ in range(4):
            st_ap, st_keys = [(gg[:, 0:2, :].rearrange("p a b -> p (a b)"), ['gg:0', 'gg:1']),
                              (gg[:, 2:4, :].rearrange("p a b -> p (a b)"), ['gg:2', 'gg:3']),
                              (o_pool[:, :, :].rearrange("p a b -> p (a b)"), ['o_pool:0', 'o_pool:1']),
                              (o_xa[:, :, :].rearrange("p a b -> p (a b)"), ['o_xa:0', 'o_xa:1'])][j_]
            P.dma('sp', st_ap, I['xp'][j_ * 128:(j_ + 1) * 128, :], (), st_keys)
        P.dma('sp', gfin[:], I['g_final'][0].partition_broadcast(128), (), ['gfin'])
        stA = xtok[:, 1, 512:640]
        stB = xtok[:, 1, 640:768]
        mset('dve', xtok[:, 1, 512:768], 0.0, ['xtok:1'])
        pst_keys = []
        for n in ['g_mix_norm', 'pool_scale', 'rnn_conv_w', 'rnn_conv_b', 'b_rg_a', 'b_rg_x', 'rg_lambda',
                  'g_mem_norm', 'g_mix_out', 'g_ffn_norm', 'ffn_conv_b']:
            r0 = PV[n]
            nr = I[n].shape[0]
            P.dma('sp', xtok[r0:r0 + nr, 1, 512:640], I[n], (), ['pst:' + n], after=['xtok:1'])
            pst_keys.append('pst:' + n)
        P.dma('sp', xtok[0:72, 1, 640:768], I['ffn_conv_w'], (), ['pst:fcw'], after=['xtok:1'])
        pst_keys.append('pst:fcw')
        tr(psb[0][:, 0:128], stA, identf[:], ['xtok:1', 'identf'] + pst_keys, ['ps:0'])
        tr(psb[0][:, 128:256], stB, identf[:], ['xtok:1', 'identf'] + pst_keys, ['ps:0'])
        cp('dve', pv[:, 0:256 - 56], psb[0][:, 0:200], ['ps:0'], ['pv'])
        tsc('dve', pv[:, PV_HBA:PV_HBA + 4], pv[:, PV['b_rg_a']:PV['b_rg_a'] + 4], 0.5, None, ALU.mult, None, ['pv'], ['pv'])
        tsc('dve', pv[:, PV_HBX:PV_HBX + 4], pv[:, PV['b_rg_x']:PV['b_rg_x'] + 4], 0.5, None, ALU.mult, None, ['pv'], ['pv'])
        act(pv[:, PV_HC:PV_HC + 4], pv[:, PV['rg_lambda']:PV['rg_lambda'] + 4], AF.Exp, ['pv'], ['pv'], scale=-1.0)
        act(pv[:, PV_HC:PV_HC + 4], pv[:, PV_HC:PV_HC + 4], AF.Ln, ['pv'], ['pv'], bias=1.0)
        tsc('dve', pv[:, PV_C2:PV_C2 + 4], pv[:, PV_HC:PV_HC + 4], -8.0, None, ALU.mult, None, ['pv'], ['pv'])
        tsc('dve', pv[:, PV_HC:PV_HC + 4], pv[:, PV_HC:PV_HC + 4], -4.0, None, ALU.mult, None, ['pv'], ['pv'])
        bst0 = xtok[:, 0, :]
        bst1 = xtok[:, 1, 0:256]
        mset('dve', xtok[:, 0, :], 0.0, ['xtok:0'])
        mset('dve', xtok[:, 1, 0:256], 0.0, ['xtok:1'])

        bds_keys = []

        def bd_dst(bi, hh):
            base = bst0 if bi < 8 else bst1
            b = bi if bi < 8 else bi - 8
            return base[hh * 64:hh * 64 + 64, b * 128 + hh * 64: b * 128 + hh * 64 + 64]

        for c in range(2):
            for hh in range(2):
                P.dma('sp', bd_dst(c, hh), I['w_pool'][2 * c + hh], (), ['bds:p%d%d' % (c, hh)], after=['xtok:0', 'xtok:1'])
                bds_keys.append('bds:p%d%d' % (c, hh))
        for c in range(4):
            for hh in range(2):
                P.dma('sp', bd_dst(2 + c, hh), I['w_rg_a'][2 * c + hh], (), ['bds:a%d%d' % (c, hh)], after=['xtok:0', 'xtok:1'])
                P.dma('sp', bd_dst(6 + c, hh), I['w_rg_x'][2 * c + hh], (), ['bds:x%d%d' % (c, hh)], after=['xtok:0', 'xtok:1'])
                bds_keys += ['bds:a%d%d' % (c, hh), 'bds:x%d%d' % (c, hh)]
        cp('dve', bdw[:, 0:8, :], xtok[:, 0, :].rearrange("p (b n) -> p b n", n=128), ['xtok:0'] + bds_keys, ['bdw'])
        cp('dve', bdw[:, 8:10, :], xtok[:, 1, 0:256].rearrange("p (b n) -> p b n", n=128), ['xtok:1'] + bds_keys, ['bdw'])


        def norm_units(ntt, tp, gname, srcs=None, dst=None, dkey='xnT'):
            if dst is None:
                dst = xnT
            if srcs is None:
                srcs = [(xtok[:, j, :], ['xtok:%d' % j]) for j in range(ntt)]
            for j in range(ntt):
                act(xnb[j % 2][:tp, :], srcs[j][0][:tp, :], AF.Square, srcs[j][1], ['xnb:%d' % (j % 2), 'ss'], accum=ss[:tp, j:j + 1])
            tsc('dve', rstd[:tp, 0:ntt], ss[:tp, 0:ntt], 1.0 / 1024, EPS, ALU.mult, ALU.add, ['ss'], ['rstd'])
            rsqrt_inplace(rstd[:tp, 0:ntt], 'rstd')
            yield

            def scale_(j):
                tsc('dve', xnb[j % 2][:tp, :], srcs[j][0][:tp, :], rstd[:tp, j:j + 1], None, ALU.mult, None,
                    list(srcs[j][1]) + ['rstd'], ['xnb:%d' % (j % 2)])

            def tr_(j):
                for k in range(8):
                    tr(ptrs[j % 2][:, k * 128:k * 128 + tp], xnb[j % 2][:tp, k * 128:(k + 1) * 128], ident[:tp, :tp],
                       ['xnb:%d' % (j % 2), 'ident'], ['ptr%d' % (j % 2)])

            def evac_(j):
                for k in range(8):
                    o = dst[:, k, j * 128:(j + 1) * 128]
                    i_ = ptrs[j % 2][:, k * 128:(k + 1) * 128]
                    if j % 2 == 0:
                        act(o, i_, AF.Identity, ['ptr0', 'pv'], ['%s:%d' % (dkey, k)], scale=pvc(gname, k))
                    else:
                        tsc('dve', o, i_, pvc(gname, k), None, ALU.mult, None, ['ptr1', 'pv'], ['%s:%d' % (dkey, k)])

            for step in range(ntt + 2):
                if step - 2 >= 0:
                    evac_(step - 2)
                if 0 <= step - 1 < ntt:
                    tr_(step - 1)
                if step < ntt:
                    scale_(step)
                yield

        def norm_and_transpose(ntt, tp, gname, srcs=None):
            for _ in norm_units(ntt, tp, gname, srcs):
                pass

        norm1_done = set()

        wmk = hT[0][:, :, :].rearrange("p a b -> p (a b)")[:, 0:2048].rearrange("p (k n) -> p k n", n=256)
        wmv = hT[1][:, :, :].rearrange("p a b -> p (a b)")[:, 0:2048].rearrange("p (k n) -> p k n", n=256)
        P.dma('pool', wmk, I['w_mem_k'].rearrange("(k p) n -> p k n", p=128), (), ['hT:0'])
        P.dma('pool', wmv, I['w_mem_v'].rearrange("(k p) n -> p k n", p=128), (), ['hT:1'])
        for j in range(2):
            P.dma('sp', xtok[:, 2 + j, :], I['mem'][j * 128:(j + 1) * 128, :], (), ['xtok:%d' % (2 + j)])
        memT = xnT
        for j in range(2):
            jj = 2 + j
            xb = xnb[j]
            xk = 'xnb:%d' % j
            act(xb[:, :], xtok[:, jj, :], AF.Square, ['xtok:%d' % jj], [xk, 'ss'], accum=ss[:, j:j + 1])
            tsc('dve', rstd[:, j:j + 1], ss[:, j:j + 1], 1.0 / 1024, EPS, ALU.mult, ALU.add, ['ss'], ['rstd'])
            rsqrt_inplace(rstd[:, j:j + 1], 'rstd')
            tsc('dve', xb[:, :], xtok[:, jj, :], rstd[:, j:j + 1], None, ALU.mult, None, ['xtok:%d' % jj, 'rstd'], [xk])
            for k in range(8):
                tr(ptr[:, k * 128:(k + 1) * 128], xb[:, k * 128:(k + 1) * 128], ident[:], [xk, 'ident'], ['ptr0'])
            for k in range(8):
                tsc('dve', memT[:, k, j * 128:(j + 1) * 128], ptr[:, k * 128:(k + 1) * 128], pvc('g_mem_norm', k), None,
                    ALU.mult, None, ['ptr0', 'pv'], ['xnT:%d' % k])
        def ffn_cast():
            pass

        for ch in range(2):
            b = bank()
            for k in range(8):
                mm(psb[b][:, 0:256], wmk[:, k, ch * 128:(ch + 1) * 128], memT[:, k, 0:256], k == 0, k == 7,
                   ['hT:0', 'xnT:%d' % k], ['ps:%d' % b])
            cp('dve', KT[:, ch, :], psb[b][:, 0:256], ['ps:%d' % b], ['KT'])
        for (wsrc, wkey, oname) in [(wmk, 'hT:0', 'mk_p'), (wmv, 'hT:1', 'mv_p')]:
            for mt in range(2):
                b = bank()
                for k in range(8):
                    mm(psb[b][:, 0:256], memT[:, k, mt * 128:(mt + 1) * 128], wsrc[:, k, :], k == 0, k == 7,
                       [wkey, 'xnT:%d' % k], ['ps:%d' % b])
                skey = 'S%d_xc' % mt
                cp('dve', S[mt]['xc'][:, 0:256], psb[b][:, 0:256], ['ps:%d' % b], [skey])
                P.dma('sp', O[oname][mt * 128:(mt + 1) * 128, :], S[mt]['xc'][:, 0:256], [skey], ())
                if oname == 'mv_p':
                    for h in range(4):
                        cp('pool', Vz[:, h, mt, (h % 2) * 64:(h % 2) * 64 + 64], S[mt]['xc'][:, h * 64:(h + 1) * 64],
                           [skey], ['Vz'])

        xstage = [(gg[:, 0:2, :].rearrange("p a b -> p (a b)"), ['gg:0', 'gg:1']),
                  (gg[:, 2:4, :].rearrange("p a b -> p (a b)"), ['gg:2', 'gg:3']),
                  (o_pool[:, :, :].rearrange("p a b -> p (a b)"), ['o_pool:0', 'o_pool:1']),
                  (o_xa[:, :, :].rearrange("p a b -> p (a b)"), ['o_xa:0', 'o_xa:1'])]

        def run_pass(pidx, kind, sti):
            if kind == 's':
                nseq, T, N, ntt, tp = 16, 4, 64, 1, 64
                xsrc = [I['xs']]
                ydst = [O['y_s']]
            else:
                nseq, T, N, ntt, tp = 1, 512, 512, 4, 128
                xsrc = [I['xp'][sti * 512 + j * 128: sti * 512 + (j + 1) * 128, :] for j in range(4)]
                ydst = [O['y_p'][sti * 512 + j * 128: sti * 512 + (j + 1) * 128, :] for j in range(4)]
            last = (kind == 'p' and sti == 3)
            WU = 15 + T
            WX = 3 + T
            WG = 2 + T

            def fv(ap2d):
                return ap2d.rearrange("p (s t) -> p s t", t=T)

            def hv(buf, c, W, lo, hi, pr=slice(0, 128)):
                b2 = buf[pr, c, 0:nseq * W] if c is not None else buf[pr, 0:nseq * W]
                return b2.rearrange("p (s w) -> p s w", w=W)[:, :, lo:hi]

            if kind == 'p' and sti == 0:
                mset('pool', upool[:, :, 0:15], 0.0, ['upool:0', 'upool:1'])
                mset('pool', xrnn[:, :, 0:3], 0.0, ['xrnn:%d' % c for c in range(4)])
            for j in range(ntt):
                P.dma('sp', xtok[:tp, j, :], xsrc[j], (), ['xtok:%d' % j])
            if kind == 's':
                for n_ in range(8):
                    if n_ < 2:
                        ap_, wk_ = kvf[n_][:, :, :], ['kvf:%d' % n_]
                    else:
                        t_ = 1 + (n_ - 2) // 2
                        h_ = (n_ - 2) % 2
                        ap_ = xtok[:, t_, h_ * 512:(h_ + 1) * 512].rearrange("p (a b) -> p a b", b=256)
                        wk_ = ['kvf:%d' % n_, 'xtok:%d' % t_]
                    P.dma('sp', ap_, I['ck'][n_].rearrange("(mt p) n -> p mt n", p=128), (), wk_)
            if pidx in norm1_done:
                xin_buf, xin_key = mixT, 'mixT'
            else:
                xin_buf, xin_key = xnT, 'xnT'
                if kind == 'p':
                    norm_and_transpose(ntt, tp, 'g_mix_norm', srcs=xstage)
                else:
                    norm_and_transpose(ntt, tp, 'g_mix_norm')
            for half in range(2):
                s = gu_slot_of[(pidx, 'in%d' % half)]
                wv = gu[s][:, 0:6144].rearrange("p (k n) -> p k n", n=768)
                for ml in range(6):
                    m = half * 6 + ml
                    b = bank()
                    for k in range(8):
                        mm(psb[b][:, 0:N], wv[:, k, ml * 128:(ml + 1) * 128], xin_buf[:, k, 0:N], k == 0, k == 7,
                           ['gu:%d' % s, 'gub:%d' % s, '%s:%d' % (xin_key, k)], ['ps:%d' % b])
                    pk = ['ps:%d' % b]
                    if m < 2:
                        act(hv(upool, m, WU, 15, 15 + T), fv(psb[b][:, 0:N]), AF.Copy, pk, ['upool:%d' % m])
                    elif m < 6:
                        act(hv(xrnn, m - 2, WX, 3, 3 + T), fv(psb[b][:, 0:N]), AF.Copy, pk, ['xrnn:%d' % (m - 2)])
                    elif m < 10:
                        act(gg[:, m - 6, 0:N], psb[b][:, 0:N], AF.Gelu_apprx_tanh, pk, ['gg:%d' % (m - 6)])
                    else:
                        cp('dve', qb[:, m - 10, 0:N], psb[b][:, 0:N], pk, ['qb:%d' % (m - 10)])
                if half == 0:
                    ffn_cast()
                gu_prefetch()

            groups = {0: ([(o_pool, 0, 'o_pool:0'), (o_pool, 1, 'o_pool:1')], 256, 0),
                      1: ([(gg, c, 'gg:%d' % c) for c in range(4)], 512, 2),
                      2: ([(o_xa, 0, 'o_xa:0'), (o_xa, 1, 'o_xa:1')], 256, 6)}
            sqi_ = [0]

            def rms_group(g):
                chunks, nfe, mi = groups[g]
                b = bank()
                for i, (buf, c, key) in enumerate(chunks):
                    sqb = sq[sqi_[0] % 2]
                    sqk = 'sq:%d' % (sqi_[0] % 2)
                    sqi_[0] += 1
                    act(sqb[:, 0:N], buf[:, c, 0:N], AF.Square, [key], [sqk])
                    mm(psb[b][:, 0:N], ones_b[:, :], sqb[:, 0:N], i == 0, i == len(chunks) - 1, ['ones_b', sqk], ['ps:%d' % b])
                tsc('dve', rbc[:, g, 0:N], psb[b][:, 0:N], 1.0 / nfe, EPS, ALU.mult, ALU.add, ['ps:%d' % b], ['rbc:%d' % g])
                rsqrt_inplace(rbc[:, g, 0:N], 'rbc:%d' % g)
                for (buf, c, key) in chunks:
                    stt('dve', mixT[:, mi, 0:N], buf[:, c, 0:N], pvc('g_mix_out', mi), rbc[:, g, 0:N], ALU.mult, ALU.mult,
                        [key, 'pv', 'rbc:%d' % g], ['mixT:%d' % mi])
                    mi += 1

            if kind == 'p':
                att_pending = []
                for ch in range(2):
                    bo = bank()
                    held.add(bo)
                    bd_ = bank()
                    held.add(bd_)
                    att_pending.append((ch, bo, bd_))
                    for e_ in range(2):
                        h = 2 * ch + e_
                        pr = slice(e_ * 64, e_ * 64 + 64)
                        pt = PT[h % 2]
                        ptk = 'PT:%d' % (h % 2)
                        for mt in range(2):
                            b = bank()
                            mm(psb[b][:, 0:N], KT[pr, ch, mt * 128:(mt + 1) * 128], qb[pr, ch, 0:N], True, True,
                               ['KT', 'qb:%d' % ch], ['ps:%d' % b])
                            act(pt[:, mt, 0:N], psb[b][:, 0:N], AF.Exp, ['ps:%d' % b], [ptk], scale=0.125)
                        for mt in range(2):
                            first = (e_ == 0 and mt == 0)
                            lastm = (e_ == 1 and mt == 1)
                            mm(psb[bo][:, 0:N], Vz[:, h, mt, :], pt[:, mt, 0:N], first, lastm, ['Vz', ptk], ['ps:%d' % bo])
                            mm(psb[bd_][:, 0:N], onesz[:, e_, :], pt[:, mt, 0:N], first, lastm, ['onesz', ptk], ['ps:%d' % bd_])

                def att_partB():
                    for (ch, bo, bd_) in att_pending:
                        P.op('dve', (lambda bd_=bd_: lambda e: e.reciprocal(out=rd[:, 0:N], in_=psb[bd_][:, 0:N]))(), ['ps:%d' % bd_], ['rd'])
                        tt('dve', o_xa[:, ch, 0:N], psb[bo][:, 0:N], rd[:, 0:N], ALU.mult, ['ps:%d' % bo, 'rd'], ['o_xa:%d' % ch])
                        held.discard(bo)
                        held.discard(bd_)
                    rms_group(2)
            else:
                bs = [bank(), bank()]
                NSLOT = 8

                def kvslot(n):
                    i = n % NSLOT
                    if i < 2:
                        return kvf[i][:, :, :], 'kvf:%d' % i, None
                    t_ = 1 + (i - 2) // 2
                    h_ = (i - 2) % 2
                    return (xtok[:, t_, h_ * 512:(h_ + 1) * 512].rearrange("p (a b) -> p a b", b=256), 'kvf:%d' % i, 'xtok:%d' % t_)

                def kvload(n):
                    ap, key, xk_ = kvslot(n)
                    src = I['ck'][n] if n < 16 else I['cv'][n - 16]
                    wk = [key] + ([xk_] if (xk_ is not None and n < NSLOT) else [])
                    P.dma('sp', ap, src.rearrange("(mt p) n -> p mt n", p=128), (), wk)

                def kprep(bb):
                    ap, kk, _ = kvslot(bb)
                    b = bank(exclude=bs)
                    for ch in range(2):
                        for mt in range(2):
                            tr(psb[b][:, (ch * 2 + mt) * 128:(ch * 2 + mt + 1) * 128], ap[:, mt, ch * 128:(ch + 1) * 128],
                               identf[:], [kk, 'identf'], ['ps:%d' % b])
                    dst = KTs[bb % 2][:, :, :].rearrange("p a b -> p (a b)")
                    if bb % 2 == 0:
                        cp('dve', dst, psb[b][:, 0:512], ['ps:%d' % b], ['KTs:%d' % (bb % 2)])
                    else:
                        act(dst, psb[b][:, 0:512], AF.Copy, ['ps:%d' % b], ['KTs:%d' % (bb % 2)])
                    kvload(bb + NSLOT)

                def kscores(bb):
                    cols = slice(bb * 4, bb * 4 + 4)
                    for e_ in range(2):
                        pr = slice(e_ * 64, e_ * 64 + 64)
                        for ch in range(2):
                            for mt in range(2):
                                c0 = (mt * 2 + ch) * 64 + bb * 4
                                mm(psb[bs[e_]][:, c0:c0 + 4], KTs[bb % 2][pr, ch, mt * 128:(mt + 1) * 128], qb[pr, ch, cols], True, True,
                                   ['KTs:%d' % (bb % 2), 'qb:%d' % ch], ['ps:%d' % bs[e_]])

                kprep(0)
                for bb in range(16):
                    if bb + 1 < 16:
                        kprep(bb + 1)
                    kscores(bb)
                for e_ in range(2):
                    act(PT[1][:, e_, 0:256], psb[bs[e_]][:, 0:256], AF.Exp, ['ps:%d' % bs[e_]], ['PT:1'], scale=0.125)
                bo = [bank(), bank()]
                bdn = [bank(), bank()]

                def vb16(bb):
                    return Vzs[bb % 2][:, :, :, :].rearrange("p a b c -> p (a b c)")[:, 0:512].rearrange("p (a b) -> p a b", b=256)

                def vprep(bb):
                    ap, kk, _ = kvslot(16 + bb)
                    if bb % 2 == 0:
                        act(vb16(bb), ap, AF.Copy, [kk], ['Vzs:%d' % (bb % 2)])
                    else:
                        cp('dve', vb16(bb), ap, [kk], ['Vzs:%d' % (bb % 2)])
                    if 16 + bb + NSLOT < 32:
                        kvload(16 + bb + NSLOT)

                def pvmm(bb):
                    vk = 'Vzs:%d' % (bb % 2)
                    vb = vb16(bb)
                    for ch in range(2):
                        for e_ in range(2):
                            oc = slice(e_ * 64 + bb * 4, e_ * 64 + bb * 4 + 4)
                            for mt in range(2):
                                c0 = (mt * 2 + ch) * 64 + bb * 4
                                mm(psb[bo[ch]][:, oc], vb[:, mt, ch * 128:(ch + 1) * 128], PT[1][:, e_, c0:c0 + 4], mt == 0, mt == 1,
                                   [vk, 'PT:1'], ['ps:%d' % bo[ch]])
                                mm(psb[bdn[ch]][:, oc], ones_b[:, :], PT[1][:, e_, c0:c0 + 4], mt == 0, mt == 1,
                                   ['ones_b', 'PT:1'], ['ps:%d' % bdn[ch]])

                vprep(0)
                for bb in range(16):
                    if bb + 1 < 16:
                        vprep(bb + 1)
                    pvmm(bb)
                for ch in range(2):
                    for e_ in range(2):
                        pr = slice(e_ * 64, e_ * 64 + 64)
                        P.op('dve', (lambda ch=ch, e_=e_, pr=pr: lambda e: e.reciprocal(
                            out=rd[pr, e_ * 64:(e_ + 1) * 64], in_=psb[bdn[ch]][pr, e_ * 64:(e_ + 1) * 64]))(), ['ps:%d' % bdn[ch]], ['rd'])
                        tt('dve', o_xa[pr, ch, 0:N], psb[bo[ch]][pr, e_ * 64:(e_ + 1) * 64], rd[pr, e_ * 64:(e_ + 1) * 64], ALU.mult,
                           ['ps:%d' % bo[ch], 'rd'], ['o_xa:%d' % ch])

                def att_partB():
                    rms_group(2)

            for c in range(2):
                uk = 'upool:%d' % c
                tt('dve', hv(sA, c, WU, 1, WU), hv(upool, c, WU, 1, WU), hv(upool, c, WU, 0, WU - 1), ALU.add, [uk], ['sA:%d' % c])
            hi = slice(64, 128)
            lo = slice(0, 64)
            tt('dve', hv(sB, 0, WU, 3, WU, hi), hv(sA, 0, WU, 3, WU, hi), hv(sA, 0, WU, 1, WU - 2, hi), ALU.add, ['sA:0'], ['sB:0'])
            tt('dve', hv(sB, 1, WU, 3, WU), hv(sA, 1, WU, 3, WU), hv(sA, 1, WU, 1, WU - 2), ALU.add, ['sA:1'], ['sB:1'])
            tt('dve', hv(sC, None, WU, 7, WU), hv(sB, 1, WU, 7, WU), hv(sB, 1, WU, 3, WU - 4), ALU.add, ['sB:1'], ['sC'])
            tt('dve', hv(sA, 1, WU, 15, WU, hi), hv(sC, None, WU, 15, WU, hi), hv(sC, None, WU, 7, WU - 8, hi), ALU.add, ['sC'], ['sA:1'])
            fin_src = [(0, lo, sA, 0, 'sA:0', 2), (0, hi, sB, 0, 'sB:0', 4), (1, lo, sC, None, 'sC', 8), (1, hi, sA, 1, 'sA:1', 16)]
            for (c, pr, buf, bc, bkey, w) in fin_src:
                stt('dve', fv(diff[pr, c, 0:N]), hv(buf, bc, WU, 15, WU, pr), 1.0 / w, hv(upool, c, WU, 15, WU, pr),
                    ALU.mult, ALU.subtract, [bkey, 'upool:%d' % c], ['diff:%d' % c])
            if kind == 'p' and sti == 0:
                for (c, pr, buf, bc, bkey, w) in fin_src:
                    src = buf[pr, bc, 15:30] if bc is not None else buf[pr, 15:30]
                    tt('dve', t16[pr, 0:15], src, rc15[pr, c, :], ALU.mult, [bkey, 'rc15'], ['t16'])
                    tt('dve', diff[pr, c, 0:15], t16[pr, 0:15], upool[pr, c, 15:30], ALU.subtract, ['t16', 'upool:%d' % c], ['diff:%d' % c])
            for c in range(2):
                b = bank()
                mm(psb[b][:, 0:N], bdw[:, c, :], diff[:, c, 0:N], True, True, ['bdw', 'diff:%d' % c], ['ps:%d' % b])
                act(o_pool[:, c, 0:N], psb[b][:, 0:N], AF.Identity, ['ps:%d' % b, 'pv'], ['o_pool:%d' % c], scale=pvc('pool_scale', c))
            if kind == 's':
                for c in range(2):
                    cp('pool', stg[:, 0:64].rearrange("p (s t) -> p s t", t=4), hv(upool, c, WU, 15, 19), ['upool:%d' % c], ['stg'])
                    rows_out(stg[:, 0:64], 64, ['stg'],
                             [(O['pool_s'][:, 11 + t, c * 128:(c + 1) * 128], t, 4, 16) for t in range(4)])
            else:
                if last:
                    cp('pool', stg[:, 0:30].rearrange("p (c r) -> p c r", r=15), upool[:, :, 512:527], ['upool:0', 'upool:1'], ['stg'])
                    rows_out(stg[:, 0:30], 30, ['stg'],
                             [(O['pool_p'][:, c * 128:(c + 1) * 128], c * 15, 1, 15) for c in range(2)])
                else:
                    for c in range(2):
                        cp('pool', upool[:, c, 0:15], upool[:, c, 512:527], ['upool:%d' % c], ['upool:%d' % c])

            att_partB()
            rms_group(0)
            ffn_cast()
            RGB = {
                0: dict(xc=(S[0]['xc'][:, 0:512], 'S0_xc'), xcb=(S[0]['xcb'][:, 0:512], 'S0_xcb'), thr=(S[0]['thr'][:, 0:512], 'S0_thr'),
                        thi=(S[0]['thi'][:, 0:512], 'S0_thi'), a2=(S[0]['a2'][:, 0:512], 'S0_a2')),
                1: dict(xc=(S[1]['xc'][:, 0:512], 'S1_xc'), xcb=(S[1]['xcb'][:, 0:512], 'S1_xcb'), thr=(S[1]['thr'][:, 0:512], 'S1_thr'),
                        thi=(S[1]['thi'][:, 0:512], 'S1_thi'), a2=(S[1]['a2'][:, 0:512], 'S1_a2')),
                2: dict(xc=(rd[:, 0:512], 'rd'), xcb=(diff[:, 0, :], 'diff:0'), thr=(sA[:, 0, 0:512], 'sA:0'),
                        thi=(sA[:, 1, 0:512], 'sA:1'), a2=(o_pool[:, 0, :], 'o_pool:0')),
                3: dict(xc=(sC[:, 0:512], 'sC'), xcb=(diff[:, 1, :], 'diff:1'), thr=(sB[:, 0, 0:512], 'sB:0'),
                        thi=(sB[:, 1, 0:512], 'sB:1'), a2=(o_pool[:, 1, :], 'o_pool:1')),
            }

            def rb(c, n):
                return RGB[c][n][0][:, 0:N]

            def rk(c, n):
                return RGB[c][n][1]

            C4 = range(4)
            for c in C4:
                act(fv(rb(c, 'xc')), hv(xrnn, c, WX, 0, T), AF.Identity, ['xrnn:%d' % c, 'pv'], [rk(c, 'xc')],
                    scale=pvc('rnn_conv_w', 0 * 4 + c), bias=pvc('rnn_conv_b', c))
            for c in C4:
                xcv = fv(rb(c, 'xc'))
                for tap in range(1, 4):
                    stt('dve', xcv, hv(xrnn, c, WX, tap, tap + T), pvc('rnn_conv_w', tap * 4 + c), xcv, ALU.mult, ALU.add,
                        ['xrnn:%d' % c, 'pv', rk(c, 'xc')], [rk(c, 'xc')])
            for c in C4:
                cp('pool', rb(c, 'xcb'), rb(c, 'xc'), [rk(c, 'xc')], [rk(c, 'xcb')])
                ffn_cast()
                if kind == 'p' and not last:
                    cp('pool', xrnn[:, c, 0:3], xrnn[:, c, 512:515], ['xrnn:%d' % c], ['xrnn:%d' % c])
            for pair in range(2):
                cs = [2 * pair, 2 * pair + 1]
                gb = {}
                for c in cs:
                    ba = bank()
                    mm(psb[ba][:, 0:N], bdw[:, 2 + c, :], rb(c, 'xcb'), True, True, ['bdw', rk(c, 'xcb')], ['ps:%d' % ba])
                    bx = bank()
                    mm(psb[bx][:, 0:N], bdw[:, 6 + c, :], rb(c, 'xcb'), True, True, ['bdw', rk(c, 'xcb')], ['ps:%d' % bx])
                    gb[c] = (ba, bx)
                for c in cs:
                    ba, bx = gb[c]
                    act(rb(c, 'thr'), psb[ba][:, 0:N], AF.Tanh, ['ps:%d' % ba, 'pv'], [rk(c, 'thr')], scale=0.5, bias=pv[:, PV_HBA + c:PV_HBA + c + 1])
                    act(rb(c, 'thi'), psb[bx][:, 0:N], AF.Tanh, ['ps:%d' % bx, 'pv'], [rk(c, 'thi')], scale=0.5, bias=pv[:, PV_HBX + c:PV_HBX + c + 1])
            for c in C4:
                hc_ = pv[:, PV_HC + c:PV_HC + c + 1]
                c2_ = pv[:, PV_C2 + c:PV_C2 + c + 1]
                act(rb(c, 'a2'), rb(c, 'thr'), AF.Exp, [rk(c, 'thr'), 'pv'], [rk(c, 'a2')], scale=c2_, bias=c2_)
                act(rb(c, 'thr'), rb(c, 'thr'), AF.Exp, [rk(c, 'thr'), 'pv'], [rk(c, 'thr')], scale=hc_, bias=hc_)
            for c in C4:
                tsc('dve', rb(c, 'a2'), rb(c, 'a2'), -1.0, 1.0 + 1.2e-7, ALU.mult, ALU.add, [rk(c, 'a2')], [rk(c, 'a2')])
                stt('dve', rb(c, 'thi'), rb(c, 'thi'), 1.0, rb(c, 'xc'), ALU.add, ALU.mult, [rk(c, 'thi'), rk(c, 'xc')], [rk(c, 'thi')])
            for c in C4:
                act(rb(c, 'a2'), rb(c, 'a2'), AF.Ln, [rk(c, 'a2')], [rk(c, 'a2')])
                act(rb(c, 'a2'), rb(c, 'a2'), AF.Exp, [rk(c, 'a2')], [rk(c, 'a2')], scale=0.5)
            for c in C4:
                stt('dve', rb(c, 'thi'), rb(c, 'thi'), 0.5, rb(c, 'a2'), ALU.mult, ALU.mult, [rk(c, 'thi'), rk(c, 'a2')], [rk(c, 'thi')])
                if kind == 's':
                    a0 = fv(rb(c, 'thr'))[:, :, 0]
                    b0 = fv(rb(c, 'thi'))[:, :, 0]
                    tt('dve', t16[:, 0:16], a0, h0T[:, c, :], ALU.mult, [rk(c, 'thr'), 'h0T'], ['t16'])
                    tt('dve', b0, b0, t16[:, 0:16], ALU.add, [rk(c, 'thi'), 't16'], [rk(c, 'thi')])
                    mset('dve', a0, 0.0, [rk(c, 'thr')])
                    init = 0.0
                else:
                    init = hcar[:, c:c + 1]
                P.op('dve', (lambda c=c, init=init: lambda e: e.tensor_tensor_scan(
                    out=rb(c, 'a2'), data0=rb(c, 'thr'), data1=rb(c, 'thi'), initial=init,
                    op0=ALU.mult, op1=ALU.add))(), [rk(c, 'thr'), rk(c, 'thi'), 'hcar'], [rk(c, 'a2')])
                if kind == 'p':
                    cp('dve', hcar[:, c:c + 1], rb(c, 'a2')[:, N - 1:N], [rk(c, 'a2')], ['hcar'])
                else:
                    cp('dve', hls[:, c, :], fv(rb(c, 'a2'))[:, :, 3], [rk(c, 'a2')], ['hls'])
                tt('dve', gg[:, c, 0:N], gg[:, c, 0:N], rb(c, 'a2'), ALU.mult, ['gg:%d' % c, rk(c, 'a2')], ['gg:%d' % c])
            if kind == 's':
                for half in range(2):
                    for ci in range(2):
                        c = half * 2 + ci
                        cp('pool', stg[:, ci * 48:(ci + 1) * 48].rearrange("p (s r) -> p s r", r=3), hv(xrnn, c, WX, 4, 7), ['xrnn:%d' % c], ['stg'])
                    dl = []
                    for ci in range(2):
                        c = half * 2 + ci
                        for r in range(3):
                            dl.append((O['conv_s'][:, r, c * 128:(c + 1) * 128], ci * 48 + r, 3, 16))
                    rows_out(stg[:, 0:96], 96, ['stg'], dl)
                cp('pool', stg[:, 0:64].rearrange("p (c s) -> p c s", s=16), hls[:, :, :], ['hls'], ['stg'])
                rows_out(stg[:, 0:64], 64, ['stg'], [(O['h_s'][:, c * 128:(c + 1) * 128], c * 16, 1, 16) for c in range(4)])
            elif last:
                cp('pool', stg[:, 0:12].rearrange("p (c r) -> p c r", r=3), xrnn[:, :, 512:515], ['xrnn:%d' % c for c in range(4)], ['stg'])
                cp('pool', stg[:, 12:16], hcar[:, :], ['hcar'], ['stg'])
                rows_out(stg[:, 0:16], 16, ['stg'],
                         [(O['conv_p'][:, c * 128:(c + 1) * 128], c * 3, 1, 3) for c in range(4)] +
                         [(O['h_p'][:, c * 128:(c + 1) * 128], 12 + c, 1, 1) for c in range(4)])

            rms_group(1)

            s = gu_slot_of[(pidx, 'out')]
            wv = gu[s][:, 0:8192].rearrange("p (k n) -> p k n", n=1024)
            for j in range(ntt):
                for half in range(2):
                    b = bank()
                    for k in range(8):
                        mm(psb[b][:tp, 0:512], mixT[:, k, j * 128:j * 128 + tp], wv[:, k, half * 512:(half + 1) * 512], k == 0, k == 7,
                           ['gu:%d' % s, 'gub:%d' % s, 'mixT:%d' % k], ['ps:%d' % b])
                    xs_ = xtok[:tp, j, half * 512:(half + 1) * 512]
                    tt('dve', xs_, xs_, psb[b][:tp, 0:512], ALU.add, ['xtok:%d' % j, 'ps:%d' % b], ['xtok:%d' % j])
            gu_prefetch()
            if d_next[0] == 0:
                d_prefetch(); d_prefetch()

            if kind == 'p' and sti < 3:
                for j in range(4):
                    P.dma('sp', xstage[j][0], I['xp'][(sti + 1) * 512 + j * 128:(sti + 1) * 512 + (j + 1) * 128, :], (), xstage[j][1])
            if last:
                state_loads()
                st_gen[0] = state_units()
            nu = [None]
            if kind == 'p' and sti < 3:
                nu[0] = norm_units(4, 128, 'g_mix_norm', xstage, mixT, 'mixT')
                norm1_done.add(pidx + 1)
            norm_and_transpose(ntt, tp, 'g_ffn_norm')

            def down_part(sg):
                ds = d_slot_of[(pidx, sg)]
                dv = dn[ds][:, :].rearrange("p (k n) -> p k n", n=1024)
                hb = hT[sg % 2]
                for j in range(ntt):
                    for half in range(2):
                        b = bank()
                        for cc in range(4):
                            mm(psb[b][:tp, 0:512], hb[:, cc, j * 128:j * 128 + tp], dv[:, cc, half * 512:(half + 1) * 512], cc == 0, cc == 3,
                               ['dn:%d' % ds, 'hT:%d' % (sg % 2)], ['ps:%d' % b])
                        xs_ = xtok[:tp, j, half * 512:(half + 1) * 512]
                        tt('dve', xs_, xs_, psb[b][:tp, 0:512], ALU.add, ['xtok:%d' % j, 'ps:%d' % b], ['xtok:%d' % j])
                d_prefetch()

            for sg in range(6):
                s = gu_slot_of[(pidx, 'gu%d' % sg)]
                gv = gu[s][:, 0:4096].rearrange("p (k n) -> p k n", n=512)
                uv = gu[s][:, 4096:8192].rearrange("p (k n) -> p k n", n=512)
                for cc in range(4):
                    c = sg * 4 + cc
                    par = c % 2
                    gk = 'gbuf:%d' % par
                    fk = 'ftmp:%d' % par
                    bg = bank()
                    for k in range(8):
                        mm(psb[bg][:, 0:N], gv[:, k, cc * 128:(cc + 1) * 128], xnT[:, k, 0:N], k == 0, k == 7,
                           ['gu:%d' % s, 'gub:%d' % s, 'xnT:%d' % k], ['ps:%d' % bg])
                    bu = bank()
                    for k in range(8):
                        mm(psb[bu][:, 0:N], uv[:, k, cc * 128:(cc + 1) * 128], xnT[:, k, 0:N], k == 0, k == 7,
                           ['gu:%d' % s, 'gub:%d' % s, 'xnT:%d' % k], ['ps:%d' % bu])
                    gb3 = gbuf[par][:, 0:nseq * WG].rearrange("p (s w) -> p s w", w=WG)
                    if kind == 's':
                        cp('pool', gb3[:, :, 0:2], gcar_s[:, c, :, :], ['gcs:%d' % c], [gk])
                    else:
                        cp('pool', gb3[:, :, 0:2], gcar[:, c:c + 1, :], ['gcar:%d' % c], [gk])
                    act(gb3[:, :, 2:2 + T], fv(psb[bg][:, 0:N]), AF.Copy, ['ps:%d' % bg], [gk])
                    fw_ = PV['ffn_conv_w']
                    ft3 = fv(ftmp[par][:, 0:N])
                    act(ft3, fv(psb[bg][:, 0:N]), AF.Identity, ['ps:%d' % bg, 'pv'], [fk],
                        scale=pv[:, fw_ + 48 + c:fw_ + 48 + c + 1], bias=pvc('ffn_conv_b', c))
                    stt('dve', ft3, gb3[:, :, 1:1 + T], pv[:, fw_ + 24 + c:fw_ + 24 + c + 1], ft3, ALU.mult, ALU.add, [gk, 'pv', fk], [fk])
                    stt('dve', ft3, gb3[:, :, 0:T], pv[:, fw_ + c:fw_ + c + 1], ft3, ALU.mult, ALU.add, [gk, 'pv', fk], [fk])
                    if kind == 's':
                        cp('pool', gcar_s[:, c, :, :], gb3[:, :, T:T + 2], [gk], ['gcs:%d' % c])
                    else:
                        cp('pool', gcar[:, c:c + 1, :], gb3[:, :, T:T + 2], [gk], ['gcar:%d' % c])
                    act(ftmp[par][:, 0:N], ftmp[par][:, 0:N], AF.Gelu_apprx_tanh, [fk], [fk])
                    tt('dve', hT[sg % 2][:, cc, 0:N], ftmp[par][:, 0:N], psb[bu][:, 0:N], ALU.mult, [fk, 'ps:%d' % bu], ['hT:%d' % (sg % 2)])
                    if cc == 1 and sg > 0:
                        down_part(sg - 1)
                    if last and sg >= 1:
                        for _ in range(2):
                            next(st_gen[0], None)
                    if nu[0] is not None and ((sg == 3 and cc >= 1) or sg == 4):
                        next(nu[0], None)
                gu_prefetch()
                if kind == 's':
                    src = gcar_s[:, sg * 4:(sg + 1) * 4, :, :].rearrange("p c s r -> p (c s r)")
                    dl = []
                    for ci in range(4):
                        c = sg * 4 + ci
                        for r in range(2):
                            dl.append((O['ffn_s'][:, r, c * 128:(c + 1) * 128], ci * 32 + r, 2, 16))
                    rows_out(src, 128, ['gcs:%d' % (sg * 4 + ci) for ci in range(4)], dl)
            if nu[0] is not None:
                for _ in nu[0]:
                    pass
            if last:
                for _ in st_gen[0]:
                    pass
            down_part(5)

            if kind == 's':
                pass
            elif last:
                src = gcar[:, :, :].rearrange("p c r -> p (c r)")
                dl = []
                for c in range(24):
                    for r in range(2):
                        dl.append((O['ffn_p'][r:r + 1, c * 128:(c + 1) * 128], c * 2 + r, 1, 1))
                rows_out(src, 48, ['gcar:%d' % c for c in range(24)], dl)

            for j in range(ntt):
                act(xnb[j % 2][:tp, :], xtok[:tp, j, :], AF.Square, ['xtok:%d' % j], ['xnb:%d' % (j % 2), 'ss'], accum=ss[:tp, 4 + j:5 + j])
            tsc('dve', rstd[:tp, 4:4 + ntt], ss[:tp, 4:4 + ntt], 1.0 / 1024, EPS, ALU.mult, ALU.add, ['ss'], ['rstd'])
            rsqrt_inplace(rstd[:tp, 4:4 + ntt], 'rstd')
            for j in range(ntt):
                stt('dve', xtok[:tp, j, :], xtok[:tp, j, :], rstd[:tp, 4 + j:5 + j], gfin[:tp, :], ALU.mult, ALU.mult,
                    ['xtok:%d' % j, 'rstd', 'gfin'], ['xtok:%d' % j])
                P.dma('sp', ydst[j], xtok[:tp, j, :], ['xtok:%d' % j], ())

        st_gen = [None]

        def state_loads():
            stf_a = gg[0:32, :, :].rearrange("p a b -> p (a b)")
            stf_b = rbc[0:32, 0:2, :].rearrange("p a b -> p (a b)")
            P.dma('sp', stf_a, I['st_ffn'][:, 0:2048], (), ['gg:0', 'gg:1', 'gg:2', 'gg:3'])
            P.dma('sp', stf_b, I['st_ffn'][:, 2048:3072], (), ['rbc:0', 'rbc:1'])
            P.dma('sp', o_pool[:, 0, 0:256], I['st_pool'][0:128, :], (), ['o_pool:0'])
            P.dma('sp', o_pool[0:112, 1, 0:256], I['st_pool'][128:240, :], (), ['o_pool:1'])
            P.dma('sp', o_xa[0:48, 0, :], I['st_conv'], (), ['o_xa:0'])
            P.dma('sp', o_xa[0:16, 1, :], I['st_h'], (), ['o_xa:1'])
            P.dma('sp', O['pool_s'][:, 0:11, :], I['st_pool'].rearrange("(b r) c -> b r c", r=15)[:, 4:15, :], (), ())

        def state_units():
            stf_a = gg[0:32, :, :].rearrange("p a b -> p (a b)")
            stf_b = rbc[0:32, 0:2, :].rearrange("p a b -> p (a b)")

            def st_tr(src_ap, nr, src_keys):
                b = bank()
                tr(psb[b][:, 0:nr], src_ap, identf[0:nr, 0:nr], list(src_keys) + ['identf'], ['ps:%d' % b])
                return b

            for c in range(2):
                flat = upool[:, c, 0:16 * 19].rearrange("p (s w) -> p s w", w=19)
                b = st_tr(o_pool[0:128, 0, c * 128:(c + 1) * 128], 128, ['o_pool:0'])
                cp('dve', flat[:, 0:8, 0:15], psb[b][:, 0:120].rearrange("p (s r) -> p s r", r=15), ['ps:%d' % b], ['upool:%d' % c])
                cp('dve', flat[:, 8, 0:8], psb[b][:, 120:128], ['ps:%d' % b], ['upool:%d' % c])
                yield
                b = st_tr(o_pool[0:112, 1, c * 128:(c + 1) * 128], 112, ['o_pool:1'])
                cp('dve', flat[:, 8, 8:15], psb[b][:, 0:7], ['ps:%d' % b], ['upool:%d' % c])
                cp('dve', flat[:, 9:16, 0:15], psb[b][:, 7:112].rearrange("p (s r) -> p s r", r=15), ['ps:%d' % b], ['upool:%d' % c])
                yield
            for c in range(4):
                b = st_tr(o_xa[0:48, 0, c * 128:(c + 1) * 128], 48, ['o_xa:0'])
                cp('dve', xrnn[:, c, 0:16 * 7].rearrange("p (s w) -> p s w", w=7)[:, :, 0:3],
                   psb[b][:, 0:48].rearrange("p (s r) -> p s r", r=3), ['ps:%d' % b], ['xrnn:%d' % c])
                yield
            for c in range(4):
                b = st_tr(o_xa[0:16, 1, c * 128:(c + 1) * 128], 16, ['o_xa:1'])
                cp('dve', h0T[:, c, :], psb[b][:, 0:16], ['ps:%d' % b], ['h0T'])
                yield
            for c in range(24):
                if c < 16:
                    src, sk_ = stf_a[:, c * 128:(c + 1) * 128], ['gg:%d' % (c // 4)]
                else:
                    src, sk_ = stf_b[:, (c - 16) * 128:(c - 15) * 128], ['rbc:%d' % ((c - 16) // 4)]
                b = st_tr(src, 32, sk_)
                act(gcar_s[:, c, :, :], psb[b][:, 0:32].rearrange("p (s r) -> p s r", r=2), AF.Copy, ['ps:%d' % b], ['gcs:%d' % c])
                yield

        for sti in range(4):
            run_pass(sti, 'p', sti)
        run_pass(4, 's', 0)
        P.emit(st)
    return nc


_NC_CACHE = {}


def kernel(**inp):
    f = lambda a: np.ascontiguousarray(np.asarray(a, dtype=np.float32))
    shared = {
        'g_mix_norm': f(inp['g_mix_norm'][0]).reshape(8, 128),
        'w_in': f(inp['w_in'][0]),
        'w_pool': f(inp['w_pool'][0]),
        'pool_scale': f(inp['pool_scale'][0]).reshape(2, 128),
        'rnn_conv_w': f(inp['rnn_conv_w'][0]).reshape(16, 128),
        'rnn_conv_b': f(inp['rnn_conv_b'][0]).reshape(4, 128),
        'w_rg_a': f(inp['w_rg_a'][0]),
        'b_rg_a': f(inp['b_rg_a'][0]).reshape(4, 128),
        'w_rg_x': f(inp['w_rg_x'][0]),
        'b_rg_x': f(inp['b_rg_x'][0]).reshape(4, 128),
        'rg_lambda': f(inp['rg_lambda'][0]).reshape(4, 128),
        'g_mem_norm': f(inp['g_mem_norm'][0]).reshape(8, 128),
        'w_mem_k': f(inp['w_mem_k'][0]),
        'w_mem_v': f(inp['w_mem_v'][0]),
        'g_mix_out': f(inp['g_mix_out'][0]).reshape(8, 128),
        'w_out': f(inp['w_out'][0]),
        'g_ffn_norm': f(inp['g_ffn_norm'][0]).reshape(8, 128),
        'w_ff_gate': f(inp['w_ff_gate'][0]),
        'w_ff_up': f(inp['w_ff_up'][0]),
        'ffn_conv_w': f(inp['ffn_conv_w'][0]).reshape(72, 128),
        'ffn_conv_b': f(inp['ffn_conv_b'][0]).reshape(24, 128),
        'w_ff_down': f(inp['w_ff_down'][0]),
        'g_final': f(inp['g_final']).reshape(1, 1024),
    }
    in_maps = []
    for c in range(NCORES):
        b = slice(16 * c, 16 * c + 16)
        m = dict(shared)
        m['xp'] = f(inp['x_prompt'][c])
        m['xs'] = f(inp['x_sample'][b]).reshape(64, 1024)
        m['mem'] = f(inp['mem_prompt'][c])
        m['st_pool'] = f(inp['state_pool'][0, b]).reshape(240, 256)
        m['st_conv'] = f(inp['state_rnn_conv'][0, b]).reshape(48, 512)
        m['st_h'] = f(inp['state_rnn_h'][0, b])
        m['st_ffn'] = f(inp['state_ffn_conv'][0, b]).reshape(32, 3072)
        m['ck'] = f(inp['cache_mem_k'][0, b]).reshape(16, 256, 256)
        m['cv'] = f(inp['cache_mem_v'][0, b]).reshape(16, 256, 256)
        in_maps.append(m)
    if 'nc' not in _NC_CACHE:
        _NC_CACHE['nc'] = build()
    nc = _NC_CACHE['nc']
    res = run_bass_kernel_spmd(nc, in_maps, core_ids=list(range(NCORES)))
    R = res.results

    def cat(name, shape):
        return np.stack([np.asarray(R[c][name], dtype=np.float32).reshape(shape) for c in range(NCORES)], axis=0)

    y_p = cat('y_p', (2048, 1024))
    y_s = cat('y_s', (16, 4, 1024)).reshape(128, 4, 1024)
    pool_p = cat('pool_p', (15, 256))[None]
    conv_p = cat('conv_p', (3, 512))[None]
    h_p = cat('h_p', (512,))[None]
    ffn_p = cat('ffn_p', (2, 3072))[None]
    mk_p = cat('mk_p', (256, 4, 64))[None]
    mv_p = cat('mv_p', (256, 4, 64))[None]
    pool_s = cat('pool_s', (16, 15, 256)).reshape(1, 128, 15, 256)
    conv_s = cat('conv_s', (16, 3, 512)).reshape(1, 128, 3, 512)
    h_s = cat('h_s', (16, 512)).reshape(1, 128, 512)
    ffn_s = cat('ffn_s', (16, 2, 3072)).reshape(1, 128, 2, 3072)
    return (y_p, y_s, pool_p, conv_p, h_p, ffn_p, mk_p, mv_p, pool_s, conv_s, h_s, ffn_s)
```

```python
import numpy as np
from contextlib import ExitStack
import concourse.bass as bass
import concourse.mybir as mybir
from concourse.bass_utils import run_bass_kernel_spmd

F32 = mybir.dt.float32
BF16 = mybir.dt.bfloat16
ALU = mybir.AluOpType
AF = mybir.ActivationFunctionType

ENGS = ['pe', 'act', 'dve', 'pool', 'sp']
N_DMA_SEMS = 32
EPS = 1e-6
NCORES = 8
KSTOP = 10 ** 9


class Prog:
    def __init__(self, nc):
        self.nc = nc
        self.ops = {e: [] for e in ENGS}
        self.key_w = {}
        self.key_r = {}
        self.waited = {e: {} for e in ENGS}
        self.dma_val = [0] * N_DMA_SEMS
        self.dma_rr = {'sp': 0, 'pool': 0, 'act': 0}
        self.signal = {e: set() for e in ENGS}

    def _deps(self, eng, reads, writes, is_dma):
        deps = []
        for k in reads:
            t = self.key_w.get(k)
            if t is not None:
                deps.append((t, 'raw'))
        for k in writes:
            t = self.key_w.get(k)
            if t is not None:
                deps.append((t, 'waw'))
            for t in self.key_r.get(k, {}).values():
                deps.append((t, 'war'))
        out = []
        w = self.waited[eng]
        for t, kind in deps:
            if t[0] == 'e':
                _, e2, idx = t
                if e2 == eng and not is_dma:
                    if eng == 'pe':
                        continue
                if w.get(('e', e2), -1) >= idx:
                    continue
                w[('e', e2)] = idx
                self.signal[e2].add(idx)
                out.append(t)
            else:
                _, s, val = t
                if w.get(('d', s), 0) >= val:
                    continue
                w[('d', s)] = val
                out.append(t)
        return out

    def _register(self, tok, reads, writes):
        for k in writes:
            self.key_w[k] = tok
            self.key_r[k] = {}
        for k in reads:
            d = self.key_r.setdefault(k, {})
            d[(tok[0], tok[1])] = tok

    def op(self, eng, fn, reads=(), writes=()):
        self.nrec = getattr(self, 'nrec', 0) + 1
        if self.nrec > KSTOP:
            return
        deps = self._deps(eng, reads, writes, False)
        idx = len(self.ops[eng])
        self.ops[eng].append(dict(fn=fn, deps=deps, dma=None))
        self._register(('e', eng, idx), reads, writes)

    def dma(self, eng, out, in_, reads=(), writes=(), after=()):
        self.nrec = getattr(self, 'nrec', 0) + 1
        if self.nrec > KSTOP:
            return
        half = N_DMA_SEMS // 2
        s = self.dma_rr[eng] + (half if eng == 'pool' else 0)
        self.dma_rr[eng] = (self.dma_rr[eng] + 1) % half
        deps = self._deps(eng, list(reads) + list(after), writes, True)
        prev = self.dma_val[s]
        w = self.waited[eng]
        if prev > 0 and w.get(('d', s), 0) < prev:
            w[('d', s)] = prev
            deps.append(('d', s, prev))
        val = prev + 16
        self.dma_val[s] = val
        self.ops[eng].append(dict(fn=lambda e: e.dma_start(out=out, in_=in_), deps=deps, dma=(s, val)))
        self._register(('d', s, val), reads, writes)

    def emit(self, stack):
        nc = self.nc
        esem = {e: stack.enter_context(nc.semaphore("s_" + e)) for e in ENGS}
        dsem = [stack.enter_context(nc.semaphore("d_%d" % i)) for i in range(N_DMA_SEMS)]
        rank = {}
        for e in ENGS:
            for r, idx in enumerate(sorted(self.signal[e])):
                rank[(e, idx)] = r + 1
        fin = [(s, self.dma_val[s]) for s in range(N_DMA_SEMS) if self.dma_val[s] > 0]

        def replay(e, engobj):
            for i, o in enumerate(self.ops[e]):
                for t in o['deps']:
                    if t[0] == 'e':
                        engobj.wait_ge(esem[t[1]], rank[(t[1], t[2])])
                    else:
                        engobj.wait_ge(dsem[t[1]], t[2])
                ins = o['fn'](engobj)
                if o['dma'] is not None:
                    ins.then_inc(dsem[o['dma'][0]], 16)
                elif i in self.signal[e]:
                    ins.then_inc(esem[e], 1)
            if e == 'sp':
                for s, v in fin:
                    engobj.wait_ge(dsem[s], v)

        with nc.Block() as block:
            @block.tensor
            def _(pe):
                replay('pe', pe)

            @block.scalar
            def _(act):
                replay('act', act)

            @block.vector
            def _(dve):
                replay('dve', dve)

            @block.gpsimd
            def _(pool):
                replay('pool', pool)

            @block.sync
            def _(sp):
                replay('sp', sp)


PV = {}
_r = 0
for _n, _rows in [('g_mix_norm', 8), ('pool_scale', 2), ('rnn_conv_w', 16), ('rnn_conv_b', 4), ('b_rg_a', 4),
                  ('b_rg_x', 4), ('rg_lambda', 4), ('g_mem_norm', 8), ('g_mix_out', 8), ('g_ffn_norm', 8),
                  ('ffn_conv_b', 24)]:
    PV[_n] = _r
    _r += _rows
PV_A_ROWS = _r
PV['ffn_conv_w'] = 128
PV_COLS = 128 + 72
PV_HBA, PV_HBX, PV_HC, PV_C2 = 200, 204, 208, 212
PV_TOT = 216


def build():
    nc = bass.Bass("TRN2", target_bir_lowering=False)

    def din(name, shape):
        return nc.dram_tensor(name, list(shape), F32, kind="ExternalInput").ap()

    def dout(name, shape):
        return nc.dram_tensor(name, list(shape), F32, kind="ExternalOutput").ap()

    def dint(name, shape):
        return nc.dram_tensor(name, list(shape), BF16, kind="Internal").ap()

    I = {}
    for n, s in [('xp', (2048, 1024)), ('xs', (64, 1024)), ('mem', (256, 1024)), ('st_pool', (240, 256)),
                 ('st_conv', (48, 512)), ('st_h', (16, 512)), ('st_ffn', (32, 3072)), ('ck', (16, 256, 256)),
                 ('cv', (16, 256, 256)),
                 ('g_mix_norm', (8, 128)), ('w_in', (1024, 1536)), ('w_pool', (4, 64, 64)), ('pool_scale', (2, 128)),
                 ('rnn_conv_w', (16, 128)), ('rnn_conv_b', (4, 128)), ('w_rg_a', (8, 64, 64)), ('b_rg_a', (4, 128)),
                 ('w_rg_x', (8, 64, 64)), ('b_rg_x', (4, 128)), ('rg_lambda', (4, 128)), ('g_mem_norm', (8, 128)),
                 ('w_mem_k', (1024, 256)), ('w_mem_v', (1024, 256)), ('g_mix_out', (8, 128)), ('w_out', (1024, 1024)),
                 ('g_ffn_norm', (8, 128)), ('w_ff_gate', (1024, 3072)), ('w_ff_up', (1024, 3072)),
                 ('ffn_conv_w', (72, 128)), ('ffn_conv_b', (24, 128)), ('w_ff_down', (3072, 1024)),
                 ('g_final', (1, 1024))]:
        I[n] = din(n, s)
    O = {}
    for n, s in [('y_p', (2048, 1024)), ('y_s', (64, 1024)), ('pool_p', (15, 256)), ('conv_p', (3, 512)),
                 ('h_p', (1, 512)), ('ffn_p', (2, 3072)), ('mk_p', (256, 256)), ('mv_p', (256, 256)),
                 ('pool_s', (16, 15, 256)), ('conv_s', (16, 3, 512)), ('h_s', (16, 512)), ('ffn_s', (16, 2, 3072))]:
        O[n] = dout(n, s)
    WB = {n: dint(n, s) for n, s in [('w_in_b', (1024, 1536)), ('w_out_b', (1024, 1024)), ('w_gate_b', (1024, 3072)),
                                     ('w_up_b', (1024, 3072)), ('w_down_b', (3072, 1024)), ('w_mk_b', (1024, 256)),
                                     ('w_mv_b', (1024, 256))]}

    with ExitStack() as st:
        def sb(name, shape, dt=F32):
            return st.enter_context(nc.sbuf_tensor(name, list(shape), dt))

        def pst(name, shape, dt=F32):
            return st.enter_context(nc.psum_tensor(name, list(shape), dt))

        P = Prog(nc)
        xtok = sb("xtok", [128, 4, 1024])
        xnb = [sb("xnb%d" % i, [128, 1024], BF16) for i in range(2)]
        xnT = sb("xnT", [128, 8, 512], BF16)
        upool = sb("upool", [128, 2, 527])
        xrnn = sb("xrnn", [128, 4, 515])
        gg = sb("gg", [128, 4, 512])
        qb = sb("qb", [128, 2, 512], BF16)
        S = [dict(xc=sb("S%d_xc" % i, [128, 512]), xcb=sb("S%d_xcb" % i, [128, 512], BF16),
                  thr=sb("S%d_thr" % i, [128, 512]), thi=sb("S%d_thi" % i, [128, 512]),
                  a2=sb("S%d_a2" % i, [128, 512])) for i in range(2)]
        o_pool = sb("o_pool", [128, 2, 512])
        o_xa = sb("o_xa", [128, 2, 512])
        sq = [sb("sq%d" % i, [128, 512], BF16) for i in range(2)]
        rbc = sb("rbc", [128, 3, 512])
        mixT = sb("mixT", [128, 8, 512], BF16)
        sA = sb("sA", [128, 2, 527]); sB = sb("sB", [128, 2, 527]); sC = sb("sC", [128, 527])
        diff = sb("diff", [128, 2, 512], BF16)
        KT = sb("KT", [128, 2, 256], BF16)
        Vz = sb("Vz", [128, 4, 2, 128], BF16)
        PT = [sb("PT%d" % i, [128, 2, 512], BF16) for i in range(2)]
        rd = sb("rd", [128, 512])
        kvf = [sb("kvf%d" % i, [128, 2, 256]) for i in range(2)]
        KTs = [sb("KTs%d" % i, [128, 2, 256], BF16) for i in range(2)]
        Vzs = [sb("Vzs%d" % i, [128, 4, 2, 128], BF16) for i in range(2)]
        gbuf = [sb("gbuf%d" % i, [128, 514]) for i in range(2)]
        ftmp = [sb("ftmp%d" % i, [128, 512]) for i in range(2)]
        hT = [sb("hT%d" % i, [128, 4, 512], BF16) for i in range(2)]
        gcar = sb("gcar", [128, 24, 2])
        gcar_s = sb("gcar_s", [128, 24, 16, 2])
        gfin = sb("gfin", [128, 1024])
        pv = sb("pv", [128, PV_TOT])
        ident = sb("ident", [128, 128], BF16); identf = sb("identf", [128, 128])
        ones_b = sb("ones_b", [128, 128], BF16)
        onesz = sb("onesz", [128, 2, 128], BF16)
        bdw = sb("bdw", [128, 10, 128], BF16)
        rc15 = sb("rc15", [128, 2, 15])
        hcar = sb("hcar", [128, 4])
        h0T = sb("h0T", [128, 4, 16])
        hls = sb("hls", [128, 4, 16])
        stg = sb("stg", [128, 128]); stg2 = [sb("stg2_%d" % i, [128, 128]) for i in range(2)]
        ss = sb("ss", [128, 8]); rstd = sb("rstd", [128, 8])
        t16 = sb("t16", [128, 16])
        gu = [sb("gu%d" % i, [128, 8192], BF16) for i in range(2)]
        dn = [sb("dn%d" % i, [128, 4096], BF16) for i in range(2)]
        psb = [pst("ps%d" % i, [128, 512]) for i in range(6)]
        ptr = pst("ptr", [128, 1024], BF16)
        ptr2 = pst("ptr2", [128, 1024], BF16)
        ptrs = [ptr, ptr2]
        bank_rr = [0]
        held = set()

        def bank(exclude=()):
            b = bank_rr[0]
            while b in exclude or b in held:
                b = (b + 1) % 6
            bank_rr[0] = (b + 1) % 6
            return b

        def act(out, in_, func, reads, writes, scale=None, bias=None, accum=None):
            kw = {}
            if scale is not None:
                kw['scale'] = scale
            if bias is not None:
                kw['bias'] = bias
            if accum is not None:
                kw['accum_out'] = accum
            P.op('act', lambda e: e.activation(out=out, in_=in_, func=func, **kw), reads, writes)

        def tsc(eng, out, in0, s1, s2, op0, op1, reads, writes):
            if s2 is None:
                P.op(eng, lambda e: e.tensor_scalar(out=out, in0=in0, scalar1=s1, scalar2=None, op0=op0), reads, writes)
            else:
                P.op(eng, lambda e: e.tensor_scalar(out=out, in0=in0, scalar1=s1, scalar2=s2, op0=op0, op1=op1), reads, writes)

        def stt(eng, out, in0, sc, in1, op0, op1, reads, writes):
            P.op(eng, lambda e: e.scalar_tensor_tensor(out=out, in0=in0, scalar=sc, in1=in1, op0=op0, op1=op1), reads, writes)

        def tt(eng, out, in0, in1, op, reads, writes):
            P.op(eng, lambda e: e.tensor_tensor(out=out, in0=in0, in1=in1, op=op), reads, writes)

        def cp(eng, out, in_, reads, writes):
            P.op(eng, lambda e: e.tensor_copy(out=out, in_=in_), reads, writes)

        def mset(eng, ap, v, writes):
            P.op(eng, lambda e: e.memset(ap, v), (), writes)

        def mm(out, lhsT, rhs, start, stop, reads, writes):
            P.op('pe', lambda e: e.matmul(out, lhsT=lhsT, rhs=rhs, start=start, stop=stop), reads, writes)

        def tr(out, in_, idn, reads, writes):
            P.op('pe', lambda e: e.transpose(out=out, in_=in_, identity=idn), reads, writes)

        def pvc(name, j=0):
            c = PV[name] + j
            return pv[:, c:c + 1]

        def rsqrt_inplace(ap, key):
            act(ap, ap, AF.Ln, [key], [key])
            act(ap, ap, AF.Exp, [key], [key], scale=-0.5)

        stg2_rr = [0]

        def rows_out(src_ap, ncols, src_keys, dmas):
            i = stg2_rr[0]
            stg2_rr[0] = 1 - i
            rb = bank()
            tr(psb[rb][0:ncols, 0:128], src_ap, identf[:], list(src_keys) + ['identf'], ['ps:%d' % rb])
            cp('dve', stg2[i][0:ncols, :], psb[rb][0:ncols, 0:128], ['ps:%d' % rb], ['stg2_%d' % i])
            for (dap, p0, step, n) in dmas:
                if step == 1:
                    sap = stg2[i][p0:p0 + n, :]
                else:
                    sap = stg2[i][p0:p0 + (n - 1) * step + 1:step, :]
                P.dma('sp', dap, sap, ['stg2_%d' % i], ())

        def cast(dst, src, key):
            P.dma('pool', dst, src, (), [key])

        NPASS = 5
        gu_seq = []
        d_seq = []
        for p in range(NPASS):
            gu_seq += [(p, 'in0'), (p, 'in1'), (p, 'out')] + [(p, 'gu%d' % sg) for sg in range(6)]
            d_seq += [(p, sg) for sg in range(6)]
        gu_next = [0]
        d_next = [0]
        gu_slot_of = {}
        d_slot_of = {}

        defer_store = [True]
        pending_store = []
        NDIRECT = 1

        def ring_load(p, dst, src32, src16, wbkey, wkeys, cast_pass):
            if p < NDIRECT:
                P.dma('pool', dst, src32.rearrange("(k p) n -> p k n", p=128), (), wkeys)
                st_fn = (lambda a=src16.rearrange("(k p) n -> p k n", p=128), b=dst, c=list(wkeys), d=wbkey:
                         P.dma('sp', a, b, c, [d]))
                if defer_store[0]:
                    pending_store.append(st_fn)
                else:
                    st_fn()
            else:
                P.dma('sp', dst, src16.rearrange("(k p) n -> p k n", p=128), [wbkey], wkeys)

        def gu_prefetch():
            i = gu_next[0]
            if i >= len(gu_seq):
                return
            gu_next[0] += 1
            s = i % 2
            item = gu_seq[i]
            gu_slot_of[item] = s
            p, k = item
            key = 'gu:%d' % s
            keyb = 'gub:%d' % s
            if k == 'in0' or k == 'in1':
                h = 0 if k == 'in0' else 1
                ring_load(p, gu[s][:, 0:6144].rearrange("p (k n) -> p k n", n=768), I['w_in'][:, h * 768:(h + 1) * 768],
                          WB['w_in_b'][:, h * 768:(h + 1) * 768], 'wb:' + k, [key, keyb], h)
            elif k == 'out':
                ring_load(p, gu[s][:, 0:8192].rearrange("p (k n) -> p k n", n=1024), I['w_out'], WB['w_out_b'], 'wb:out',
                          [key, keyb], 0)
            else:
                sg = int(k[2:])
                cs = slice(sg * 512, (sg + 1) * 512)
                ring_load(p, gu[s][:, 0:4096].rearrange("p (k n) -> p k n", n=512), I['w_ff_gate'][:, cs], WB['w_gate_b'][:, cs],
                          'wb:g%d' % sg, [key], sg % 2)
                ring_load(p, gu[s][:, 4096:8192].rearrange("p (k n) -> p k n", n=512), I['w_ff_up'][:, cs], WB['w_up_b'][:, cs],
                          'wb:u%d' % sg, [keyb], sg % 2)

        def d_prefetch():
            i = d_next[0]
            if i >= len(d_seq):
                return
            d_next[0] += 1
            s = i % 2
            item = d_seq[i]
            d_slot_of[item] = s
            p, sg = item
            rs = slice(sg * 512, (sg + 1) * 512)
            ring_load(p, dn[s][:, :].rearrange("p (k n) -> p k n", n=1024), I['w_ff_down'][rs, :], WB['w_down_b'][rs, :],
                      'wb:d%d' % sg, ['dn:%d' % s], sg % 2)

        gu_prefetch(); gu_prefetch()

        mset('pool', identf[:], 1.0, ['identf'])
        P.op('pool', lambda e: e.affine_select(out=identf[:], in_=identf[:], pattern=[[-1, 128]],
                                               compare_op=ALU.is_equal, fill=0.0, base=0, channel_multiplier=1),
             ['identf'], ['identf'])
        cp('pool', ident[:], identf[:], ['identf'], ['ident'])
        mset('pool', ones_b[:], 1.0, ['ones_b'])
        mset('pool', onesz[:], 0.0, ['onesz'])
        mset('pool', onesz[:, 0, 0:64], 1.0, ['onesz'])
        mset('pool', onesz[:, 1, 64:128], 1.0, ['onesz'])
        mset('pool', Vz[:], 0.0, ['Vz'])
        mset('pool', gcar[:], 0.0, ['gcar:%d' % c for c in range(24)])
        mset('pool', hcar[:], 0.0, ['hcar'])
        for c in range(2):
            for hh in range(2):
                w = (2, 4, 8, 16)[2 * c + hh]
                pr = slice(hh * 64, hh * 64 + 64)
                mset('pool', rc15[pr, c, :], 1.0 / w, ['rc15'])
                for t in range(w - 1):
                    mset('pool', rc15[pr, c, t:t + 1], 1.0 / (t + 1), ['rc15'])
        for j_ in range(4):
            st_ap, st_keys = [(gg[:, 0:2, :].rearrange("p a b -> p (a b)"), ['gg:0', 'gg:1']),
                              (gg[:, 2:4, :].rearrange("p a b -> p (a b)"), ['gg:2', 'gg:3']),
                              (o_pool[:, :, :].rearrange("p a b -> p (a b)"), ['o_pool:0', 'o_pool:1']),
                              (o_xa[:, :, :].rearrange("p a b -> p (a b)"), ['o_xa:0', 'o_xa:1'])][j_]
            P.dma('sp', st_ap, I['xp'][j_ * 128:(j_ + 1) * 128, :], (), st_keys)
        P.dma('sp', gfin[:], I['g_final'][0].partition_broadcast(128), (), ['gfin'])
        stA = xtok[:, 1, 512:640]
        stB = xtok[:, 1, 640:768]
        mset('dve', xtok[:, 1, 512:768], 0.0, ['xtok:1'])
        pst_keys = []
        for n in ['g_mix_norm', 'pool_scale', 'rnn_conv_w', 'rnn_conv_b', 'b_rg_a', 'b_rg_x', 'rg_lambda',
                  'g_mem_norm', 'g_mix_out', 'g_ffn_norm', 'ffn_conv_b']:
            r0 = PV[n]
            nr = I[n].shape[0]
            P.dma('sp', xtok[r0:r0 + nr, 1, 512:640], I[n], (), ['pst:' + n], after=['xtok:1'])
            pst_keys.append('pst:' + n)
        P.dma('sp', xtok[0:72, 1, 640:768], I['ffn_conv_w'], (), ['pst:fcw'], after=['xtok:1'])
        pst_keys.append('pst:fcw')
        tr(psb[0][:, 0:128], stA, identf[:], ['xtok:1', 'identf'] + pst_keys, ['ps:0'])
        tr(psb[0][:, 128:256], stB, identf[:], ['xtok:1', 'identf'] + pst_keys, ['ps:0'])
        cp('dve', pv[:, 0:256 - 56], psb[0][:, 0:200], ['ps:0'], ['pv'])
        tsc('dve', pv[:, PV_HBA:PV_HBA + 4], pv[:, PV['b_rg_a']:PV['b_rg_a'] + 4], 0.5, None, ALU.mult, None, ['pv'], ['pv'])
        tsc('dve', pv[:, PV_HBX:PV_HBX + 4], pv[:, PV['b_rg_x']:PV['b_rg_x'] + 4], 0.5, None, ALU.mult, None, ['pv'], ['pv'])
        act(pv[:, PV_HC:PV_HC + 4], pv[:, PV['rg_lambda']:PV['rg_lambda'] + 4], AF.Exp, ['pv'], ['pv'], scale=-1.0)
        act(pv[:, PV_HC:PV_HC + 4], pv[:, PV_HC:PV_HC + 4], AF.Ln, ['pv'], ['pv'], bias=1.0)
        tsc('dve', pv[:, PV_C2:PV_C2 + 4], pv[:, PV_HC:PV_HC + 4], -8.0, None, ALU.mult, None, ['pv'], ['pv'])
        tsc('dve', pv[:, PV_HC:PV_HC + 4], pv[:, PV_HC:PV_HC + 4], -4.0, None, ALU.mult, None, ['pv'], ['pv'])
        bst0 = xtok[:, 0, :]
        bst1 = xtok[:, 1, 0:256]
        mset('dve', xtok[:, 0, :], 0.0, ['xtok:0'])
        mset('dve', xtok[:, 1, 0:256], 0.0, ['xtok:1'])

        bds_keys = []

        def bd_dst(bi, hh):
            base = bst0 if bi < 8 else bst1
            b = bi if bi < 8 else bi - 8
            return base[hh * 64:hh * 64 + 64, b * 128 + hh * 64: b * 128 + hh * 64 + 64]

        for c in range(2):
            for hh in range(2):
                P.dma('sp', bd_dst(c, hh), I['w_pool'][2 * c + hh], (), ['bds:p%d%d' % (c, hh)], after=['xtok:0', 'xtok:1'])
                bds_keys.append('bds:p%d%d' % (c, hh))
        for c in range(4):
            for hh in range(2):
                P.dma('sp', bd_dst(2 + c, hh), I['w_rg_a'][2 * c + hh], (), ['bds:a%d%d' % (c, hh)], after=['xtok:0', 'xtok:1'])
                P.dma('sp', bd_dst(6 + c, hh), I['w_rg_x'][2 * c + hh], (), ['bds:x%d%d' % (c, hh)], after=['xtok:0', 'xtok:1'])
                bds_keys += ['bds:a%d%d' % (c, hh), 'bds:x%d%d' % (c, hh)]
        cp('dve', bdw[:, 0:8, :], xtok[:, 0, :].rearrange("p (b n) -> p b n", n=128), ['xtok:0'] + bds_keys, ['bdw'])
        cp('dve', bdw[:, 8:10, :], xtok[:, 1, 0:256].rearrange("p (b n) -> p b n", n=128), ['xtok:1'] + bds_keys, ['bdw'])


        def norm_units(ntt, tp, gname, srcs=None, dst=None, dkey='xnT'):
            if dst is None:
                dst = xnT
            if srcs is None:
                srcs = [(xtok[:, j, :], ['xtok:%d' % j]) for j in range(ntt)]
            for j in range(ntt):
                act(xnb[j % 2][:tp, :], srcs[j][0][:tp, :], AF.Square, srcs[j][1], ['xnb:%d' % (j % 2), 'ss'], accum=ss[:tp, j:j + 1])
            tsc('dve', rstd[:tp, 0:ntt], ss[:tp, 0:ntt], 1.0 / 1024, EPS, ALU.mult, ALU.add, ['ss'], ['rstd'])
            rsqrt_inplace(rstd[:tp, 0:ntt], 'rstd')
            yield

            def scale_(j):
                tsc('dve', xnb[j % 2][:tp, :], srcs[j][0][:tp, :], rstd[:tp, j:j + 1], None, ALU.mult, None,
                    list(srcs[j][1]) + ['rstd'], ['xnb:%d' % (j % 2)])

            def tr_(j):
                for k in range(8):
                    tr(ptrs[j % 2][:, k * 128:k * 128 + tp], xnb[j % 2][:tp, k * 128:(k + 1) * 128], ident[:tp, :tp],
                       ['xnb:%d' % (j % 2), 'ident'], ['ptr%d' % (j % 2)])

            def evac_(j):
                for k in range(8):
                    o = dst[:, k, j * 128:(j + 1) * 128]
                    i_ = ptrs[j % 2][:, k * 128:(k + 1) * 128]
                    if j % 2 == 0:
                        act(o, i_, AF.Identity, ['ptr0', 'pv'], ['%s:%d' % (dkey, k)], scale=pvc(gname, k))
                    else:
                        tsc('dve', o, i_, pvc(gname, k), None, ALU.mult, None, ['ptr1', 'pv'], ['%s:%d' % (dkey, k)])

            for step in range(ntt + 2):
                if step - 2 >= 0:
                    evac_(step - 2)
                if 0 <= step - 1 < ntt:
                    tr_(step - 1)
                if step < ntt:
                    scale_(step)
                yield

        def norm_and_transpose(ntt, tp, gname, srcs=None):
            for _ in norm_units(ntt, tp, gname, srcs):
                pass

        norm1_done = set()

        wmk = hT[0][:, :, :].rearrange("p a b -> p (a b)")[:, 0:2048].rearrange("p (k n) -> p k n", n=256)
        wmv = hT[1][:, :, :].rearrange("p a b -> p (a b)")[:, 0:2048].rearrange("p (k n) -> p k n", n=256)
        P.dma('pool', wmk, I['w_mem_k'].rearrange("(k p) n -> p k n", p=128), (), ['hT:0'])
        P.dma('pool', wmv, I['w_mem_v'].rearrange("(k p) n -> p k n", p=128), (), ['hT:1'])
        for j in range(2):
            P.dma('sp', xtok[:, 2 + j, :], I['mem'][j * 128:(j + 1) * 128, :], (), ['xtok:%d' % (2 + j)])
        memT = xnT
        for j in range(2):
            jj = 2 + j
            xb = xnb[j]
            xk = 'xnb:%d' % j
            act(xb[:, :], xtok[:, jj, :], AF.Square, ['xtok:%d' % jj], [xk, 'ss'], accum=ss[:, j:j + 1])
            tsc('dve', rstd[:, j:j + 1], ss[:, j:j + 1], 1.0 / 1024, EPS, ALU.mult, ALU.add, ['ss'], ['rstd'])
            rsqrt_inplace(rstd[:, j:j + 1], 'rstd')
            tsc('dve', xb[:, :], xtok[:, jj, :], rstd[:, j:j + 1], None, ALU.mult, None, ['xtok:%d' % jj, 'rstd'], [xk])
            for k in range(8):
                tr(ptr[:, k * 128:(k + 1) * 128], xb[:, k * 128:(k + 1) * 128], ident[:], [xk, 'ident'], ['ptr0'])
            for k in range(8):
                tsc('dve', memT[:, k, j * 128:(j + 1) * 128], ptr[:, k * 128:(k + 1) * 128], pvc('g_mem_norm', k), None,
                    ALU.mult, None, ['ptr0', 'pv'], ['xnT:%d' % k])
        def ffn_cast():
            pass

        for ch in range(2):
            b = bank()
            for k in range(8):
                mm(psb[b][:, 0:256], wmk[:, k, ch * 128:(ch + 1) * 128], memT[:, k, 0:256], k == 0, k == 7,
                   ['hT:0', 'xnT:%d' % k], ['ps:%d' % b])
            cp('dve', KT[:, ch, :], psb[b][:, 0:256], ['ps:%d' % b], ['KT'])
        for (wsrc, wkey, oname) in [(wmk, 'hT:0', 'mk_p'), (wmv, 'hT:1', 'mv_p')]:
            for mt in range(2):
                b = bank()
                for k in range(8):
                    mm(psb[b][:, 0:256], memT[:, k, mt * 128:(mt + 1) * 128], wsrc[:, k, :], k == 0, k == 7,
                       [wkey, 'xnT:%d' % k], ['ps:%d' % b])
                skey = 'S%d_xc' % mt
                cp('dve', S[mt]['xc'][:, 0:256], psb[b][:, 0:256], ['ps:%d' % b], [skey])
                P.dma('sp', O[oname][mt * 128:(mt + 1) * 128, :], S[mt]['xc'][:, 0:256], [skey], ())
                if oname == 'mv_p':
                    for h in range(4):
                        cp('pool', Vz[:, h, mt, (h % 2) * 64:(h % 2) * 64 + 64], S[mt]['xc'][:, h * 64:(h + 1) * 64],
                           [skey], ['Vz'])

        xstage = [(gg[:, 0:2, :].rearrange("p a b -> p (a b)"), ['gg:0', 'gg:1']),
                  (gg[:, 2:4, :].rearrange("p a b -> p (a b)"), ['gg:2', 'gg:3']),
                  (o_pool[:, :, :].rearrange("p a b -> p (a b)"), ['o_pool:0', 'o_pool:1']),
                  (o_xa[:, :, :].rearrange("p a b -> p (a b)"), ['o_xa:0', 'o_xa:1'])]

        def run_pass(pidx, kind, sti):
            if kind == 's':
                nseq, T, N, ntt, tp = 16, 4, 64, 1, 64
                xsrc = [I['xs']]
                ydst = [O['y_s']]
            else:
                nseq, T, N, ntt, tp = 1, 512, 512, 4, 128
                xsrc = [I['xp'][sti * 512 + j * 128: sti * 512 + (j + 1) * 128, :] for j in range(4)]
                ydst = [O['y_p'][sti * 512 + j * 128: sti * 512 + (j + 1) * 128, :] for j in range(4)]
            last = (kind == 'p' and sti == 3)
            WU = 15 + T
            WX = 3 + T
            WG = 2 + T

            def fv(ap2d):
                return ap2d.rearrange("p (s t) -> p s t", t=T)

            def hv(buf, c, W, lo, hi, pr=slice(0, 128)):
                b2 = buf[pr, c, 0:nseq * W] if c is not None else buf[pr, 0:nseq * W]
                return b2.rearrange("p (s w) -> p s w", w=W)[:, :, lo:hi]

            if defer_store[0]:
                defer_store[0] = False
                for fn_ in pending_store:
                    fn_()
            if kind == 'p' and sti == 0:
                mset('pool', upool[:, :, 0:15], 0.0, ['upool:0', 'upool:1'])
                mset('pool', xrnn[:, :, 0:3], 0.0, ['xrnn:%d' % c for c in range(4)])
            for j in range(ntt):
                P.dma('sp', xtok[:tp, j, :], xsrc[j], (), ['xtok:%d' % j])
            if kind == 's':
                for n_ in range(8):
                    if n_ < 2:
                        ap_, wk_ = kvf[n_][:, :, :], ['kvf:%d' % n_]
                    else:
                        t_ = 1 + (n_ - 2) // 2
                        h_ = (n_ - 2) % 2
                        ap_ = xtok[:, t_, h_ * 512:(h_ + 1) * 512].rearrange("p (a b) -> p a b", b=256)
                        wk_ = ['kvf:%d' % n_, 'xtok:%d' % t_]
                    P.dma('sp', ap_, I['ck'][n_].rearrange("(mt p) n -> p mt n", p=128), (), wk_)
            if pidx in norm1_done:
                xin_buf, xin_key = mixT, 'mixT'
            else:
                xin_buf, xin_key = xnT, 'xnT'
                if kind == 'p':
                    norm_and_transpose(ntt, tp, 'g_mix_norm', srcs=xstage)
                else:
                    norm_and_transpose(ntt, tp, 'g_mix_norm')
            for half in range(2):
                s = gu_slot_of[(pidx, 'in%d' % half)]
                wv = gu[s][:, 0:6144].rearrange("p (k n) -> p k n", n=768)
                for ml in range(6):
                    m = half * 6 + ml
                    b = bank()
                    for k in range(8):
                        mm(psb[b][:, 0:N], wv[:, k, ml * 128:(ml + 1) * 128], xin_buf[:, k, 0:N], k == 0, k == 7,
                           ['gu:%d' % s, 'gub:%d' % s, '%s:%d' % (xin_key, k)], ['ps:%d' % b])
                    pk = ['ps:%d' % b]
                    if m < 2:
                        act(hv(upool, m, WU, 15, 15 + T), fv(psb[b][:, 0:N]), AF.Copy, pk, ['upool:%d' % m])
                    elif m < 6:
                        act(hv(xrnn, m - 2, WX, 3, 3 + T), fv(psb[b][:, 0:N]), AF.Copy, pk, ['xrnn:%d' % (m - 2)])
                    elif m < 10:
                        act(gg[:, m - 6, 0:N], psb[b][:, 0:N], AF.Gelu_apprx_tanh, pk, ['gg:%d' % (m - 6)])
                    else:
                        cp('dve', qb[:, m - 10, 0:N], psb[b][:, 0:N], pk, ['qb:%d' % (m - 10)])
                if half == 0:
                    ffn_cast()
                gu_prefetch()

            groups = {0: ([(o_pool, 0, 'o_pool:0'), (o_pool, 1, 'o_pool:1')], 256, 0),
                      1: ([(gg, c, 'gg:%d' % c) for c in range(4)], 512, 2),
                      2: ([(o_xa, 0, 'o_xa:0'), (o_xa, 1, 'o_xa:1')], 256, 6)}
            sqi_ = [0]

            def rms_group(g):
                chunks, nfe, mi = groups[g]
                b = bank()
                for i, (buf, c, key) in enumerate(chunks):
                    sqb = sq[sqi_[0] % 2]
                    sqk = 'sq:%d' % (sqi_[0] % 2)
                    sqi_[0] += 1
                    act(sqb[:, 0:N], buf[:, c, 0:N], AF.Square, [key], [sqk])
                    mm(psb[b][:, 0:N], ones_b[:, :], sqb[:, 0:N], i == 0, i == len(chunks) - 1, ['ones_b', sqk], ['ps:%d' % b])
                tsc('dve', rbc[:, g, 0:N], psb[b][:, 0:N], 1.0 / nfe, EPS, ALU.mult, ALU.add, ['ps:%d' % b], ['rbc:%d' % g])
                rsqrt_inplace(rbc[:, g, 0:N], 'rbc:%d' % g)
                for (buf, c, key) in chunks:
                    stt('dve', mixT[:, mi, 0:N], buf[:, c, 0:N], pvc('g_mix_out', mi), rbc[:, g, 0:N], ALU.mult, ALU.mult,
                        [key, 'pv', 'rbc:%d' % g], ['mixT:%d' % mi])
                    mi += 1

            if kind == 'p':
                att_pending = []
                for ch in range(2):
                    bo = bank()
                    held.add(bo)
                    bd_ = bank()
                    held.add(bd_)
                    att_pending.append((ch, bo, bd_))
                    for e_ in range(2):
                        h = 2 * ch + e_
                        pr = slice(e_ * 64, e_ * 64 + 64)
                        pt = PT[h % 2]
                        ptk = 'PT:%d' % (h % 2)
                        for mt in range(2):
                            b = bank()
                            mm(psb[b][:, 0:N], KT[pr, ch, mt * 128:(mt + 1) * 128], qb[pr, ch, 0:N], True, True,
                               ['KT', 'qb:%d' % ch], ['ps:%d' % b])
                            act(pt[:, mt, 0:N], psb[b][:, 0:N], AF.Exp, ['ps:%d' % b], [ptk], scale=0.125)
                        for mt in range(2):
                            first = (e_ == 0 and mt == 0)
                            lastm = (e_ == 1 and mt == 1)
                            mm(psb[bo][:, 0:N], Vz[:, h, mt, :], pt[:, mt, 0:N], first, lastm, ['Vz', ptk], ['ps:%d' % bo])
                            mm(psb[bd_][:, 0:N], onesz[:, e_, :], pt[:, mt, 0:N], first, lastm, ['onesz', ptk], ['ps:%d' % bd_])

                def att_partB():
                    for (ch, bo, bd_) in att_pending:
                        P.op('dve', (lambda bd_=bd_: lambda e: e.reciprocal(out=rd[:, 0:N], in_=psb[bd_][:, 0:N]))(), ['ps:%d' % bd_], ['rd'])
                        tt('dve', o_xa[:, ch, 0:N], psb[bo][:, 0:N], rd[:, 0:N], ALU.mult, ['ps:%d' % bo, 'rd'], ['o_xa:%d' % ch])
                        held.discard(bo)
                        held.discard(bd_)
                    rms_group(2)
            else:
                bs = [bank(), bank()]
                NSLOT = 8

                def kvslot(n):
                    i = n % NSLOT
                    if i < 2:
                        return kvf[i][:, :, :], 'kvf:%d' % i, None
                    t_ = 1 + (i - 2) // 2
                    h_ = (i - 2) % 2
                    return (xtok[:, t_, h_ * 512:(h_ + 1) * 512].rearrange("p (a b) -> p a b", b=256), 'kvf:%d' % i, 'xtok:%d' % t_)

                def kvload(n):
                    ap, key, xk_ = kvslot(n)
                    src = I['ck'][n] if n < 16 else I['cv'][n - 16]
                    wk = [key] + ([xk_] if (xk_ is not None and n < NSLOT) else [])
                    P.dma('sp', ap, src.rearrange("(mt p) n -> p mt n", p=128), (), wk)

                def kprep(bb):
                    ap, kk, _ = kvslot(bb)
                    b = bank(exclude=bs)
                    for ch in range(2):
                        for mt in range(2):
                            tr(psb[b][:, (ch * 2 + mt) * 128:(ch * 2 + mt + 1) * 128], ap[:, mt, ch * 128:(ch + 1) * 128],
                               identf[:], [kk, 'identf'], ['ps:%d' % b])
                    dst = KTs[bb % 2][:, :, :].rearrange("p a b -> p (a b)")
                    if bb % 2 == 0:
                        cp('dve', dst, psb[b][:, 0:512], ['ps:%d' % b], ['KTs:%d' % (bb % 2)])
                    else:
                        act(dst, psb[b][:, 0:512], AF.Copy, ['ps:%d' % b], ['KTs:%d' % (bb % 2)])
                    kvload(bb + NSLOT)

                def kscores(bb):
                    cols = slice(bb * 4, bb * 4 + 4)
                    for e_ in range(2):
                        pr = slice(e_ * 64, e_ * 64 + 64)
                        for ch in range(2):
                            for mt in range(2):
                                c0 = (mt * 2 + ch) * 64 + bb * 4
                                mm(psb[bs[e_]][:, c0:c0 + 4], KTs[bb % 2][pr, ch, mt * 128:(mt + 1) * 128], qb[pr, ch, cols], True, True,
                                   ['KTs:%d' % (bb % 2), 'qb:%d' % ch], ['ps:%d' % bs[e_]])

                kprep(0)
                for bb in range(16):
                    if bb + 1 < 16:
                        kprep(bb + 1)
                    kscores(bb)
                for e_ in range(2):
                    act(PT[1][:, e_, 0:256], psb[bs[e_]][:, 0:256], AF.Exp, ['ps:%d' % bs[e_]], ['PT:1'], scale=0.125)
                bo = [bank(), bank()]
                bdn = [bank(), bank()]

                def vb16(bb):
                    return Vzs[bb % 2][:, :, :, :].rearrange("p a b c -> p (a b c)")[:, 0:512].rearrange("p (a b) -> p a b", b=256)

                def vprep(bb):
                    ap, kk, _ = kvslot(16 + bb)
                    if bb % 2 == 0:
                        act(vb16(bb), ap, AF.Copy, [kk], ['Vzs:%d' % (bb % 2)])
                    else:
                        cp('dve', vb16(bb), ap, [kk], ['Vzs:%d' % (bb % 2)])
                    if 16 + bb + NSLOT < 32:
                        kvload(16 + bb + NSLOT)

                def pvmm(bb):
                    vk = 'Vzs:%d' % (bb % 2)
                    vb = vb16(bb)
                    for ch in range(2):
                        for e_ in range(2):
                            oc = slice(e_ * 64 + bb * 4, e_ * 64 + bb * 4 + 4)
                            for mt in range(2):
                                c0 = (mt * 2 + ch) * 64 + bb * 4
                                mm(psb[bo[ch]][:, oc], vb[:, mt, ch * 128:(ch + 1) * 128], PT[1][:, e_, c0:c0 + 4], mt == 0, mt == 1,
                                   [vk, 'PT:1'], ['ps:%d' % bo[ch]])
                                mm(psb[bdn[ch]][:, oc], ones_b[:, :], PT[1][:, e_, c0:c0 + 4], mt == 0, mt == 1,
                                   ['ones_b', 'PT:1'], ['ps:%d' % bdn[ch]])

                vprep(0)
                for bb in range(16):
                    if bb + 1 < 16:
                        vprep(bb + 1)
                    pvmm(bb)
                for ch in range(2):
                    for e_ in range(2):
                        pr = slice(e_ * 64, e_ * 64 + 64)
                        P.op('dve', (lambda ch=ch, e_=e_, pr=pr: lambda e: e.reciprocal(
                            out=rd[pr, e_ * 64:(e_ + 1) * 64], in_=psb[bdn[ch]][pr, e_ * 64:(e_ + 1) * 64]))(), ['ps:%d' % bdn[ch]], ['rd'])
                        tt('dve', o_xa[pr, ch, 0:N], psb[bo[ch]][pr, e_ * 64:(e_ + 1) * 64], rd[pr, e_ * 64:(e_ + 1) * 64], ALU.mult,
                           ['ps:%d' % bo[ch], 'rd'], ['o_xa:%d' % ch])

                def att_partB():
                    rms_group(2)

            for c in range(2):
                uk = 'upool:%d' % c
                tt('dve', hv(sA, c, WU, 1, WU), hv(upool, c, WU, 1, WU), hv(upool, c, WU, 0, WU - 1), ALU.add, [uk], ['sA:%d' % c])
            hi = slice(64, 128)
            lo = slice(0, 64)
            tt('dve', hv(sB, 0, WU, 3, WU, hi), hv(sA, 0, WU, 3, WU, hi), hv(sA, 0, WU, 1, WU - 2, hi), ALU.add, ['sA:0'], ['sB:0'])
            tt('dve', hv(sB, 1, WU, 3, WU), hv(sA, 1, WU, 3, WU), hv(sA, 1, WU, 1, WU - 2), ALU.add, ['sA:1'], ['sB:1'])
            tt('dve', hv(sC, None, WU, 7, WU), hv(sB, 1, WU, 7, WU), hv(sB, 1, WU, 3, WU - 4), ALU.add, ['sB:1'], ['sC'])
            tt('dve', hv(sA, 1, WU, 15, WU, hi), hv(sC, None, WU, 15, WU, hi), hv(sC, None, WU, 7, WU - 8, hi), ALU.add, ['sC'], ['sA:1'])
            fin_src = [(0, lo, sA, 0, 'sA:0', 2), (0, hi, sB, 0, 'sB:0', 4), (1, lo, sC, None, 'sC', 8), (1, hi, sA, 1, 'sA:1', 16)]
            for (c, pr, buf, bc, bkey, w) in fin_src:
                stt('dve', fv(diff[pr, c, 0:N]), hv(buf, bc, WU, 15, WU, pr), 1.0 / w, hv(upool, c, WU, 15, WU, pr),
                    ALU.mult, ALU.subtract, [bkey, 'upool:%d' % c], ['diff:%d' % c])
            if kind == 'p' and sti == 0:
                for (c, pr, buf, bc, bkey, w) in fin_src:
                    src = buf[pr, bc, 15:30] if bc is not None else buf[pr, 15:30]
                    tt('dve', t16[pr, 0:15], src, rc15[pr, c, :], ALU.mult, [bkey, 'rc15'], ['t16'])
                    tt('dve', diff[pr, c, 0:15], t16[pr, 0:15], upool[pr, c, 15:30], ALU.subtract, ['t16', 'upool:%d' % c], ['diff:%d' % c])
            for c in range(2):
                b = bank()
                mm(psb[b][:, 0:N], bdw[:, c, :], diff[:, c, 0:N], True, True, ['bdw', 'diff:%d' % c], ['ps:%d' % b])
                act(o_pool[:, c, 0:N], psb[b][:, 0:N], AF.Identity, ['ps:%d' % b, 'pv'], ['o_pool:%d' % c], scale=pvc('pool_scale', c))
            if kind == 's':
                for c in range(2):
                    cp('pool', stg[:, 0:64].rearrange("p (s t) -> p s t", t=4), hv(upool, c, WU, 15, 19), ['upool:%d' % c], ['stg'])
                    rows_out(stg[:, 0:64], 64, ['stg'],
                             [(O['pool_s'][:, 11 + t, c * 128:(c + 1) * 128], t, 4, 16) for t in range(4)])
            else:
                if last:
                    cp('pool', stg[:, 0:30].rearrange("p (c r) -> p c r", r=15), upool[:, :, 512:527], ['upool:0', 'upool:1'], ['stg'])
                    rows_out(stg[:, 0:30], 30, ['stg'],
                             [(O['pool_p'][:, c * 128:(c + 1) * 128], c * 15, 1, 15) for c in range(2)])
                else:
                    for c in range(2):
                        cp('pool', upool[:, c, 0:15], upool[:, c, 512:527], ['upool:%d' % c], ['upool:%d' % c])

            att_partB()
            rms_group(0)
            ffn_cast()
            RGB = {
                0: dict(xc=(S[0]['xc'][:, 0:512], 'S0_xc'), xcb=(S[0]['xcb'][:, 0:512], 'S0_xcb'), thr=(S[0]['thr'][:, 0:512], 'S0_thr'),
                        thi=(S[0]['thi'][:, 0:512], 'S0_thi'), a2=(S[0]['a2'][:, 0:512], 'S0_a2')),
                1: dict(xc=(S[1]['xc'][:, 0:512], 'S1_xc'), xcb=(S[1]['xcb'][:, 0:512], 'S1_xcb'), thr=(S[1]['thr'][:, 0:512], 'S1_thr'),
                        thi=(S[1]['thi'][:, 0:512], 'S1_thi'), a2=(S[1]['a2'][:, 0:512], 'S1_a2')),
                2: dict(xc=(rd[:, 0:512], 'rd'), xcb=(diff[:, 0, :], 'diff:0'), thr=(sA[:, 0, 0:512], 'sA:0'),
                        thi=(sA[:, 1, 0:512], 'sA:1'), a2=(o_pool[:, 0, :], 'o_pool:0')),
                3: dict(xc=(sC[:, 0:512], 'sC'), xcb=(diff[:, 1, :], 'diff:1'), thr=(sB[:, 0, 0:512], 'sB:0'),
                        thi=(sB[:, 1, 0:512], 'sB:1'), a2=(o_pool[:, 1, :], 'o_pool:1')),
            }

            def rb(c, n):
                return RGB[c][n][0][:, 0:N]

            def rk(c, n):
                return RGB[c][n][1]

            C4 = range(4)
            for c in C4:
                act(fv(rb(c, 'xc')), hv(xrnn, c, WX, 0, T), AF.Identity, ['xrnn:%d' % c, 'pv'], [rk(c, 'xc')],
                    scale=pvc('rnn_conv_w', 0 * 4 + c), bias=pvc('rnn_conv_b', c))
            for c in C4:
                xcv = fv(rb(c, 'xc'))
                for tap in range(1, 4):
                    stt('dve', xcv, hv(xrnn, c, WX, tap, tap + T), pvc('rnn_conv_w', tap * 4 + c), xcv, ALU.mult, ALU.add,
                        ['xrnn:%d' % c, 'pv', rk(c, 'xc')], [rk(c, 'xc')])
            for c in C4:
                cp('pool', rb(c, 'xcb'), rb(c, 'xc'), [rk(c, 'xc')], [rk(c, 'xcb')])
                ffn_cast()
                if kind == 'p' and not last:
                    cp('pool', xrnn[:, c, 0:3], xrnn[:, c, 512:515], ['xrnn:%d' % c], ['xrnn:%d' % c])
            for pair in range(2):
                cs = [2 * pair, 2 * pair + 1]
                gb = {}
                for c in cs:
                    ba = bank()
                    mm(psb[ba][:, 0:N], bdw[:, 2 + c, :], rb(c, 'xcb'), True, True, ['bdw', rk(c, 'xcb')], ['ps:%d' % ba])
                    bx = bank()
                    mm(psb[bx][:, 0:N], bdw[:, 6 + c, :], rb(c, 'xcb'), True, True, ['bdw', rk(c, 'xcb')], ['ps:%d' % bx])
                    gb[c] = (ba, bx)
                for c in cs:
                    ba, bx = gb[c]
                    act(rb(c, 'thr'), psb[ba][:, 0:N], AF.Tanh, ['ps:%d' % ba, 'pv'], [rk(c, 'thr')], scale=0.5, bias=pv[:, PV_HBA + c:PV_HBA + c + 1])
                    act(rb(c, 'thi'), psb[bx][:, 0:N], AF.Tanh, ['ps:%d' % bx, 'pv'], [rk(c, 'thi')], scale=0.5, bias=pv[:, PV_HBX + c:PV_HBX + c + 1])
            for c in C4:
                hc_ = pv[:, PV_HC + c:PV_HC + c + 1]
                c2_ = pv[:, PV_C2 + c:PV_C2 + c + 1]
                act(rb(c, 'a2'), rb(c, 'thr'), AF.Exp, [rk(c, 'thr'), 'pv'], [rk(c, 'a2')], scale=c2_, bias=c2_)
                act(rb(c, 'thr'), rb(c, 'thr'), AF.Exp, [rk(c, 'thr'), 'pv'], [rk(c, 'thr')], scale=hc_, bias=hc_)
            for c in C4:
                tsc('dve', rb(c, 'a2'), rb(c, 'a2'), -1.0, 1.0 + 1.2e-7, ALU.mult, ALU.add, [rk(c, 'a2')], [rk(c, 'a2')])
                stt('dve', rb(c, 'thi'), rb(c, 'thi'), 1.0, rb(c, 'xc'), ALU.add, ALU.mult, [rk(c, 'thi'), rk(c, 'xc')], [rk(c, 'thi')])
            for c in C4:
                act(rb(c, 'a2'), rb(c, 'a2'), AF.Ln, [rk(c, 'a2')], [rk(c, 'a2')])
                act(rb(c, 'a2'), rb(c, 'a2'), AF.Exp, [rk(c, 'a2')], [rk(c, 'a2')], scale=0.5)
            for c in C4:
                stt('dve', rb(c, 'thi'), rb(c, 'thi'), 0.5, rb(c, 'a2'), ALU.mult, ALU.mult, [rk(c, 'thi'), rk(c, 'a2')], [rk(c, 'thi')])
                if kind == 's':
                    a0 = fv(rb(c, 'thr'))[:, :, 0]
                    b0 = fv(rb(c, 'thi'))[:, :, 0]
                    tt('dve', t16[:, 0:16], a0, h0T[:, c, :], ALU.mult, [rk(c, 'thr'), 'h0T'], ['t16'])
                    tt('dve', b0, b0, t16[:, 0:16], ALU.add, [rk(c, 'thi'), 't16'], [rk(c, 'thi')])
                    mset('dve', a0, 0.0, [rk(c, 'thr')])
                    init = 0.0
                else:
                    init = hcar[:, c:c + 1]
                P.op('dve', (lambda c=c, init=init: lambda e: e.tensor_tensor_scan(
                    out=rb(c, 'a2'), data0=rb(c, 'thr'), data1=rb(c, 'thi'), initial=init,
                    op0=ALU.mult, op1=ALU.add))(), [rk(c, 'thr'), rk(c, 'thi'), 'hcar'], [rk(c, 'a2')])
                if kind == 'p':
                    cp('dve', hcar[:, c:c + 1], rb(c, 'a2')[:, N - 1:N], [rk(c, 'a2')], ['hcar'])
                else:
                    cp('dve', hls[:, c, :], fv(rb(c, 'a2'))[:, :, 3], [rk(c, 'a2')], ['hls'])
                tt('dve', gg[:, c, 0:N], gg[:, c, 0:N], rb(c, 'a2'), ALU.mult, ['gg:%d' % c, rk(c, 'a2')], ['gg:%d' % c])
            if kind == 's':
                for half in range(2):
                    for ci in range(2):
                        c = half * 2 + ci
                        cp('pool', stg[:, ci * 48:(ci + 1) * 48].rearrange("p (s r) -> p s r", r=3), hv(xrnn, c, WX, 4, 7), ['xrnn:%d' % c], ['stg'])
                    dl = []
                    for ci in range(2):
                        c = half * 2 + ci
                        for r in range(3):
                            dl.append((O['conv_s'][:, r, c * 128:(c + 1) * 128], ci * 48 + r, 3, 16))
                    rows_out(stg[:, 0:96], 96, ['stg'], dl)
                cp('pool', stg[:, 0:64].rearrange("p (c s) -> p c s", s=16), hls[:, :, :], ['hls'], ['stg'])
                rows_out(stg[:, 0:64], 64, ['stg'], [(O['h_s'][:, c * 128:(c + 1) * 128], c * 16, 1, 16) for c in range(4)])
            elif last:
                cp('pool', stg[:, 0:12].rearrange("p (c r) -> p c r", r=3), xrnn[:, :, 512:515], ['xrnn:%d' % c for c in range(4)], ['stg'])
                cp('pool', stg[:, 12:16], hcar[:, :], ['hcar'], ['stg'])
                rows_out(stg[:, 0:16], 16, ['stg'],
                         [(O['conv_p'][:, c * 128:(c + 1) * 128], c * 3, 1, 3) for c in range(4)] +
                         [(O['h_p'][:, c * 128:(c + 1) * 128], 12 + c, 1, 1) for c in range(4)])

            rms_group(1)

            s = gu_slot_of[(pidx, 'out')]
            wv = gu[s][:, 0:8192].rearrange("p (k n) -> p k n", n=1024)
            for j in range(ntt):
                for half in range(2):
                    b = bank()
                    for k in range(8):
                        mm(psb[b][:tp, 0:512], mixT[:, k, j * 128:j * 128 + tp], wv[:, k, half * 512:(half + 1) * 512], k == 0, k == 7,
                           ['gu:%d' % s, 'gub:%d' % s, 'mixT:%d' % k], ['ps:%d' % b])
                    xs_ = xtok[:tp, j, half * 512:(half + 1) * 512]
                    tt('dve', xs_, xs_, psb[b][:tp, 0:512], ALU.add, ['xtok:%d' % j, 'ps:%d' % b], ['xtok:%d' % j])
            gu_prefetch()
            if d_next[0] == 0:
                d_prefetch(); d_prefetch()

            if kind == 'p' and sti < 3:
                for j in range(4):
                    P.dma('sp', xstage[j][0], I['xp'][(sti + 1) * 512 + j * 128:(sti + 1) * 512 + (j + 1) * 128, :], (), xstage[j][1])
            if last:
                state_loads()
                st_gen[0] = state_units()
            nu = [None]
            if kind == 'p' and sti < 3:
                nu[0] = norm_units(4, 128, 'g_mix_norm', xstage, mixT, 'mixT')
                norm1_done.add(pidx + 1)
            norm_and_transpose(ntt, tp, 'g_ffn_norm')

            def down_part(sg):
                ds = d_slot_of[(pidx, sg)]
                dv = dn[ds][:, :].rearrange("p (k n) -> p k n", n=1024)
                hb = hT[sg % 2]
                for j in range(ntt):
                    for half in range(2):
                        b = bank()
                        for cc in range(4):
                            mm(psb[b][:tp, 0:512], hb[:, cc, j * 128:j * 128 + tp], dv[:, cc, half * 512:(half + 1) * 512], cc == 0, cc == 3,
                               ['dn:%d' % ds, 'hT:%d' % (sg % 2)], ['ps:%d' % b])
                        xs_ = xtok[:tp, j, half * 512:(half + 1) * 512]
                        tt('dve', xs_, xs_, psb[b][:tp, 0:512], ALU.add, ['xtok:%d' % j, 'ps:%d' % b], ['xtok:%d' % j])
                d_prefetch()

            for sg in range(6):
                s = gu_slot_of[(pidx, 'gu%d' % sg)]
                gv = gu[s][:, 0:4096].rearrange("p (k n) -> p k n", n=512)
                uv = gu[s][:, 4096:8192].rearrange("p (k n) -> p k n", n=512)
                for cc in range(4):
                    c = sg * 4 + cc
                    par = c % 2
                    gk = 'gbuf:%d' % par
                    fk = 'ftmp:%d' % par
                    bg = bank()
                    for k in range(8):
                        mm(psb[bg][:, 0:N], gv[:, k, cc * 128:(cc + 1) * 128], xnT[:, k, 0:N], k == 0, k == 7,
                           ['gu:%d' % s, 'gub:%d' % s, 'xnT:%d' % k], ['ps:%d' % bg])
                    bu = bank()
                    for k in range(8):
                        mm(psb[bu][:, 0:N], uv[:, k, cc * 128:(cc + 1) * 128], xnT[:, k, 0:N], k == 0, k == 7,
                           ['gu:%d' % s, 'gub:%d' % s, 'xnT:%d' % k], ['ps:%d' % bu])
                    gb3 = gbuf[par][:, 0:nseq * WG].rearrange("p (s w) -> p s w", w=WG)
                    if kind == 's':
                        cp('pool', gb3[:, :, 0:2], gcar_s[:, c, :, :], ['gcs:%d' % c], [gk])
                    else:
                        cp('pool', gb3[:, :, 0:2], gcar[:, c:c + 1, :], ['gcar:%d' % c], [gk])
                    act(gb3[:, :, 2:2 + T], fv(psb[bg][:, 0:N]), AF.Copy, ['ps:%d' % bg], [gk])
                    fw_ = PV['ffn_conv_w']
                    ft3 = fv(ftmp[par][:, 0:N])
                    act(ft3, fv(psb[bg][:, 0:N]), AF.Identity, ['ps:%d' % bg, 'pv'], [fk],
                        scale=pv[:, fw_ + 48 + c:fw_ + 48 + c + 1], bias=pvc('ffn_conv_b', c))
                    stt('dve', ft3, gb3[:, :, 1:1 + T], pv[:, fw_ + 24 + c:fw_ + 24 + c + 1], ft3, ALU.mult, ALU.add, [gk, 'pv', fk], [fk])
                    stt('dve', ft3, gb3[:, :, 0:T], pv[:, fw_ + c:fw_ + c + 1], ft3, ALU.mult, ALU.add, [gk, 'pv', fk], [fk])
                    if kind == 's':
                        cp('pool', gcar_s[:, c, :, :], gb3[:, :, T:T + 2], [gk], ['gcs:%d' % c])
                    else:
                        cp('pool', gcar[:, c:c + 1, :], gb3[:, :, T:T + 2], [gk], ['gcar:%d' % c])
                    act(ftmp[par][:, 0:N], ftmp[par][:, 0:N], AF.Gelu_apprx_tanh, [fk], [fk])
                    tt('dve', hT[sg % 2][:, cc, 0:N], ftmp[par][:, 0:N], psb[bu][:, 0:N], ALU.mult, [fk, 'ps:%d' % bu], ['hT:%d' % (sg % 2)])
                    if cc == 1 and sg > 0:
                        down_part(sg - 1)
                    if last and sg >= 1:
                        for _ in range(2):
                            next(st_gen[0], None)
                    if nu[0] is not None and ((sg == 3 and cc >= 1) or sg == 4):
                        next(nu[0], None)
                gu_prefetch()
                if kind == 's':
                    src = gcar_s[:, sg * 4:(sg + 1) * 4, :, :].rearrange("p c s r -> p (c s r)")
                    dl = []
                    for ci in range(4):
                        c = sg * 4 + ci
                        for r in range(2):
                            dl.append((O['ffn_s'][:, r, c * 128:(c + 1) * 128], ci * 32 + r, 2, 16))
                    rows_out(src, 128, ['gcs:%d' % (sg * 4 + ci) for ci in range(4)], dl)
            if nu[0] is not None:
                for _ in nu[0]:
                    pass
            if last:
                for _ in st_gen[0]:
                    pass
            down_part(5)

            if kind == 's':
                pass
            elif last:
                src = gcar[:, :, :].rearrange("p c r -> p (c r)")
                dl = []
                for c in range(24):
                    for r in range(2):
                        dl.append((O['ffn_p'][r:r + 1, c * 128:(c + 1) * 128], c * 2 + r, 1, 1))
                rows_out(src, 48, ['gcar:%d' % c for c in range(24)], dl)

            for j in range(ntt):
                act(xnb[j % 2][:tp, :], xtok[:tp, j, :], AF.Square, ['xtok:%d' % j], ['xnb:%d' % (j % 2), 'ss'], accum=ss[:tp, 4 + j:5 + j])
            tsc('dve', rstd[:tp, 4:4 + ntt], ss[:tp, 4:4 + ntt], 1.0 / 1024, EPS, ALU.mult, ALU.add, ['ss'], ['rstd'])
            rsqrt_inplace(rstd[:tp, 4:4 + ntt], 'rstd')
            for j in range(ntt):
                stt('dve', xtok[:tp, j, :], xtok[:tp, j, :], rstd[:tp, 4 + j:5 + j], gfin[:tp, :], ALU.mult, ALU.mult,
                    ['xtok:%d' % j, 'rstd', 'gfin'], ['xtok:%d' % j])
                P.dma('sp', ydst[j], xtok[:tp, j, :], ['xtok:%d' % j], ())

        st_gen = [None]

        def state_loads():
            stf_a = gg[0:32, :, :].rearrange("p a b -> p (a b)")
            stf_b = rbc[0:32, 0:2, :].rearrange("p a b -> p (a b)")
            P.dma('sp', stf_a, I['st_ffn'][:, 0:2048], (), ['gg:0', 'gg:1', 'gg:2', 'gg:3'])
            P.dma('sp', stf_b, I['st_ffn'][:, 2048:3072], (), ['rbc:0', 'rbc:1'])
            P.dma('sp', o_pool[:, 0, 0:256], I['st_pool'][0:128, :], (), ['o_pool:0'])
            P.dma('sp', o_pool[0:112, 1, 0:256], I['st_pool'][128:240, :], (), ['o_pool:1'])
            P.dma('sp', o_xa[0:48, 0, :], I['st_conv'], (), ['o_xa:0'])
            P.dma('sp', o_xa[0:16, 1, :], I['st_h'], (), ['o_xa:1'])
            P.dma('sp', O['pool_s'][:, 0:11, :], I['st_pool'].rearrange("(b r) c -> b r c", r=15)[:, 4:15, :], (), ())

        def state_units():
            stf_a = gg[0:32, :, :].rearrange("p a b -> p (a b)")
            stf_b = rbc[0:32, 0:2, :].rearrange("p a b -> p (a b)")

            def st_tr(src_ap, nr, src_keys):
                b = bank()
                tr(psb[b][:, 0:nr], src_ap, identf[0:nr, 0:nr], list(src_keys) + ['identf'], ['ps:%d' % b])
                return b

            for c in range(2):
                flat = upool[:, c, 0:16 * 19].rearrange("p (s w) -> p s w", w=19)
                b = st_tr(o_pool[0:128, 0, c * 128:(c + 1) * 128], 128, ['o_pool:0'])
                cp('dve', flat[:, 0:8, 0:15], psb[b][:, 0:120].rearrange("p (s r) -> p s r", r=15), ['ps:%d' % b], ['upool:%d' % c])
                cp('dve', flat[:, 8, 0:8], psb[b][:, 120:128], ['ps:%d' % b], ['upool:%d' % c])
                yield
                b = st_tr(o_pool[0:112, 1, c * 128:(c + 1) * 128], 112, ['o_pool:1'])
                cp('dve', flat[:, 8, 8:15], psb[b][:, 0:7], ['ps:%d' % b], ['upool:%d' % c])
                cp('dve', flat[:, 9:16, 0:15], psb[b][:, 7:112].rearrange("p (s r) -> p s r", r=15), ['ps:%d' % b], ['upool:%d' % c])
                yield
            for c in range(4):
                b = st_tr(o_xa[0:48, 0, c * 128:(c + 1) * 128], 48, ['o_xa:0'])
                cp('dve', xrnn[:, c, 0:16 * 7].rearrange("p (s w) -> p s w", w=7)[:, :, 0:3],
                   psb[b][:, 0:48].rearrange("p (s r) -> p s r", r=3), ['ps:%d' % b], ['xrnn:%d' % c])
                yield
            for c in range(4):
                b = st_tr(o_xa[0:16, 1, c * 128:(c + 1) * 128], 16, ['o_xa:1'])
                cp('dve', h0T[:, c, :], psb[b][:, 0:16], ['ps:%d' % b], ['h0T'])
                yield
            for c in range(24):
                if c < 16:
                    src, sk_ = stf_a[:, c * 128:(c + 1) * 128], ['gg:%d' % (c // 4)]
                else:
                    src, sk_ = stf_b[:, (c - 16) * 128:(c - 15) * 128], ['rbc:%d' % ((c - 16) // 4)]
                b = st_tr(src, 32, sk_)
                act(gcar_s[:, c, :, :], psb[b][:, 0:32].rearrange("p (s r) -> p s r", r=2), AF.Copy, ['ps:%d' % b], ['gcs:%d' % c])
                yield

        for sti in range(4):
            run_pass(sti, 'p', sti)
        run_pass(4, 's', 0)
        P.emit(st)
    return nc


_NC_CACHE = {}


def kernel(**inp):
    f = lambda a: np.ascontiguousarray(np.asarray(a, dtype=np.float32))
    shared = {
        'g_mix_norm': f(inp['g_mix_norm'][0]).reshape(8, 128),
        'w_in': f(inp['w_in'][0]),
        'w_pool': f(inp['w_pool'][0]),
        'pool_scale': f(inp['pool_scale'][0]).reshape(2, 128),
        'rnn_conv_w': f(inp['rnn_conv_w'][0]).reshape(16, 128),
        'rnn_conv_b': f(inp['rnn_conv_b'][0]).reshape(4, 128),
        'w_rg_a': f(inp['w_rg_a'][0]),
        'b_rg_a': f(inp['b_rg_a'][0]).reshape(4, 128),
        'w_rg_x': f(inp['w_rg_x'][0]),
        'b_rg_x': f(inp['b_rg_x'][0]).reshape(4, 128),
        'rg_lambda': f(inp['rg_lambda'][0]).reshape(4, 128),
        'g_mem_norm': f(inp['g_mem_norm'][0]).reshape(8, 128),
        'w_mem_k': f(inp['w_mem_k'][0]),
        'w_mem_v': f(inp['w_mem_v'][0]),
        'g_mix_out': f(inp['g_mix_out'][0]).reshape(8, 128),
        'w_out': f(inp['w_out'][0]),
        'g_ffn_norm': f(inp['g_ffn_norm'][0]).reshape(8, 128),
        'w_ff_gate': f(inp['w_ff_gate'][0]),
        'w_ff_up': f(inp['w_ff_up'][0]),
        'ffn_conv_w': f(inp['ffn_conv_w'][0]).reshape(72, 128),
        'ffn_conv_b': f(inp['ffn_conv_b'][0]).reshape(24, 128),
        'w_ff_down': f(inp['w_ff_down'][0]),
        'g_final': f(inp['g_final']).reshape(1, 1024),
    }
    in_maps = []
    for c in range(NCORES):
        b = slice(16 * c, 16 * c + 16)
        m = dict(shared)
        m['xp'] = f(inp['x_prompt'][c])
        m['xs'] = f(inp['x_sample'][b]).reshape(64, 1024)
        m['mem'] = f(inp['mem_prompt'][c])
        m['st_pool'] = f(inp['state_pool'][0, b]).reshape(240, 256)
        m['st_conv'] = f(inp['state_rnn_conv'][0, b]).reshape(48, 512)
        m['st_h'] = f(inp['state_rnn_h'][0, b])
        m['st_ffn'] = f(inp['state_ffn_conv'][0, b]).reshape(32, 3072)
        m['ck'] = f(inp['cache_mem_k'][0, b]).reshape(16, 256, 256)
        m['cv'] = f(inp['cache_mem_v'][0, b]).reshape(16, 256, 256)
        in_maps.append(m)
    if 'nc' not in _NC_CACHE:
        _NC_CACHE['nc'] = build()
    nc = _NC_CACHE['nc']
    res = run_bass_kernel_spmd(nc, in_maps, core_ids=list(range(NCORES)))
    R = res.results

    def cat(name, shape):
        return np.stack([np.asarray(R[c][name], dtype=np.float32).reshape(shape) for c in range(NCORES)], axis=0)

    y_p = cat('y_p', (2048, 1024))
    y_s = cat('y_s', (16, 4, 1024)).reshape(128, 4, 1024)
    pool_p = cat('pool_p', (15, 256))[None]
    conv_p = cat('conv_p', (3, 512))[None]
    h_p = cat('h_p', (512,))[None]
    ffn_p = cat('ffn_p', (2, 3072))[None]
    mk_p = cat('mk_p', (256, 4, 64))[None]
    mv_p = cat('mv_p', (256, 4, 64))[None]
    pool_s = cat('pool_s', (16, 15, 256)).reshape(1, 128, 15, 256)
    conv_s = cat('conv_s', (16, 3, 512)).reshape(1, 128, 3, 512)
    h_s = cat('h_s', (16, 512)).reshape(1, 128, 512)
    ffn_s = cat('ffn_s', (16, 2, 3072)).reshape(1, 128, 2, 3072)
    return (y_p, y_s, pool_p, conv_p, h_p, ffn_p, mk_p, mv_p, pool_s, conv_s, h_s, ffn_s)
```

```python
import numpy as np
from contextlib import ExitStack
import concourse.bass as bass
import concourse.mybir as mybir
from concourse.bass_utils import run_bass_kernel_spmd

F32 = mybir.dt.float32
BF16 = mybir.dt.bfloat16
ALU = mybir.AluOpType
AF = mybir.ActivationFunctionType

ENGS = ['pe', 'act', 'dve', 'pool', 'sp']
N_DMA_SEMS = 32
EPS = 1e-6
NCORES = 8
KSTOP = 10 ** 9


class Prog:
    def __init__(self, nc):
        self.nc = nc
        self.ops = {e: [] for e in ENGS}
        self.key_w = {}
        self.key_r = {}
        self.waited = {e: {} for e in ENGS}
        self.dma_val = [0] * N_DMA_SEMS
        self.dma_rr = {'sp': 0, 'pool': 0, 'act': 0}
        self.signal = {e: set() for e in ENGS}

    def _deps(self, eng, reads, writes, is_dma):
        deps = []
        for k in reads:
            t = self.key_w.get(k)
            if t is not None:
                deps.append((t, 'raw'))
        for k in writes:
            t = self.key_w.get(k)
            if t is not None:
                deps.append((t, 'waw'))
            for t in self.key_r.get(k, {}).values():
                deps.append((t, 'war'))
        out = []
        w = self.waited[eng]
        for t, kind in deps:
            if t[0] == 'e':
                _, e2, idx = t
                if e2 == eng and not is_dma:
                    if eng == 'pe':
                        continue
                if w.get(('e', e2), -1) >= idx:
                    continue
                w[('e', e2)] = idx
                self.signal[e2].add(idx)
                out.append(t)
            else:
                _, s, val = t
                if w.get(('d', s), 0) >= val:
                    continue
                w[('d', s)] = val
                out.append(t)
        return out

    def _register(self, tok, reads, writes):
        for k in writes:
            self.key_w[k] = tok
            self.key_r[k] = {}
        for k in reads:
            d = self.key_r.setdefault(k, {})
            d[(tok[0], tok[1])] = tok

    def op(self, eng, fn, reads=(), writes=()):
        self.nrec = getattr(self, 'nrec', 0) + 1
        if self.nrec > KSTOP:
            return
        deps = self._deps(eng, reads, writes, False)
        idx = len(self.ops[eng])
        self.ops[eng].append(dict(fn=fn, deps=deps, dma=None))
        self._register(('e', eng, idx), reads, writes)

    def dma(self, eng, out, in_, reads=(), writes=(), after=()):
        self.nrec = getattr(self, 'nrec', 0) + 1
        if self.nrec > KSTOP:
            return
        half = N_DMA_SEMS // 2
        s = self.dma_rr[eng] + (half if eng == 'pool' else 0)
        self.dma_rr[eng] = (self.dma_rr[eng] + 1) % half
        deps = self._deps(eng, list(reads) + list(after), writes, True)
        prev = self.dma_val[s]
        w = self.waited[eng]
        if prev > 0 and w.get(('d', s), 0) < prev:
            w[('d', s)] = prev
            deps.append(('d', s, prev))
        val = prev + 16
        self.dma_val[s] = val
        self.ops[eng].append(dict(fn=lambda e: e.dma_start(out=out, in_=in_), deps=deps, dma=(s, val)))
        self._register(('d', s, val), reads, writes)

    def emit(self, stack):
        nc = self.nc
        esem = {e: stack.enter_context(nc.semaphore("s_" + e)) for e in ENGS}
        dsem = [stack.enter_context(nc.semaphore("d_%d" % i)) for i in range(N_DMA_SEMS)]
        rank = {}
        for e in ENGS:
            for r, idx in enumerate(sorted(self.signal[e])):
                rank[(e, idx)] = r + 1
        fin = [(s, self.dma_val[s]) for s in range(N_DMA_SEMS) if self.dma_val[s] > 0]

        def replay(e, engobj):
            for i, o in enumerate(self.ops[e]):
                for t in o['deps']:
                    if t[0] == 'e':
                        engobj.wait_ge(esem[t[1]], rank[(t[1], t[2])])
                    else:
                        engobj.wait_ge(dsem[t[1]], t[2])
                ins = o['fn'](engobj)
                if o['dma'] is not None:
                    ins.then_inc(dsem[o['dma'][0]], 16)
                elif i in self.signal[e]:
                    ins.then_inc(esem[e], 1)
            if e == 'sp':
                for s, v in fin:
                    engobj.wait_ge(dsem[s], v)

        with nc.Block() as block:
            @block.tensor
            def _(pe):
                replay('pe', pe)

            @block.scalar
            def _(act):
                replay('act', act)

            @block.vector
            def _(dve):
                replay('dve', dve)

            @block.gpsimd
            def _(pool):
                replay('pool', pool)

            @block.sync
            def _(sp):
                replay('sp', sp)


PV = {}
_r = 0
for _n, _rows in [('g_mix_norm', 8), ('pool_scale', 2), ('rnn_conv_w', 16), ('rnn_conv_b', 4), ('b_rg_a', 4),
                  ('b_rg_x', 4), ('rg_lambda', 4), ('g_mem_norm', 8), ('g_mix_out', 8), ('g_ffn_norm', 8),
                  ('ffn_conv_b', 24)]:
    PV[_n] = _r
    _r += _rows
PV_A_ROWS = _r
PV['ffn_conv_w'] = 128
PV_COLS = 128 + 72
PV_HBA, PV_HBX, PV_HC, PV_C2 = 200, 204, 208, 212
PV_TOT = 216


def build():
    nc = bass.Bass("TRN2", target_bir_lowering=False)

    def din(name, shape):
        return nc.dram_tensor(name, list(shape), F32, kind="ExternalInput").ap()

    def dout(name, shape):
        return nc.dram_tensor(name, list(shape), F32, kind="ExternalOutput").ap()

    def dint(name, shape):
        return nc.dram_tensor(name, list(shape), BF16, kind="Internal").ap()

    I = {}
    for n, s in [('xp', (2048, 1024)), ('xs', (64, 1024)), ('mem', (256, 1024)), ('st_pool', (240, 256)),
                 ('st_conv', (48, 512)), ('st_h', (16, 512)), ('st_ffn', (32, 3072)), ('ck', (16, 256, 256)),
                 ('cv', (16, 256, 256)),
                 ('g_mix_norm', (8, 128)), ('w_in', (1024, 1536)), ('w_pool', (4, 64, 64)), ('pool_scale', (2, 128)),
                 ('rnn_conv_w', (16, 128)), ('rnn_conv_b', (4, 128)), ('w_rg_a', (8, 64, 64)), ('b_rg_a', (4, 128)),
                 ('w_rg_x', (8, 64, 64)), ('b_rg_x', (4, 128)), ('rg_lambda', (4, 128)), ('g_mem_norm', (8, 128)),
                 ('w_mem_k', (1024, 256)), ('w_mem_v', (1024, 256)), ('g_mix_out', (8, 128)), ('w_out', (1024, 1024)),
                 ('g_ffn_norm', (8, 128)), ('w_ff_gate', (1024, 3072)), ('w_ff_up', (1024, 3072)),
                 ('ffn_conv_w', (72, 128)), ('ffn_conv_b', (24, 128)), ('w_ff_down', (3072, 1024)),
                 ('g_final', (1, 1024))]:
        I[n] = din(n, s)
    O = {}
    for n, s in [('y_p', (2048, 1024)), ('y_s', (64, 1024)), ('pool_p', (15, 256)), ('conv_p', (3, 512)),
                 ('h_p', (1, 512)), ('ffn_p', (2, 3072)), ('mk_p', (256, 256)), ('mv_p', (256, 256)),
                 ('pool_s', (16, 15, 256)), ('conv_s', (16, 3, 512)), ('h_s', (16, 512)), ('ffn_s', (16, 2, 3072))]:
        O[n] = dout(n, s)
    WB = {n: dint(n, s) for n, s in [('w_in_b', (1024, 1536)), ('w_out_b', (1024, 1024)), ('w_gate_b', (1024, 3072)),
                                     ('w_up_b', (1024, 3072)), ('w_down_b', (3072, 1024)), ('w_mk_b', (1024, 256)),
                                     ('w_mv_b', (1024, 256))]}

    with ExitStack() as st:
        def sb(name, shape, dt=F32):
            return st.enter_context(nc.sbuf_tensor(name, list(shape), dt))

        def pst(name, shape, dt=F32):
            return st.enter_context(nc.psum_tensor(name, list(shape), dt))

        P = Prog(nc)
        xtok = sb("xtok", [128, 4, 1024])
        xnb = [sb("xnb%d" % i, [128, 1024], BF16) for i in range(2)]
        xnT = sb("xnT", [128, 8, 512], BF16)
        upool = sb("upool", [128, 2, 527])
        xrnn = sb("xrnn", [128, 4, 515])
        gg = sb("gg", [128, 4, 512])
        qb = sb("qb", [128, 2, 512], BF16)
        S = [dict(xc=sb("S%d_xc" % i, [128, 512]), xcb=sb("S%d_xcb" % i, [128, 512], BF16),
                  thr=sb("S%d_thr" % i, [128, 512]), thi=sb("S%d_thi" % i, [128, 512]),
                  a2=sb("S%d_a2" % i, [128, 512])) for i in range(2)]
        o_pool = sb("o_pool", [128, 2, 512])
        o_xa = sb("o_xa", [128, 2, 512])
        sq = [sb("sq%d" % i, [128, 512], BF16) for i in range(2)]
        rbc = sb("rbc", [128, 3, 512])
        mixT = sb("mixT", [128, 8, 512], BF16)
        sA = sb("sA", [128, 2, 527]); sB = sb("sB", [128, 2, 527]); sC = sb("sC", [128, 527])
        diff = sb("diff", [128, 2, 512], BF16)
        KT = sb("KT", [128, 2, 256], BF16)
        Vz = sb("Vz", [128, 4, 2, 128], BF16)
        PT = [sb("PT%d" % i, [128, 2, 512], BF16) for i in range(2)]
        rd = sb("rd", [128, 512])
        kvf = [sb("kvf%d" % i, [128, 2, 256]) for i in range(2)]
        KTs = [sb("KTs%d" % i, [128, 2, 256], BF16) for i in range(2)]
        Vzs = [sb("Vzs%d" % i, [128, 4, 2, 128], BF16) for i in range(2)]
        gbuf = [sb("gbuf%d" % i, [128, 514]) for i in range(2)]
        ftmp = [sb("ftmp%d" % i, [128, 512]) for i in range(2)]
        hT = [sb("hT%d" % i, [128, 4, 512], BF16) for i in range(2)]
        gcar = sb("gcar", [128, 24, 2])
        gcar_s = sb("gcar_s", [128, 24, 16, 2])
        gfin = sb("gfin", [128, 1024])
        pv = sb("pv", [128, PV_TOT])
        ident = sb("ident", [128, 128], BF16); identf = sb("identf", [128, 128])
        ones_b = sb("ones_b", [128, 128], BF16)
        onesz = sb("onesz", [128, 2, 128], BF16)
        bdw = sb("bdw", [128, 10, 128], BF16)
        rc15 = sb("rc15", [128, 2, 15])
        hcar = sb("hcar", [128, 4])
        h0T = sb("h0T", [128, 4, 16])
        hls = sb("hls", [128, 4, 16])
        stg = sb("stg", [128, 128]); stg2 = [sb("stg2_%d" % i, [128, 128]) for i in range(2)]
        ss = sb("ss", [128, 8]); rstd = sb("rstd", [128, 8])
        t16 = sb("t16", [128, 16])
        gu = [sb("gu%d" % i, [128, 8192], BF16) for i in range(2)]
        dn = [sb("dn%d" % i, [128, 4096], BF16) for i in range(2)]
        psb = [pst("ps%d" % i, [128, 512]) for i in range(6)]
        ptr = pst("ptr", [128, 1024], BF16)
        ptr2 = pst("ptr2", [128, 1024], BF16)
        ptrs = [ptr, ptr2]
        bank_rr = [0]
        held = set()

        def bank(exclude=()):
            b = bank_rr[0]
            while b in exclude or b in held:
                b = (b + 1) % 6
            bank_rr[0] = (b + 1) % 6
            return b

        def act(out, in_, func, reads, writes, scale=None, bias=None, accum=None):
            kw = {}
            if scale is not None:
                kw['scale'] = scale
            if bias is not None:
                kw['bias'] = bias
            if accum is not None:
                kw['accum_out'] = accum
            P.op('act', lambda e: e.activation(out=out, in_=in_, func=func, **kw), reads, writes)

        def tsc(eng, out, in0, s1, s2, op0, op1, reads, writes):
            if s2 is None:
                P.op(eng, lambda e: e.tensor_scalar(out=out, in0=in0, scalar1=s1, scalar2=None, op0=op0), reads, writes)
            else:
                P.op(eng, lambda e: e.tensor_scalar(out=out, in0=in0, scalar1=s1, scalar2=s2, op0=op0, op1=op1), reads, writes)

        def stt(eng, out, in0, sc, in1, op0, op1, reads, writes):
            P.op(eng, lambda e: e.scalar_tensor_tensor(out=out, in0=in0, scalar=sc, in1=in1, op0=op0, op1=op1), reads, writes)

        def tt(eng, out, in0, in1, op, reads, writes):
            P.op(eng, lambda e: e.tensor_tensor(out=out, in0=in0, in1=in1, op=op), reads, writes)

        def cp(eng, out, in_, reads, writes):
            P.op(eng, lambda e: e.tensor_copy(out=out, in_=in_), reads, writes)

        def mset(eng, ap, v, writes):
            P.op(eng, lambda e: e.memset(ap, v), (), writes)

        def mm(out, lhsT, rhs, start, stop, reads, writes):
            P.op('pe', lambda e: e.matmul(out, lhsT=lhsT, rhs=rhs, start=start, stop=stop), reads, writes)

        def tr(out, in_, idn, reads, writes):
            P.op('pe', lambda e: e.transpose(out=out, in_=in_, identity=idn), reads, writes)

        def pvc(name, j=0):
            c = PV[name] + j
            return pv[:, c:c + 1]

        def rsqrt_inplace(ap, key):
            act(ap, ap, AF.Ln, [key], [key])
            act(ap, ap, AF.Exp, [key], [key], scale=-0.5)

        stg2_rr = [0]

        def rows_out(src_ap, ncols, src_keys, dmas):
            i = stg2_rr[0]
            stg2_rr[0] = 1 - i
            rb = bank()
            tr(psb[rb][0:ncols, 0:128], src_ap, identf[:], list(src_keys) + ['identf'], ['ps:%d' % rb])
            cp('dve', stg2[i][0:ncols, :], psb[rb][0:ncols, 0:128], ['ps:%d' % rb], ['stg2_%d' % i])
            for (dap, p0, step, n) in dmas:
                if step == 1:
                    sap = stg2[i][p0:p0 + n, :]
                else:
                    sap = stg2[i][p0:p0 + (n - 1) * step + 1:step, :]
                P.dma('sp', dap, sap, ['stg2_%d' % i], ())

        def cast(dst, src, key):
            P.dma('pool', dst, src, (), [key])

        NPASS = 5
        gu_seq = []
        d_seq = []
        for p in range(NPASS):
            gu_seq += [(p, 'in0'), (p, 'in1'), (p, 'out')] + [(p, 'gu%d' % sg) for sg in range(6)]
            d_seq += [(p, sg) for sg in range(6)]
        gu_next = [0]
        d_next = [0]
        gu_slot_of = {}
        d_slot_of = {}

        defer_store = [True]
        pending_store = []
        NDIRECT = 1

        def ring_load(p, dst, src32, src16, wbkey, wkeys, cast_pass):
            if p == 0 or (p == 1 and cast_pass == 1):
                P.dma('pool', dst, src32.rearrange("(k p) n -> p k n", p=128), (), wkeys)
                if p != cast_pass:
                    return
                st_fn = (lambda a=src16.rearrange("(k p) n -> p k n", p=128), b=dst, c=list(wkeys), d=wbkey:
                         P.dma('sp', a, b, c, [d]))
                if defer_store[0]:
                    pending_store.append(st_fn)
                else:
                    st_fn()
            else:
                P.dma('sp', dst, src16.rearrange("(k p) n -> p k n", p=128), [wbkey], wkeys)

        def gu_prefetch():
            i = gu_next[0]
            if i >= len(gu_seq):
                return
            gu_next[0] += 1
            s = i % 2
            item = gu_seq[i]
            gu_slot_of[item] = s
            p, k = item
            key = 'gu:%d' % s
            keyb = 'gub:%d' % s
            if k == 'in0' or k == 'in1':
                h = 0 if k == 'in0' else 1
                ring_load(p, gu[s][:, 0:6144].rearrange("p (k n) -> p k n", n=768), I['w_in'][:, h * 768:(h + 1) * 768],
                          WB['w_in_b'][:, h * 768:(h + 1) * 768], 'wb:' + k, [key, keyb], h)
            elif k == 'out':
                ring_load(p, gu[s][:, 0:8192].rearrange("p (k n) -> p k n", n=1024), I['w_out'], WB['w_out_b'], 'wb:out',
                          [key, keyb], 0)
            else:
                sg = int(k[2:])
                cs = slice(sg * 512, (sg + 1) * 512)
                ring_load(p, gu[s][:, 0:4096].rearrange("p (k n) -> p k n", n=512), I['w_ff_gate'][:, cs], WB['w_gate_b'][:, cs],
                          'wb:g%d' % sg, [key], sg % 2)
                ring_load(p, gu[s][:, 4096:8192].rearrange("p (k n) -> p k n", n=512), I['w_ff_up'][:, cs], WB['w_up_b'][:, cs],
                          'wb:u%d' % sg, [keyb], sg % 2)

        def d_prefetch():
            i = d_next[0]
            if i >= len(d_seq):
                return
            d_next[0] += 1
            s = i % 2
            item = d_seq[i]
            d_slot_of[item] = s
            p, sg = item
            rs = slice(sg * 512, (sg + 1) * 512)
            ring_load(p, dn[s][:, :].rearrange("p (k n) -> p k n", n=1024), I['w_ff_down'][rs, :], WB['w_down_b'][rs, :],
                      'wb:d%d' % sg, ['dn:%d' % s], sg % 2)

        gu_prefetch(); gu_prefetch()

        mset('pool', identf[:], 1.0, ['identf'])
        P.op('pool', lambda e: e.affine_select(out=identf[:], in_=identf[:], pattern=[[-1, 128]],
                                               compare_op=ALU.is_equal, fill=0.0, base=0, channel_multiplier=1),
             ['identf'], ['identf'])
        cp('pool', ident[:], identf[:], ['identf'], ['ident'])
        mset('pool', ones_b[:], 1.0, ['ones_b'])
        mset('pool', onesz[:], 0.0, ['onesz'])
        mset('pool', onesz[:, 0, 0:64], 1.0, ['onesz'])
        mset('pool', onesz[:, 1, 64:128], 1.0, ['onesz'])
        mset('pool', Vz[:], 0.0, ['Vz'])
        mset('pool', gcar[:], 0.0, ['gcar:%d' % c for c in range(24)])
        mset('pool', hcar[:], 0.0, ['hcar'])
        for c in range(2):
            for hh in range(2):
                w = (2, 4, 8, 16)[2 * c + hh]
                pr = slice(hh * 64, hh * 64 + 64)
                mset('pool', rc15[pr, c, :], 1.0 / w, ['rc15'])
                for t in range(w - 1):
                    mset('pool', rc15[pr, c, t:t + 1], 1.0 / (t + 1), ['rc15'])
        for j_ in range(4):
            st_ap, st_keys = [(gg[:, 0:2, :].rearrange("p a b -> p (a b)"), ['gg:0', 'gg:1']),
                              (gg[:, 2:4, :].rearrange("p a b -> p (a b)"), ['gg:2', 'gg:3']),
                              (o_pool[:, :, :].rearrange("p a b -> p (a b)"), ['o_pool:0', 'o_pool:1']),
                              (o_xa[:, :, :].rearrange("p a b -> p (a b)"), ['o_xa:0', 'o_xa:1'])][j_]
            P.dma('sp', st_ap, I['xp'][j_ * 128:(j_ + 1) * 128, :], (), st_keys)
        P.dma('sp', gfin[:], I['g_final'][0].partition_broadcast(128), (), ['gfin'])
        stA = xtok[:, 1, 512:640]
        stB = xtok[:, 1, 640:768]
        mset('dve', xtok[:, 1, 512:768], 0.0, ['xtok:1'])
        pst_keys = []
        for n in ['g_mix_norm', 'pool_scale', 'rnn_conv_w', 'rnn_conv_b', 'b_rg_a', 'b_rg_x', 'rg_lambda',
                  'g_mem_norm', 'g_mix_out', 'g_ffn_norm', 'ffn_conv_b']:
            r0 = PV[n]
            nr = I[n].shape[0]
            P.dma('sp', xtok[r0:r0 + nr, 1, 512:640], I[n], (), ['pst:' + n], after=['xtok:1'])
            pst_keys.append('pst:' + n)
        P.dma('sp', xtok[0:72, 1, 640:768], I['ffn_conv_w'], (), ['pst:fcw'], after=['xtok:1'])
        pst_keys.append('pst:fcw')
        tr(psb[0][:, 0:128], stA, identf[:], ['xtok:1', 'identf'] + pst_keys, ['ps:0'])
        tr(psb[0][:, 128:256], stB, identf[:], ['xtok:1', 'identf'] + pst_keys, ['ps:0'])
        cp('dve', pv[:, 0:256 - 56], psb[0][:, 0:200], ['ps:0'], ['pv'])
        tsc('dve', pv[:, PV_HBA:PV_HBA + 4], pv[:, PV['b_rg_a']:PV['b_rg_a'] + 4], 0.5, None, ALU.mult, None, ['pv'], ['pv'])
        tsc('dve', pv[:, PV_HBX:PV_HBX + 4], pv[:, PV['b_rg_x']:PV['b_rg_x'] + 4], 0.5, None, ALU.mult, None, ['pv'], ['pv'])
        act(pv[:, PV_HC:PV_HC + 4], pv[:, PV['rg_lambda']:PV['rg_lambda'] + 4], AF.Exp, ['pv'], ['pv'], scale=-1.0)
        act(pv[:, PV_HC:PV_HC + 4], pv[:, PV_HC:PV_HC + 4], AF.Ln, ['pv'], ['pv'], bias=1.0)
        tsc('dve', pv[:, PV_C2:PV_C2 + 4], pv[:, PV_HC:PV_HC + 4], -8.0, None, ALU.mult, None, ['pv'], ['pv'])
        tsc('dve', pv[:, PV_HC:PV_HC + 4], pv[:, PV_HC:PV_HC + 4], -4.0, None, ALU.mult, None, ['pv'], ['pv'])
        bst0 = xtok[:, 0, :]
        bst1 = xtok[:, 1, 0:256]
        mset('dve', xtok[:, 0, :], 0.0, ['xtok:0'])
        mset('dve', xtok[:, 1, 0:256], 0.0, ['xtok:1'])

        bds_keys = []

        def bd_dst(bi, hh):
            base = bst0 if bi < 8 else bst1
            b = bi if bi < 8 else bi - 8
            return base[hh * 64:hh * 64 + 64, b * 128 + hh * 64: b * 128 + hh * 64 + 64]

        for c in range(2):
            for hh in range(2):
                P.dma('sp', bd_dst(c, hh), I['w_pool'][2 * c + hh], (), ['bds:p%d%d' % (c, hh)], after=['xtok:0', 'xtok:1'])
                bds_keys.append('bds:p%d%d' % (c, hh))
        for c in range(4):
            for hh in range(2):
                P.dma('sp', bd_dst(2 + c, hh), I['w_rg_a'][2 * c + hh], (), ['bds:a%d%d' % (c, hh)], after=['xtok:0', 'xtok:1'])
                P.dma('sp', bd_dst(6 + c, hh), I['w_rg_x'][2 * c + hh], (), ['bds:x%d%d' % (c, hh)], after=['xtok:0', 'xtok:1'])
                bds_keys += ['bds:a%d%d' % (c, hh), 'bds:x%d%d' % (c, hh)]
        cp('dve', bdw[:, 0:8, :], xtok[:, 0, :].rearrange("p (b n) -> p b n", n=128), ['xtok:0'] + bds_keys, ['bdw'])
        cp('dve', bdw[:, 8:10, :], xtok[:, 1, 0:256].rearrange("p (b n) -> p b n", n=128), ['xtok:1'] + bds_keys, ['bdw'])


        def norm_units(ntt, tp, gname, srcs=None, dst=None, dkey='xnT'):
            if dst is None:
                dst = xnT
            if srcs is None:
                srcs = [(xtok[:, j, :], ['xtok:%d' % j]) for j in range(ntt)]
            for j in range(ntt):
                act(xnb[j % 2][:tp, :], srcs[j][0][:tp, :], AF.Square, srcs[j][1], ['xnb:%d' % (j % 2), 'ss'], accum=ss[:tp, j:j + 1])
            tsc('dve', rstd[:tp, 0:ntt], ss[:tp, 0:ntt], 1.0 / 1024, EPS, ALU.mult, ALU.add, ['ss'], ['rstd'])
            rsqrt_inplace(rstd[:tp, 0:ntt], 'rstd')
            yield

            def scale_(j):
                tsc('dve', xnb[j % 2][:tp, :], srcs[j][0][:tp, :], rstd[:tp, j:j + 1], None, ALU.mult, None,
                    list(srcs[j][1]) + ['rstd'], ['xnb:%d' % (j % 2)])

            def tr_(j):
                for k in range(8):
                    tr(ptrs[j % 2][:, k * 128:k * 128 + tp], xnb[j % 2][:tp, k * 128:(k + 1) * 128], ident[:tp, :tp],
                       ['xnb:%d' % (j % 2), 'ident'], ['ptr%d' % (j % 2)])

            def evac_(j):
                for k in range(8):
                    o = dst[:, k, j * 128:(j + 1) * 128]
                    i_ = ptrs[j % 2][:, k * 128:(k + 1) * 128]
                    if j % 2 == 0:
                        act(o, i_, AF.Identity, ['ptr0', 'pv'], ['%s:%d' % (dkey, k)], scale=pvc(gname, k))
                    else:
                        tsc('dve', o, i_, pvc(gname, k), None, ALU.mult, None, ['ptr1', 'pv'], ['%s:%d' % (dkey, k)])

            for step in range(ntt + 2):
                if step - 2 >= 0:
                    evac_(step - 2)
                if 0 <= step - 1 < ntt:
                    tr_(step - 1)
                if step < ntt:
                    scale_(step)
                yield

        def norm_and_transpose(ntt, tp, gname, srcs=None):
            for _ in norm_units(ntt, tp, gname, srcs):
                pass

        norm1_done = set()

        wmk = hT[0][:, :, :].rearrange("p a b -> p (a b)")[:, 0:2048].rearrange("p (k n) -> p k n", n=256)
        wmv = hT[1][:, :, :].rearrange("p a b -> p (a b)")[:, 0:2048].rearrange("p (k n) -> p k n", n=256)
        P.dma('pool', wmk, I['w_mem_k'].rearrange("(k p) n -> p k n", p=128), (), ['hT:0'])
        P.dma('pool', wmv, I['w_mem_v'].rearrange("(k p) n -> p k n", p=128), (), ['hT:1'])
        for j in range(2):
            P.dma('sp', xtok[:, 2 + j, :], I['mem'][j * 128:(j + 1) * 128, :], (), ['xtok:%d' % (2 + j)])
        memT = xnT
        for j in range(2):
            jj = 2 + j
            xb = xnb[j]
            xk = 'xnb:%d' % j
            act(xb[:, :], xtok[:, jj, :], AF.Square, ['xtok:%d' % jj], [xk, 'ss'], accum=ss[:, j:j + 1])
            tsc('dve', rstd[:, j:j + 1], ss[:, j:j + 1], 1.0 / 1024, EPS, ALU.mult, ALU.add, ['ss'], ['rstd'])
            rsqrt_inplace(rstd[:, j:j + 1], 'rstd')
            tsc('dve', xb[:, :], xtok[:, jj, :], rstd[:, j:j + 1], None, ALU.mult, None, ['xtok:%d' % jj, 'rstd'], [xk])
            for k in range(8):
                tr(ptr[:, k * 128:(k + 1) * 128], xb[:, k * 128:(k + 1) * 128], ident[:], [xk, 'ident'], ['ptr0'])
            for k in range(8):
                tsc('dve', memT[:, k, j * 128:(j + 1) * 128], ptr[:, k * 128:(k + 1) * 128], pvc('g_mem_norm', k), None,
                    ALU.mult, None, ['ptr0', 'pv'], ['xnT:%d' % k])
        def ffn_cast():
            pass

        for ch in range(2):
            b = bank()
            for k in range(8):
                mm(psb[b][:, 0:256], wmk[:, k, ch * 128:(ch + 1) * 128], memT[:, k, 0:256], k == 0, k == 7,
                   ['hT:0', 'xnT:%d' % k], ['ps:%d' % b])
            cp('dve', KT[:, ch, :], psb[b][:, 0:256], ['ps:%d' % b], ['KT'])
        for (wsrc, wkey, oname) in [(wmk, 'hT:0', 'mk_p'), (wmv, 'hT:1', 'mv_p')]:
            for mt in range(2):
                b = bank()
                for k in range(8):
                    mm(psb[b][:, 0:256], memT[:, k, mt * 128:(mt + 1) * 128], wsrc[:, k, :], k == 0, k == 7,
                       [wkey, 'xnT:%d' % k], ['ps:%d' % b])
                skey = 'S%d_xc' % mt
                cp('dve', S[mt]['xc'][:, 0:256], psb[b][:, 0:256], ['ps:%d' % b], [skey])
                P.dma('sp', O[oname][mt * 128:(mt + 1) * 128, :], S[mt]['xc'][:, 0:256], [skey], ())
                if oname == 'mv_p':
                    for h in range(4):
                        cp('pool', Vz[:, h, mt, (h % 2) * 64:(h % 2) * 64 + 64], S[mt]['xc'][:, h * 64:(h + 1) * 64],
                           [skey], ['Vz'])

        xstage = [(gg[:, 0:2, :].rearrange("p a b -> p (a b)"), ['gg:0', 'gg:1']),
                  (gg[:, 2:4, :].rearrange("p a b -> p (a b)"), ['gg:2', 'gg:3']),
                  (o_pool[:, :, :].rearrange("p a b -> p (a b)"), ['o_pool:0', 'o_pool:1']),
                  (o_xa[:, :, :].rearrange("p a b -> p (a b)"), ['o_xa:0', 'o_xa:1'])]

        def run_pass(pidx, kind, sti):
            if kind == 's':
                nseq, T, N, ntt, tp = 16, 4, 64, 1, 64
                xsrc = [I['xs']]
                ydst = [O['y_s']]
            else:
                nseq, T, N, ntt, tp = 1, 512, 512, 4, 128
                xsrc = [I['xp'][sti * 512 + j * 128: sti * 512 + (j + 1) * 128, :] for j in range(4)]
                ydst = [O['y_p'][sti * 512 + j * 128: sti * 512 + (j + 1) * 128, :] for j in range(4)]
            last = (kind == 'p' and sti == 3)
            WU = 15 + T
            WX = 3 + T
            WG = 2 + T

            def fv(ap2d):
                return ap2d.rearrange("p (s t) -> p s t", t=T)

            def hv(buf, c, W, lo, hi, pr=slice(0, 128)):
                b2 = buf[pr, c, 0:nseq * W] if c is not None else buf[pr, 0:nseq * W]
                return b2.rearrange("p (s w) -> p s w", w=W)[:, :, lo:hi]

            if defer_store[0]:
                defer_store[0] = False
                for fn_ in pending_store:
                    fn_()
            if kind == 'p' and sti == 0:
                mset('pool', upool[:, :, 0:15], 0.0, ['upool:0', 'upool:1'])
                mset('pool', xrnn[:, :, 0:3], 0.0, ['xrnn:%d' % c for c in range(4)])
            for j in range(ntt):
                P.dma('sp', xtok[:tp, j, :], xsrc[j], (), ['xtok:%d' % j])
            if kind == 's':
                for n_ in range(8):
                    if n_ < 2:
                        ap_, wk_ = kvf[n_][:, :, :], ['kvf:%d' % n_]
                    else:
                        t_ = 1 + (n_ - 2) // 2
                        h_ = (n_ - 2) % 2
                        ap_ = xtok[:, t_, h_ * 512:(h_ + 1) * 512].rearrange("p (a b) -> p a b", b=256)
                        wk_ = ['kvf:%d' % n_, 'xtok:%d' % t_]
                    P.dma('sp', ap_, I['ck'][n_].rearrange("(mt p) n -> p mt n", p=128), (), wk_)
            if pidx in norm1_done:
                xin_buf, xin_key = mixT, 'mixT'
            else:
                xin_buf, xin_key = xnT, 'xnT'
                if kind == 'p':
                    norm_and_transpose(ntt, tp, 'g_mix_norm', srcs=xstage)
                else:
                    norm_and_transpose(ntt, tp, 'g_mix_norm')
            for half in range(2):
                s = gu_slot_of[(pidx, 'in%d' % half)]
                wv = gu[s][:, 0:6144].rearrange("p (k n) -> p k n", n=768)
                for ml in range(6):
                    m = half * 6 + ml
                    b = bank()
                    for k in range(8):
                        mm(psb[b][:, 0:N], wv[:, k, ml * 128:(ml + 1) * 128], xin_buf[:, k, 0:N], k == 0, k == 7,
                           ['gu:%d' % s, 'gub:%d' % s, '%s:%d' % (xin_key, k)], ['ps:%d' % b])
                    pk = ['ps:%d' % b]
                    if m < 2:
                        act(hv(upool, m, WU, 15, 15 + T), fv(psb[b][:, 0:N]), AF.Copy, pk, ['upool:%d' % m])
                    elif m < 6:
                        act(hv(xrnn, m - 2, WX, 3, 3 + T), fv(psb[b][:, 0:N]), AF.Copy, pk, ['xrnn:%d' % (m - 2)])
                    elif m < 10:
                        act(gg[:, m - 6, 0:N], psb[b][:, 0:N], AF.Gelu_apprx_tanh, pk, ['gg:%d' % (m - 6)])
                    else:
                        cp('dve', qb[:, m - 10, 0:N], psb[b][:, 0:N], pk, ['qb:%d' % (m - 10)])
                if half == 0:
                    ffn_cast()
                gu_prefetch()

            groups = {0: ([(o_pool, 0, 'o_pool:0'), (o_pool, 1, 'o_pool:1')], 256, 0),
                      1: ([(gg, c, 'gg:%d' % c) for c in range(4)], 512, 2),
                      2: ([(o_xa, 0, 'o_xa:0'), (o_xa, 1, 'o_xa:1')], 256, 6)}
            sqi_ = [0]

            def rms_group(g):
                chunks, nfe, mi = groups[g]
                b = bank()
                for i, (buf, c, key) in enumerate(chunks):
                    sqb = sq[sqi_[0] % 2]
                    sqk = 'sq:%d' % (sqi_[0] % 2)
                    sqi_[0] += 1
                    act(sqb[:, 0:N], buf[:, c, 0:N], AF.Square, [key], [sqk])
                    mm(psb[b][:, 0:N], ones_b[:, :], sqb[:, 0:N], i == 0, i == len(chunks) - 1, ['ones_b', sqk], ['ps:%d' % b])
                tsc('dve', rbc[:, g, 0:N], psb[b][:, 0:N], 1.0 / nfe, EPS, ALU.mult, ALU.add, ['ps:%d' % b], ['rbc:%d' % g])
                rsqrt_inplace(rbc[:, g, 0:N], 'rbc:%d' % g)
                for (buf, c, key) in chunks:
                    stt('dve', mixT[:, mi, 0:N], buf[:, c, 0:N], pvc('g_mix_out', mi), rbc[:, g, 0:N], ALU.mult, ALU.mult,
                        [key, 'pv', 'rbc:%d' % g], ['mixT:%d' % mi])
                    mi += 1

            if kind == 'p':
                att_pending = []
                for ch in range(2):
                    bo = bank()
                    held.add(bo)
                    bd_ = bank()
                    held.add(bd_)
                    att_pending.append((ch, bo, bd_))
                    for e_ in range(2):
                        h = 2 * ch + e_
                        pr = slice(e_ * 64, e_ * 64 + 64)
                        pt = PT[h % 2]
                        ptk = 'PT:%d' % (h % 2)
                        for mt in range(2):
                            b = bank()
                            mm(psb[b][:, 0:N], KT[pr, ch, mt * 128:(mt + 1) * 128], qb[pr, ch, 0:N], True, True,
                               ['KT', 'qb:%d' % ch], ['ps:%d' % b])
                            act(pt[:, mt, 0:N], psb[b][:, 0:N], AF.Exp, ['ps:%d' % b], [ptk], scale=0.125)
                        for mt in range(2):
                            first = (e_ == 0 and mt == 0)
                            lastm = (e_ == 1 and mt == 1)
                            mm(psb[bo][:, 0:N], Vz[:, h, mt, :], pt[:, mt, 0:N], first, lastm, ['Vz', ptk], ['ps:%d' % bo])
                            mm(psb[bd_][:, 0:N], onesz[:, e_, :], pt[:, mt, 0:N], first, lastm, ['onesz', ptk], ['ps:%d' % bd_])

                def att_partB():
                    for (ch, bo, bd_) in att_pending:
                        P.op('dve', (lambda bd_=bd_: lambda e: e.reciprocal(out=rd[:, 0:N], in_=psb[bd_][:, 0:N]))(), ['ps:%d' % bd_], ['rd'])
                        tt('dve', o_xa[:, ch, 0:N], psb[bo][:, 0:N], rd[:, 0:N], ALU.mult, ['ps:%d' % bo, 'rd'], ['o_xa:%d' % ch])
                        held.discard(bo)
                        held.discard(bd_)
                    rms_group(2)
            else:
                bs = [bank(), bank()]
                NSLOT = 8

                def kvslot(n):
                    i = n % NSLOT
                    if i < 2:
                        return kvf[i][:, :, :], 'kvf:%d' % i, None
                    t_ = 1 + (i - 2) // 2
                    h_ = (i - 2) % 2
                    return (xtok[:, t_, h_ * 512:(h_ + 1) * 512].rearrange("p (a b) -> p a b", b=256), 'kvf:%d' % i, 'xtok:%d' % t_)

                def kvload(n):
                    ap, key, xk_ = kvslot(n)
                    src = I['ck'][n] if n < 16 else I['cv'][n - 16]
                    wk = [key] + ([xk_] if (xk_ is not None and n < NSLOT) else [])
                    P.dma('sp', ap, src.rearrange("(mt p) n -> p mt n", p=128), (), wk)

                def kprep(bb):
                    ap, kk, _ = kvslot(bb)
                    b = bank(exclude=bs)
                    for ch in range(2):
                        for mt in range(2):
                            tr(psb[b][:, (ch * 2 + mt) * 128:(ch * 2 + mt + 1) * 128], ap[:, mt, ch * 128:(ch + 1) * 128],
                               identf[:], [kk, 'identf'], ['ps:%d' % b])
                    dst = KTs[bb % 2][:, :, :].rearrange("p a b -> p (a b)")
                    if bb % 2 == 0:
                        cp('dve', dst, psb[b][:, 0:512], ['ps:%d' % b], ['KTs:%d' % (bb % 2)])
                    else:
                        act(dst, psb[b][:, 0:512], AF.Copy, ['ps:%d' % b], ['KTs:%d' % (bb % 2)])
                    kvload(bb + NSLOT)

                def kscores(bb):
                    cols = slice(bb * 4, bb * 4 + 4)
                    for e_ in range(2):
                        pr = slice(e_ * 64, e_ * 64 + 64)
                        for ch in range(2):
                            for mt in range(2):
                                c0 = (mt * 2 + ch) * 64 + bb * 4
                                mm(psb[bs[e_]][:, c0:c0 + 4], KTs[bb % 2][pr, ch, mt * 128:(mt + 1) * 128], qb[pr, ch, cols], True, True,
                                   ['KTs:%d' % (bb % 2), 'qb:%d' % ch], ['ps:%d' % bs[e_]])

                kprep(0)
                for bb in range(16):
                    if bb + 1 < 16:
                        kprep(bb + 1)
                    kscores(bb)
                for e_ in range(2):
                    act(PT[1][:, e_, 0:256], psb[bs[e_]][:, 0:256], AF.Exp, ['ps:%d' % bs[e_]], ['PT:1'], scale=0.125)
                bo = [bank(), bank()]
                bdn = [bank(), bank()]

                def vb16(bb):
                    return Vzs[bb % 2][:, :, :, :].rearrange("p a b c -> p (a b c)")[:, 0:512].rearrange("p (a b) -> p a b", b=256)

                def vprep(bb):
                    ap, kk, _ = kvslot(16 + bb)
                    if bb % 2 == 0:
                        act(vb16(bb), ap, AF.Copy, [kk], ['Vzs:%d' % (bb % 2)])
                    else:
                        cp('dve', vb16(bb), ap, [kk], ['Vzs:%d' % (bb % 2)])
                    if 16 + bb + NSLOT < 32:
                        kvload(16 + bb + NSLOT)

                def pvmm(bb):
                    vk = 'Vzs:%d' % (bb % 2)
                    vb = vb16(bb)
                    for ch in range(2):
                        for e_ in range(2):
                            oc = slice(e_ * 64 + bb * 4, e_ * 64 + bb * 4 + 4)
                            for mt in range(2):
                                c0 = (mt * 2 + ch) * 64 + bb * 4
                                mm(psb[bo[ch]][:, oc], vb[:, mt, ch * 128:(ch + 1) * 128], PT[1][:, e_, c0:c0 + 4], mt == 0, mt == 1,
                                   [vk, 'PT:1'], ['ps:%d' % bo[ch]])
                                mm(psb[bdn[ch]][:, oc], ones_b[:, :], PT[1][:, e_, c0:c0 + 4], mt == 0, mt == 1,
                                   ['ones_b', 'PT:1'], ['ps:%d' % bdn[ch]])

                vprep(0)
                for bb in range(16):
                    if bb + 1 < 16:
                        vprep(bb + 1)
                    pvmm(bb)
                for ch in range(2):
                    for e_ in range(2):
                        pr = slice(e_ * 64, e_ * 64 + 64)
                        P.op('dve', (lambda ch=ch, e_=e_, pr=pr: lambda e: e.reciprocal(
                            out=rd[pr, e_ * 64:(e_ + 1) * 64], in_=psb[bdn[ch]][pr, e_ * 64:(e_ + 1) * 64]))(), ['ps:%d' % bdn[ch]], ['rd'])
                        tt('dve', o_xa[pr, ch, 0:N], psb[bo[ch]][pr, e_ * 64:(e_ + 1) * 64], rd[pr, e_ * 64:(e_ + 1) * 64], ALU.mult,
                           ['ps:%d' % bo[ch], 'rd'], ['o_xa:%d' % ch])

                def att_partB():
                    rms_group(2)

            for c in range(2):
                uk = 'upool:%d' % c
                tt('dve', hv(sA, c, WU, 1, WU), hv(upool, c, WU, 1, WU), hv(upool, c, WU, 0, WU - 1), ALU.add, [uk], ['sA:%d' % c])
            hi = slice(64, 128)
            lo = slice(0, 64)
            tt('dve', hv(sB, 0, WU, 3, WU, hi), hv(sA, 0, WU, 3, WU, hi), hv(sA, 0, WU, 1, WU - 2, hi), ALU.add, ['sA:0'], ['sB:0'])
            tt('dve', hv(sB, 1, WU, 3, WU), hv(sA, 1, WU, 3, WU), hv(sA, 1, WU, 1, WU - 2), ALU.add, ['sA:1'], ['sB:1'])
            tt('dve', hv(sC, None, WU, 7, WU), hv(sB, 1, WU, 7, WU), hv(sB, 1, WU, 3, WU - 4), ALU.add, ['sB:1'], ['sC'])
            tt('dve', hv(sA, 1, WU, 15, WU, hi), hv(sC, None, WU, 15, WU, hi), hv(sC, None, WU, 7, WU - 8, hi), ALU.add, ['sC'], ['sA:1'])
            fin_src = [(0, lo, sA, 0, 'sA:0', 2), (0, hi, sB, 0, 'sB:0', 4), (1, lo, sC, None, 'sC', 8), (1, hi, sA, 1, 'sA:1', 16)]
            for (c, pr, buf, bc, bkey, w) in fin_src:
                stt('dve', fv(diff[pr, c, 0:N]), hv(buf, bc, WU, 15, WU, pr), 1.0 / w, hv(upool, c, WU, 15, WU, pr),
                    ALU.mult, ALU.subtract, [bkey, 'upool:%d' % c], ['diff:%d' % c])
            if kind == 'p' and sti == 0:
                for (c, pr, buf, bc, bkey, w) in fin_src:
                    src = buf[pr, bc, 15:30] if bc is not None else buf[pr, 15:30]
                    tt('dve', t16[pr, 0:15], src, rc15[pr, c, :], ALU.mult, [bkey, 'rc15'], ['t16'])
                    tt('dve', diff[pr, c, 0:15], t16[pr, 0:15], upool[pr, c, 15:30], ALU.subtract, ['t16', 'upool:%d' % c], ['diff:%d' % c])
            for c in range(2):
                b = bank()
                mm(psb[b][:, 0:N], bdw[:, c, :], diff[:, c, 0:N], True, True, ['bdw', 'diff:%d' % c], ['ps:%d' % b])
                act(o_pool[:, c, 0:N], psb[b][:, 0:N], AF.Identity, ['ps:%d' % b, 'pv'], ['o_pool:%d' % c], scale=pvc('pool_scale', c))
            if kind == 's':
                for c in range(2):
                    cp('pool', stg[:, 0:64].rearrange("p (s t) -> p s t", t=4), hv(upool, c, WU, 15, 19), ['upool:%d' % c], ['stg'])
                    rows_out(stg[:, 0:64], 64, ['stg'],
                             [(O['pool_s'][:, 11 + t, c * 128:(c + 1) * 128], t, 4, 16) for t in range(4)])
            else:
                if last:
                    cp('pool', stg[:, 0:30].rearrange("p (c r) -> p c r", r=15), upool[:, :, 512:527], ['upool:0', 'upool:1'], ['stg'])
                    rows_out(stg[:, 0:30], 30, ['stg'],
                             [(O['pool_p'][:, c * 128:(c + 1) * 128], c * 15, 1, 15) for c in range(2)])
                else:
                    for c in range(2):
                        cp('pool', upool[:, c, 0:15], upool[:, c, 512:527], ['upool:%d' % c], ['upool:%d' % c])

            att_partB()
            rms_group(0)
            ffn_cast()
            RGB = {
                0: dict(xc=(S[0]['xc'][:, 0:512], 'S0_xc'), xcb=(S[0]['xcb'][:, 0:512], 'S0_xcb'), thr=(S[0]['thr'][:, 0:512], 'S0_thr'),
                        thi=(S[0]['thi'][:, 0:512], 'S0_thi'), a2=(S[0]['a2'][:, 0:512], 'S0_a2')),
                1: dict(xc=(S[1]['xc'][:, 0:512], 'S1_xc'), xcb=(S[1]['xcb'][:, 0:512], 'S1_xcb'), thr=(S[1]['thr'][:, 0:512], 'S1_thr'),
                        thi=(S[1]['thi'][:, 0:512], 'S1_thi'), a2=(S[1]['a2'][:, 0:512], 'S1_a2')),
                2: dict(xc=(rd[:, 0:512], 'rd'), xcb=(diff[:, 0, :], 'diff:0'), thr=(sA[:, 0, 0:512], 'sA:0'),
                        thi=(sA[:, 1, 0:512], 'sA:1'), a2=(o_pool[:, 0, :], 'o_pool:0')),
                3: dict(xc=(sC[:, 0:512], 'sC'), xcb=(diff[:, 1, :], 'diff:1'), thr=(sB[:, 0, 0:512], 'sB:0'),
                        thi=(sB[:, 1, 0:512], 'sB:1'), a2=(o_pool[:, 1, :], 'o_pool:1')),
            }

            def rb(c, n):
                return RGB[c][n][0][:, 0:N]

            def rk(c, n):
                return RGB[c][n][1]

            C4 = range(4)
            for c in C4:
                act(fv(rb(c, 'xc')), hv(xrnn, c, WX, 0, T), AF.Identity, ['xrnn:%d' % c, 'pv'], [rk(c, 'xc')],
                    scale=pvc('rnn_conv_w', 0 * 4 + c), bias=pvc('rnn_conv_b', c))
            for c in C4:
                xcv = fv(rb(c, 'xc'))
                for tap in range(1, 4):
                    stt('dve', xcv, hv(xrnn, c, WX, tap, tap + T), pvc('rnn_conv_w', tap * 4 + c), xcv, ALU.mult, ALU.add,
                        ['xrnn:%d' % c, 'pv', rk(c, 'xc')], [rk(c, 'xc')])
            for c in C4:
                cp('pool', rb(c, 'xcb'), rb(c, 'xc'), [rk(c, 'xc')], [rk(c, 'xcb')])
                ffn_cast()
                if kind == 'p' and not last:
                    cp('pool', xrnn[:, c, 0:3], xrnn[:, c, 512:515], ['xrnn:%d' % c], ['xrnn:%d' % c])
            for pair in range(2):
                cs = [2 * pair, 2 * pair + 1]
                gb = {}
                for c in cs:
                    ba = bank()
                    mm(psb[ba][:, 0:N], bdw[:, 2 + c, :], rb(c, 'xcb'), True, True, ['bdw', rk(c, 'xcb')], ['ps:%d' % ba])
                    bx = bank()
                    mm(psb[bx][:, 0:N], bdw[:, 6 + c, :], rb(c, 'xcb'), True, True, ['bdw', rk(c, 'xcb')], ['ps:%d' % bx])
                    gb[c] = (ba, bx)
                for c in cs:
                    ba, bx = gb[c]
                    act(rb(c, 'thr'), psb[ba][:, 0:N], AF.Tanh, ['ps:%d' % ba, 'pv'], [rk(c, 'thr')], scale=0.5, bias=pv[:, PV_HBA + c:PV_HBA + c + 1])
                    act(rb(c, 'thi'), psb[bx][:, 0:N], AF.Tanh, ['ps:%d' % bx, 'pv'], [rk(c, 'thi')], scale=0.5, bias=pv[:, PV_HBX + c:PV_HBX + c + 1])
            for c in C4:
                hc_ = pv[:, PV_HC + c:PV_HC + c + 1]
                c2_ = pv[:, PV_C2 + c:PV_C2 + c + 1]
                act(rb(c, 'a2'), rb(c, 'thr'), AF.Exp, [rk(c, 'thr'), 'pv'], [rk(c, 'a2')], scale=c2_, bias=c2_)
                act(rb(c, 'thr'), rb(c, 'thr'), AF.Exp, [rk(c, 'thr'), 'pv'], [rk(c, 'thr')], scale=hc_, bias=hc_)
            for c in C4:
                tsc('dve', rb(c, 'a2'), rb(c, 'a2'), -1.0, 1.0 + 1.2e-7, ALU.mult, ALU.add, [rk(c, 'a2')], [rk(c, 'a2')])
                stt('dve', rb(c, 'thi'), rb(c, 'thi'), 1.0, rb(c, 'xc'), ALU.add, ALU.mult, [rk(c, 'thi'), rk(c, 'xc')], [rk(c, 'thi')])
            for c in C4:
                act(rb(c, 'a2'), rb(c, 'a2'), AF.Ln, [rk(c, 'a2')], [rk(c, 'a2')])
                act(rb(c, 'a2'), rb(c, 'a2'), AF.Exp, [rk(c, 'a2')], [rk(c, 'a2')], scale=0.5)
            for c in C4:
                stt('dve', rb(c, 'thi'), rb(c, 'thi'), 0.5, rb(c, 'a2'), ALU.mult, ALU.mult, [rk(c, 'thi'), rk(c, 'a2')], [rk(c, 'thi')])
                if kind == 's':
                    a0 = fv(rb(c, 'thr'))[:, :, 0]
                    b0 = fv(rb(c, 'thi'))[:, :, 0]
                    tt('dve', t16[:, 0:16], a0, h0T[:, c, :], ALU.mult, [rk(c, 'thr'), 'h0T'], ['t16'])
                    tt('dve', b0, b0, t16[:, 0:16], ALU.add, [rk(c, 'thi'), 't16'], [rk(c, 'thi')])
                    mset('dve', a0, 0.0, [rk(c, 'thr')])
                    init = 0.0
                else:
                    init = hcar[:, c:c + 1]
                P.op('dve', (lambda c=c, init=init: lambda e: e.tensor_tensor_scan(
                    out=rb(c, 'a2'), data0=rb(c, 'thr'), data1=rb(c, 'thi'), initial=init,
                    op0=ALU.mult, op1=ALU.add))(), [rk(c, 'thr'), rk(c, 'thi'), 'hcar'], [rk(c, 'a2')])
                if kind == 'p':
                    cp('dve', hcar[:, c:c + 1], rb(c, 'a2')[:, N - 1:N], [rk(c, 'a2')], ['hcar'])
                else:
                    cp('dve', hls[:, c, :], fv(rb(c, 'a2'))[:, :, 3], [rk(c, 'a2')], ['hls'])
                tt('dve', gg[:, c, 0:N], gg[:, c, 0:N], rb(c, 'a2'), ALU.mult, ['gg:%d' % c, rk(c, 'a2')], ['gg:%d' % c])
            if kind == 's':
                for half in range(2):
                    for ci in range(2):
                        c = half * 2 + ci
                        cp('pool', stg[:, ci * 48:(ci + 1) * 48].rearrange("p (s r) -> p s r", r=3), hv(xrnn, c, WX, 4, 7), ['xrnn:%d' % c], ['stg'])
                    dl = []
                    for ci in range(2):
                        c = half * 2 + ci
                        for r in range(3):
                            dl.append((O['conv_s'][:, r, c * 128:(c + 1) * 128], ci * 48 + r, 3, 16))
                    rows_out(stg[:, 0:96], 96, ['stg'], dl)
                cp('pool', stg[:, 0:64].rearrange("p (c s) -> p c s", s=16), hls[:, :, :], ['hls'], ['stg'])
                rows_out(stg[:, 0:64], 64, ['stg'], [(O['h_s'][:, c * 128:(c + 1) * 128], c * 16, 1, 16) for c in range(4)])
            elif last:
                cp('pool', stg[:, 0:12].rearrange("p (c r) -> p c r", r=3), xrnn[:, :, 512:515], ['xrnn:%d' % c for c in range(4)], ['stg'])
                cp('pool', stg[:, 12:16], hcar[:, :], ['hcar'], ['stg'])
                rows_out(stg[:, 0:16], 16, ['stg'],
                         [(O['conv_p'][:, c * 128:(c + 1) * 128], c * 3, 1, 3) for c in range(4)] +
                         [(O['h_p'][:, c * 128:(c + 1) * 128], 12 + c, 1, 1) for c in range(4)])

            rms_group(1)

            s = gu_slot_of[(pidx, 'out')]
            wv = gu[s][:, 0:8192].rearrange("p (k n) -> p k n", n=1024)
            for j in range(ntt):
                for half in range(2):
                    b = bank()
                    for k in range(8):
                        mm(psb[b][:tp, 0:512], mixT[:, k, j * 128:j * 128 + tp], wv[:, k, half * 512:(half + 1) * 512], k == 0, k == 7,
                           ['gu:%d' % s, 'gub:%d' % s, 'mixT:%d' % k], ['ps:%d' % b])
                    xs_ = xtok[:tp, j, half * 512:(half + 1) * 512]
                    tt('dve', xs_, xs_, psb[b][:tp, 0:512], ALU.add, ['xtok:%d' % j, 'ps:%d' % b], ['xtok:%d' % j])
            gu_prefetch()
            if d_next[0] == 0:
                d_prefetch(); d_prefetch()

            if kind == 'p' and sti < 3:
                for j in range(4):
                    P.dma('sp', xstage[j][0], I['xp'][(sti + 1) * 512 + j * 128:(sti + 1) * 512 + (j + 1) * 128, :], (), xstage[j][1])
            if last:
                state_loads()
                st_gen[0] = state_units()
            nu = [None]
            if kind == 'p' and sti < 3:
                nu[0] = norm_units(4, 128, 'g_mix_norm', xstage, mixT, 'mixT')
                norm1_done.add(pidx + 1)
            norm_and_transpose(ntt, tp, 'g_ffn_norm')

            def down_part(sg):
                ds = d_slot_of[(pidx, sg)]
                dv = dn[ds][:, :].rearrange("p (k n) -> p k n", n=1024)
                hb = hT[sg % 2]
                for j in range(ntt):
                    for half in range(2):
                        b = bank()
                        for cc in range(4):
                            mm(psb[b][:tp, 0:512], hb[:, cc, j * 128:j * 128 + tp], dv[:, cc, half * 512:(half + 1) * 512], cc == 0, cc == 3,
                               ['dn:%d' % ds, 'hT:%d' % (sg % 2)], ['ps:%d' % b])
                        xs_ = xtok[:tp, j, half * 512:(half + 1) * 512]
                        tt('dve', xs_, xs_, psb[b][:tp, 0:512], ALU.add, ['xtok:%d' % j, 'ps:%d' % b], ['xtok:%d' % j])
                d_prefetch()

            for sg in range(6):
                s = gu_slot_of[(pidx, 'gu%d' % sg)]
                gv = gu[s][:, 0:4096].rearrange("p (k n) -> p k n", n=512)
                uv = gu[s][:, 4096:8192].rearrange("p (k n) -> p k n", n=512)
                for cc in range(4):
                    c = sg * 4 + cc
                    par = c % 2
                    gk = 'gbuf:%d' % par
                    fk = 'ftmp:%d' % par
                    bg = bank()
                    for k in range(8):
                        mm(psb[bg][:, 0:N], gv[:, k, cc * 128:(cc + 1) * 128], xnT[:, k, 0:N], k == 0, k == 7,
                           ['gu:%d' % s, 'gub:%d' % s, 'xnT:%d' % k], ['ps:%d' % bg])
                    bu = bank()
                    for k in range(8):
                        mm(psb[bu][:, 0:N], uv[:, k, cc * 128:(cc + 1) * 128], xnT[:, k, 0:N], k == 0, k == 7,
                           ['gu:%d' % s, 'gub:%d' % s, 'xnT:%d' % k], ['ps:%d' % bu])
                    gb3 = gbuf[par][:, 0:nseq * WG].rearrange("p (s w) -> p s w", w=WG)
                    if kind == 's':
                        cp('pool', gb3[:, :, 0:2], gcar_s[:, c, :, :], ['gcs:%d' % c], [gk])
                    else:
                        cp('pool', gb3[:, :, 0:2], gcar[:, c:c + 1, :], ['gcar:%d' % c], [gk])
                    act(gb3[:, :, 2:2 + T], fv(psb[bg][:, 0:N]), AF.Copy, ['ps:%d' % bg], [gk])
                    fw_ = PV['ffn_conv_w']
                    ft3 = fv(ftmp[par][:, 0:N])
                    act(ft3, fv(psb[bg][:, 0:N]), AF.Identity, ['ps:%d' % bg, 'pv'], [fk],
                        scale=pv[:, fw_ + 48 + c:fw_ + 48 + c + 1], bias=pvc('ffn_conv_b', c))
                    stt('dve', ft3, gb3[:, :, 1:1 + T], pv[:, fw_ + 24 + c:fw_ + 24 + c + 1], ft3, ALU.mult, ALU.add, [gk, 'pv', fk], [fk])
                    stt('dve', ft3, gb3[:, :, 0:T], pv[:, fw_ + c:fw_ + c + 1], ft3, ALU.mult, ALU.add, [gk, 'pv', fk], [fk])
                    if kind == 's':
                        cp('pool', gcar_s[:, c, :, :], gb3[:, :, T:T + 2], [gk], ['gcs:%d' % c])
                    else:
                        cp('pool', gcar[:, c:c + 1, :], gb3[:, :, T:T + 2], [gk], ['gcar:%d' % c])
                    act(ftmp[par][:, 0:N], ftmp[par][:, 0:N], AF.Gelu_apprx_tanh, [fk], [fk])
                    tt('dve', hT[sg % 2][:, cc, 0:N], ftmp[par][:, 0:N], psb[bu][:, 0:N], ALU.mult, [fk, 'ps:%d' % bu], ['hT:%d' % (sg % 2)])
                    if cc == 1 and sg > 0:
                        down_part(sg - 1)
                    if last and sg >= 1:
                        for _ in range(2):
                            next(st_gen[0], None)
                    if nu[0] is not None and ((sg == 3 and cc >= 1) or sg == 4):
                        next(nu[0], None)
                gu_prefetch()
                if kind == 's':
                    src = gcar_s[:, sg * 4:(sg + 1) * 4, :, :].rearrange("p c s r -> p (c s r)")
                    dl = []
                    for ci in range(4):
                        c = sg * 4 + ci
                        for r in range(2):
                            dl.append((O['ffn_s'][:, r, c * 128:(c + 1) * 128], ci * 32 + r, 2, 16))
                    rows_out(src, 128, ['gcs:%d' % (sg * 4 + ci) for ci in range(4)], dl)
            if nu[0] is not None:
                for _ in nu[0]:
                    pass
            if last:
                for _ in st_gen[0]:
                    pass
            down_part(5)

            if kind == 's':
                pass
            elif last:
                src = gcar[:, :, :].rearrange("p c r -> p (c r)")
                dl = []
                for c in range(24):
                    for r in range(2):
                        dl.append((O['ffn_p'][r:r + 1, c * 128:(c + 1) * 128], c * 2 + r, 1, 1))
                rows_out(src, 48, ['gcar:%d' % c for c in range(24)], dl)

            for j in range(ntt):
                act(xnb[j % 2][:tp, :], xtok[:tp, j, :], AF.Square, ['xtok:%d' % j], ['xnb:%d' % (j % 2), 'ss'], accum=ss[:tp, 4 + j:5 + j])
            tsc('dve', rstd[:tp, 4:4 + ntt], ss[:tp, 4:4 + ntt], 1.0 / 1024, EPS, ALU.mult, ALU.add, ['ss'], ['rstd'])
            rsqrt_inplace(rstd[:tp, 4:4 + ntt], 'rstd')
            for j in range(ntt):
                stt('dve', xtok[:tp, j, :], xtok[:tp, j, :], rstd[:tp, 4 + j:5 + j], gfin[:tp, :], ALU.mult, ALU.mult,
                    ['xtok:%d' % j, 'rstd', 'gfin'], ['xtok:%d' % j])
                P.dma('sp', ydst[j], xtok[:tp, j, :], ['xtok:%d' % j], ())

        st_gen = [None]

        def state_loads():
            stf_a = gg[0:32, :, :].rearrange("p a b -> p (a b)")
            stf_b = rbc[0:32, 0:2, :].rearrange("p a b -> p (a b)")
            P.dma('sp', stf_a, I['st_ffn'][:, 0:2048], (), ['gg:0', 'gg:1', 'gg:2', 'gg:3'])
            P.dma('sp', stf_b, I['st_ffn'][:, 2048:3072], (), ['rbc:0', 'rbc:1'])
            P.dma('sp', o_pool[:, 0, 0:256], I['st_pool'][0:128, :], (), ['o_pool:0'])
            P.dma('sp', o_pool[0:112, 1, 0:256], I['st_pool'][128:240, :], (), ['o_pool:1'])
            P.dma('sp', o_xa[0:48, 0, :], I['st_conv'], (), ['o_xa:0'])
            P.dma('sp', o_xa[0:16, 1, :], I['st_h'], (), ['o_xa:1'])
            P.dma('sp', O['pool_s'][:, 0:11, :], I['st_pool'].rearrange("(b r) c -> b r c", r=15)[:, 4:15, :], (), ())

        def state_units():
            stf_a = gg[0:32, :, :].rearrange("p a b -> p (a b)")
            stf_b = rbc[0:32, 0:2, :].rearrange("p a b -> p (a b)")

            def st_tr(src_ap, nr, src_keys):
                b = bank()
                tr(psb[b][:, 0:nr], src_ap, identf[0:nr, 0:nr], list(src_keys) + ['identf'], ['ps:%d' % b])
                return b

            for c in range(2):
                flat = upool[:, c, 0:16 * 19].rearrange("p (s w) -> p s w", w=19)
                b = st_tr(o_pool[0:128, 0, c * 128:(c + 1) * 128], 128, ['o_pool:0'])
                cp('dve', flat[:, 0:8, 0:15], psb[b][:, 0:120].rearrange("p (s r) -> p s r", r=15), ['ps:%d' % b], ['upool:%d' % c])
                cp('dve', flat[:, 8, 0:8], psb[b][:, 120:128], ['ps:%d' % b], ['upool:%d' % c])
                yield
                b = st_tr(o_pool[0:112, 1, c * 128:(c + 1) * 128], 112, ['o_pool:1'])
                cp('dve', flat[:, 8, 8:15], psb[b][:, 0:7], ['ps:%d' % b], ['upool:%d' % c])
                cp('dve', flat[:, 9:16, 0:15], psb[b][:, 7:112].rearrange("p (s r) -> p s r", r=15), ['ps:%d' % b], ['upool:%d' % c])
                yield
            for c in range(4):
                b = st_tr(o_xa[0:48, 0, c * 128:(c + 1) * 128], 48, ['o_xa:0'])
                cp('dve', xrnn[:, c, 0:16 * 7].rearrange("p (s w) -> p s w", w=7)[:, :, 0:3],
                   psb[b][:, 0:48].rearrange("p (s r) -> p s r", r=3), ['ps:%d' % b], ['xrnn:%d' % c])
                yield
            for c in range(4):
                b = st_tr(o_xa[0:16, 1, c * 128:(c + 1) * 128], 16, ['o_xa:1'])
                cp('dve', h0T[:, c, :], psb[b][:, 0:16], ['ps:%d' % b], ['h0T'])
                yield
            for c in range(24):
                if c < 16:
                    src, sk_ = stf_a[:, c * 128:(c + 1) * 128], ['gg:%d' % (c // 4)]
                else:
                    src, sk_ = stf_b[:, (c - 16) * 128:(c - 15) * 128], ['rbc:%d' % ((c - 16) // 4)]
                b = st_tr(src, 32, sk_)
                act(gcar_s[:, c, :, :], psb[b][:, 0:32].rearrange("p (s r) -> p s r", r=2), AF.Copy, ['ps:%d' % b], ['gcs:%d' % c])
                yield

        for sti in range(4):
            run_pass(sti, 'p', sti)
        run_pass(4, 's', 0)
        P.emit(st)
    return nc


_NC_CACHE = {}


def kernel(**inp):
    f = lambda a: np.ascontiguousarray(np.asarray(a, dtype=np.float32))
    shared = {
        'g_mix_norm': f(inp['g_mix_norm'][0]).reshape(8, 128),
        'w_in': f(inp['w_in'][0]),
        'w_pool': f(inp['w_pool'][0]),
        'pool_scale': f(inp['pool_scale'][0]).reshape(2, 128),
        'rnn_conv_w': f(inp['rnn_conv_w'][0]).reshape(16, 128),
        'rnn_conv_b': f(inp['rnn_conv_b'][0]).reshape(4, 128),
        'w_rg_a': f(inp['w_rg_a'][0]),
        'b_rg_a': f(inp['b_rg_a'][0]).reshape(4, 128),
        'w_rg_x': f(inp['w_rg_x'][0]),
        'b_rg_x': f(inp['b_rg_x'][0]).reshape(4, 128),
        'rg_lambda': f(inp['rg_lambda'][0]).reshape(4, 128),
        'g_mem_norm': f(inp['g_mem_norm'][0]).reshape(8, 128),
        'w_mem_k': f(inp['w_mem_k'][0]),
        'w_mem_v': f(inp['w_mem_v'][0]),
        'g_mix_out': f(inp['g_mix_out'][0]).reshape(8, 128),
        'w_out': f(inp['w_out'][0]),
        'g_ffn_norm': f(inp['g_ffn_norm'][0]).reshape(8, 128),
        'w_ff_gate': f(inp['w_ff_gate'][0]),
        'w_ff_up': f(inp['w_ff_up'][0]),
        'ffn_conv_w': f(inp['ffn_conv_w'][0]).reshape(72, 128),
        'ffn_conv_b': f(inp['ffn_conv_b'][0]).reshape(24, 128),
        'w_ff_down': f(inp['w_ff_down'][0]),
        'g_final': f(inp['g_final']).reshape(1, 1024),
    }
    in_maps = []
    for c in range(NCORES):
        b = slice(16 * c, 16 * c + 16)
        m = dict(shared)
        m['xp'] = f(inp['x_prompt'][c])
        m['xs'] = f(inp['x_sample'][b]).reshape(64, 1024)
        m['mem'] = f(inp['mem_prompt'][c])
        m['st_pool'] = f(inp['state_pool'][0, b]).reshape(240, 256)
        m['st_conv'] = f(inp['state_rnn_conv'][0, b]).reshape(48, 512)
        m['st_h'] = f(inp['state_rnn_h'][0, b])
        m['st_ffn'] = f(inp['state_ffn_conv'][0, b]).reshape(32, 3072)
        m['ck'] = f(inp['cache_mem_k'][0, b]).reshape(16, 256, 256)
        m['cv'] = f(inp['cache_mem_v'][0, b]).reshape(16, 256, 256)
        in_maps.append(m)
    if 'nc' not in _NC_CACHE:
        _NC_CACHE['nc'] = build()
    nc = _NC_CACHE['nc']
    res = run_bass_kernel_spmd(nc, in_maps, core_ids=list(range(NCORES)))
    R = res.results

    def cat(name, shape):
        return np.stack([np.asarray(R[c][name], dtype=np.float32).reshape(shape) for c in range(NCORES)], axis=0)

    y_p = cat('y_p', (2048, 1024))
    y_s = cat('y_s', (16, 4, 1024)).reshape(128, 4, 1024)
    pool_p = cat('pool_p', (15, 256))[None]
    conv_p = cat('conv_p', (3, 512))[None]
    h_p = cat('h_p', (512,))[None]
    ffn_p = cat('ffn_p', (2, 3072))[None]
    mk_p = cat('mk_p', (256, 4, 64))[None]
    mv_p = cat('mv_p', (256, 4, 64))[None]
    pool_s = cat('pool_s', (16, 15, 256)).reshape(1, 128, 15, 256)
    conv_s = cat('conv_s', (16, 3, 512)).reshape(1, 128, 3, 512)
    h_s = cat('h_s', (16, 512)).reshape(1, 128, 512)
    ffn_s = cat('ffn_s', (16, 2, 3072)).reshape(1, 128, 2, 3072)
    return (y_p, y_s, pool_p, conv_p, h_p, ffn_p, mk_p, mv_p, pool_s, conv_s, h_s, ffn_s)
```

```python
import numpy as np
from contextlib import ExitStack
import concourse.bass as bass
import concourse.mybir as mybir
from concourse.bass_utils import run_bass_kernel_spmd

F32 = mybir.dt.float32
BF16 = mybir.dt.bfloat16
ALU = mybir.AluOpType
AF = mybir.ActivationFunctionType

ENGS = ['pe', 'act', 'dve', 'pool', 'sp']
N_DMA_SEMS = 32
EPS = 1e-6
NCORES = 8
KSTOP = 10 ** 9


class Prog:
    def __init__(self, nc):
        self.nc = nc
        self.ops = {e: [] for e in ENGS}
        self.key_w = {}
        self.key_r = {}
        self.waited = {e: {} for e in ENGS}
        self.dma_val = [0] * N_DMA_SEMS
        self.dma_rr = {'sp': 0, 'pool': 0, 'act': 0}
        self.signal = {e: set() for e in ENGS}

    def _deps(self, eng, reads, writes, is_dma):
        deps = []
        for k in reads:
            t = self.key_w.get(k)
            if t is not None:
                deps.append((t, 'raw'))
        for k in writes:
            t = self.key_w.get(k)
            if t is not None:
                deps.append((t, 'waw'))
            for t in self.key_r.get(k, {}).values():
                deps.append((t, 'war'))
        out = []
        w = self.waited[eng]
        for t, kind in deps:
            if t[0] == 'e':
                _, e2, idx = t
                if e2 == eng and not is_dma:
                    if eng == 'pe':
                        continue
                if w.get(('e', e2), -1) >= idx:
                    continue
                w[('e', e2)] = idx
                self.signal[e2].add(idx)
                out.append(t)
            else:
                _, s, val = t
                if w.get(('d', s), 0) >= val:
                    continue
                w[('d', s)] = val
                out.append(t)
        return out

    def _register(self, tok, reads, writes):
        for k in writes:
            self.key_w[k] = tok
            self.key_r[k] = {}
        for k in reads:
            d = self.key_r.setdefault(k, {})
            d[(tok[0], tok[1])] = tok

    def op(self, eng, fn, reads=(), writes=()):
        self.nrec = getattr(self, 'nrec', 0) + 1
        if self.nrec > KSTOP:
            return
        deps = self._deps(eng, reads, writes, False)
        idx = len(self.ops[eng])
        self.ops[eng].append(dict(fn=fn, deps=deps, dma=None))
        self._register(('e', eng, idx), reads, writes)

    def dma(self, eng, out, in_, reads=(), writes=(), after=()):
        self.nrec = getattr(self, 'nrec', 0) + 1
        if self.nrec > KSTOP:
            return
        half = N_DMA_SEMS // 2
        s = self.dma_rr[eng] + (half if eng == 'pool' else 0)
        self.dma_rr[eng] = (self.dma_rr[eng] + 1) % half
        deps = self._deps(eng, list(reads) + list(after), writes, True)
        prev = self.dma_val[s]
        w = self.waited[eng]
        if prev > 0 and w.get(('d', s), 0) < prev:
            w[('d', s)] = prev
            deps.append(('d', s, prev))
        val = prev + 16
        self.dma_val[s] = val
        self.ops[eng].append(dict(fn=lambda e: e.dma_start(out=out, in_=in_), deps=deps, dma=(s, val)))
        self._register(('d', s, val), reads, writes)

    def emit(self, stack):
        nc = self.nc
        esem = {e: stack.enter_context(nc.semaphore("s_" + e)) for e in ENGS}
        dsem = [stack.enter_context(nc.semaphore("d_%d" % i)) for i in range(N_DMA_SEMS)]
        rank = {}
        for e in ENGS:
            for r, idx in enumerate(sorted(self.signal[e])):
                rank[(e, idx)] = r + 1
        fin = [(s, self.dma_val[s]) for s in range(N_DMA_SEMS) if self.dma_val[s] > 0]

        def replay(e, engobj):
            for i, o in enumerate(self.ops[e]):
                for t in o['deps']:
                    if t[0] == 'e':
                        engobj.wait_ge(esem[t[1]], rank[(t[1], t[2])])
                    else:
                        engobj.wait_ge(dsem[t[1]], t[2])
                ins = o['fn'](engobj)
                if o['dma'] is not None:
                    ins.then_inc(dsem[o['dma'][0]], 16)
                elif i in self.signal[e]:
                    ins.then_inc(esem[e], 1)
            if e == 'sp':
                for s, v in fin:
                    engobj.wait_ge(dsem[s], v)

        with nc.Block() as block:
            @block.tensor
            def _(pe):
                replay('pe', pe)

            @block.scalar
            def _(act):
                replay('act', act)

            @block.vector
            def _(dve):
                replay('dve', dve)

            @block.gpsimd
            def _(pool):
                replay('pool', pool)

            @block.sync
            def _(sp):
                replay('sp', sp)


PV = {}
_r = 0
for _n, _rows in [('g_mix_norm', 8), ('pool_scale', 2), ('rnn_conv_w', 16), ('rnn_conv_b', 4), ('b_rg_a', 4),
                  ('b_rg_x', 4), ('rg_lambda', 4), ('g_mem_norm', 8), ('g_mix_out', 8), ('g_ffn_norm', 8),
                  ('ffn_conv_b', 24)]:
    PV[_n] = _r
    _r += _rows
PV_A_ROWS = _r
PV['ffn_conv_w'] = 128
PV_COLS = 128 + 72
PV_HBA, PV_HBX, PV_HC, PV_C2 = 200, 204, 208, 212
PV_TOT = 216


def build():
    nc = bass.Bass("TRN2", target_bir_lowering=False)

    def din(name, shape):
        return nc.dram_tensor(name, list(shape), F32, kind="ExternalInput").ap()

    def dout(name, shape):
        return nc.dram_tensor(name, list(shape), F32, kind="ExternalOutput").ap()

    def dint(name, shape):
        return nc.dram_tensor(name, list(shape), BF16, kind="Internal").ap()

    I = {}
    for n, s in [('xp', (2048, 1024)), ('xs', (64, 1024)), ('mem', (256, 1024)), ('st_pool', (240, 256)),
                 ('st_conv', (48, 512)), ('st_h', (16, 512)), ('st_ffn', (32, 3072)), ('ck', (16, 256, 256)),
                 ('cv', (16, 256, 256)),
                 ('g_mix_norm', (8, 128)), ('w_in', (1024, 1536)), ('w_pool', (4, 64, 64)), ('pool_scale', (2, 128)),
                 ('rnn_conv_w', (16, 128)), ('rnn_conv_b', (4, 128)), ('w_rg_a', (8, 64, 64)), ('b_rg_a', (4, 128)),
                 ('w_rg_x', (8, 64, 64)), ('b_rg_x', (4, 128)), ('rg_lambda', (4, 128)), ('g_mem_norm', (8, 128)),
                 ('w_mem_k', (1024, 256)), ('w_mem_v', (1024, 256)), ('g_mix_out', (8, 128)), ('w_out', (1024, 1024)),
                 ('g_ffn_norm', (8, 128)), ('w_ff_gate', (1024, 3072)), ('w_ff_up', (1024, 3072)),
                 ('ffn_conv_w', (72, 128)), ('ffn_conv_b', (24, 128)), ('w_ff_down', (3072, 1024)),
                 ('g_final', (1, 1024))]:
        I[n] = din(n, s)
    O = {}
    for n, s in [('y_p', (2048, 1024)), ('y_s', (64, 1024)), ('pool_p', (15, 256)), ('conv_p', (3, 512)),
                 ('h_p', (1, 512)), ('ffn_p', (2, 3072)), ('mk_p', (256, 256)), ('mv_p', (256, 256)),
                 ('pool_s', (16, 15, 256)), ('conv_s', (16, 3, 512)), ('h_s', (16, 512)), ('ffn_s', (16, 2, 3072))]:
        O[n] = dout(n, s)
    WB = {n: dint(n, s) for n, s in [('w_in_b', (1024, 1536)), ('w_out_b', (1024, 1024)), ('w_gate_b', (1024, 3072)),
                                     ('w_up_b', (1024, 3072)), ('w_down_b', (3072, 1024)), ('w_mk_b', (1024, 256)),
                                     ('w_mv_b', (1024, 256))]}

    with ExitStack() as st:
        def sb(name, shape, dt=F32):
            return st.enter_context(nc.sbuf_tensor(name, list(shape), dt))

        def pst(name, shape, dt=F32):
            return st.enter_context(nc.psum_tensor(name, list(shape), dt))

        P = Prog(nc)
        xtok = sb("xtok", [128, 4, 1024])
        xnb = [sb("xnb%d" % i, [128, 1024], BF16) for i in range(2)]
        xnT = sb("xnT", [128, 8, 512], BF16)
        upool = sb("upool", [128, 2, 527])
        xrnn = sb("xrnn", [128, 4, 515])
        gg = sb("gg", [128, 4, 512])
        qb = sb("qb", [128, 2, 512], BF16)
        S = [dict(xc=sb("S%d_xc" % i, [128, 512]), xcb=sb("S%d_xcb" % i, [128, 512], BF16),
                  thr=sb("S%d_thr" % i, [128, 512]), thi=sb("S%d_thi" % i, [128, 512]),
                  a2=sb("S%d_a2" % i, [128, 512])) for i in range(2)]
        o_pool = sb("o_pool", [128, 2, 512])
        o_xa = sb("o_xa", [128, 2, 512])
        sq = [sb("sq%d" % i, [128, 512], BF16) for i in range(2)]
        rbc = sb("rbc", [128, 3, 512])
        mixT = sb("mixT", [128, 8, 512], BF16)
        sA = sb("sA", [128, 2, 527]); sB = sb("sB", [128, 2, 527]); sC = sb("sC", [128, 527])
        diff = sb("diff", [128, 2, 512], BF16)
        KT = sb("KT", [128, 2, 256], BF16)
        Vz = sb("Vz", [128, 4, 2, 128], BF16)
        PT = [sb("PT%d" % i, [128, 2, 512], BF16) for i in range(2)]
        rd = sb("rd", [128, 512])
        kvf = [sb("kvf%d" % i, [128, 2, 256]) for i in range(2)]
        KTs = [sb("KTs%d" % i, [128, 2, 256], BF16) for i in range(2)]
        Vzs = [sb("Vzs%d" % i, [128, 4, 2, 128], BF16) for i in range(2)]
        gbuf = [sb("gbuf%d" % i, [128, 514]) for i in range(2)]
        ftmp = [sb("ftmp%d" % i, [128, 512]) for i in range(2)]
        hT = [sb("hT%d" % i, [128, 4, 512], BF16) for i in range(2)]
        gcar = sb("gcar", [128, 24, 2])
        gcar_s = sb("gcar_s", [128, 24, 16, 2])
        gfin = sb("gfin", [128, 1024])
        pv = sb("pv", [128, PV_TOT])
        ident = sb("ident", [128, 128], BF16); identf = sb("identf", [128, 128])
        ones_b = sb("ones_b", [128, 128], BF16)
        onesz = sb("onesz", [128, 2, 128], BF16)
        bdw = sb("bdw", [128, 10, 128], BF16)
        rc15 = sb("rc15", [128, 2, 15])
        hcar = sb("hcar", [128, 4])
        h0T = sb("h0T", [128, 4, 16])
        hls = sb("hls", [128, 4, 16])
        stg = sb("stg", [128, 128]); stg2 = [sb("stg2_%d" % i, [128, 128]) for i in range(2)]
        ss = sb("ss", [128, 8]); rstd = sb("rstd", [128, 8])
        t16 = sb("t16", [128, 16])
        gu = [sb("gu%d" % i, [128, 8192], BF16) for i in range(2)]
        dn = [sb("dn%d" % i, [128, 4096], BF16) for i in range(2)]
        psb = [pst("ps%d" % i, [128, 512]) for i in range(6)]
        ptr = pst("ptr", [128, 1024], BF16)
        ptr2 = pst("ptr2", [128, 1024], BF16)
        ptrs = [ptr, ptr2]
        bank_rr = [0]
        held = set()

        def bank(exclude=()):
            b = bank_rr[0]
            while b in exclude or b in held:
                b = (b + 1) % 6
            bank_rr[0] = (b + 1) % 6
            return b

        def act(out, in_, func, reads, writes, scale=None, bias=None, accum=None):
            kw = {}
            if scale is not None:
                kw['scale'] = scale
            if bias is not None:
                kw['bias'] = bias
            if accum is not None:
                kw['accum_out'] = accum
            P.op('act', lambda e: e.activation(out=out, in_=in_, func=func, **kw), reads, writes)

        def tsc(eng, out, in0, s1, s2, op0, op1, reads, writes):
            if s2 is None:
                P.op(eng, lambda e: e.tensor_scalar(out=out, in0=in0, scalar1=s1, scalar2=None, op0=op0), reads, writes)
            else:
                P.op(eng, lambda e: e.tensor_scalar(out=out, in0=in0, scalar1=s1, scalar2=s2, op0=op0, op1=op1), reads, writes)

        def stt(eng, out, in0, sc, in1, op0, op1, reads, writes):
            P.op(eng, lambda e: e.scalar_tensor_tensor(out=out, in0=in0, scalar=sc, in1=in1, op0=op0, op1=op1), reads, writes)

        def tt(eng, out, in0, in1, op, reads, writes):
            P.op(eng, lambda e: e.tensor_tensor(out=out, in0=in0, in1=in1, op=op), reads, writes)

        def cp(eng, out, in_, reads, writes):
            P.op(eng, lambda e: e.tensor_copy(out=out, in_=in_), reads, writes)

        def mset(eng, ap, v, writes):
            P.op(eng, lambda e: e.memset(ap, v), (), writes)

        def mm(out, lhsT, rhs, start, stop, reads, writes):
            P.op('pe', lambda e: e.matmul(out, lhsT=lhsT, rhs=rhs, start=start, stop=stop), reads, writes)

        def tr(out, in_, idn, reads, writes):
            P.op('pe', lambda e: e.transpose(out=out, in_=in_, identity=idn), reads, writes)

        def pvc(name, j=0):
            c = PV[name] + j
            return pv[:, c:c + 1]

        def rsqrt_inplace(ap, key):
            act(ap, ap, AF.Ln, [key], [key])
            act(ap, ap, AF.Exp, [key], [key], scale=-0.5)

        stg2_rr = [0]

        def rows_out(src_ap, ncols, src_keys, dmas):
            i = stg2_rr[0]
            stg2_rr[0] = 1 - i
            rb = bank()
            tr(psb[rb][0:ncols, 0:128], src_ap, identf[:], list(src_keys) + ['identf'], ['ps:%d' % rb])
            cp('dve', stg2[i][0:ncols, :], psb[rb][0:ncols, 0:128], ['ps:%d' % rb], ['stg2_%d' % i])
            for (dap, p0, step, n) in dmas:
                if step == 1:
                    sap = stg2[i][p0:p0 + n, :]
                else:
                    sap = stg2[i][p0:p0 + (n - 1) * step + 1:step, :]
                P.dma('sp', dap, sap, ['stg2_%d' % i], ())

        def cast(dst, src, key):
            P.dma('pool', dst, src, (), [key])

        NPASS = 5
        gu_seq = []
        d_seq = []
        for p in range(NPASS):
            gu_seq += [(p, 'in0'), (p, 'in1'), (p, 'out')] + [(p, 'gu%d' % sg) for sg in range(6)]
            d_seq += [(p, sg) for sg in range(6)]
        gu_next = [0]
        d_next = [0]
        gu_slot_of = {}
        d_slot_of = {}

        defer_store = [True]
        pending_store = []
        NDIRECT = 1

        def ring_load(p, dst, src32, src16, wbkey, wkeys, cast_pass):
            if p == 0 or (p == 1 and cast_pass == 1):
                P.dma('pool', dst, src32.rearrange("(k p) n -> p k n", p=128), (), wkeys)
                if p != cast_pass:
                    return
                st_fn = (lambda a=src16.rearrange("(k p) n -> p k n", p=128), b=dst, c=list(wkeys), d=wbkey:
                         P.dma('sp', a, b, c, [d]))
                if defer_store[0]:
                    pending_store.append(st_fn)
                else:
                    st_fn()
            else:
                P.dma('sp', dst, src16.rearrange("(k p) n -> p k n", p=128), [wbkey], wkeys)

        def gu_prefetch():
            i = gu_next[0]
            if i >= len(gu_seq):
                return
            gu_next[0] += 1
            s = i % 2
            item = gu_seq[i]
            gu_slot_of[item] = s
            p, k = item
            key = 'gu:%d' % s
            keyb = 'gub:%d' % s
            if k == 'in0' or k == 'in1':
                h = 0 if k == 'in0' else 1
                ring_load(p, gu[s][:, 0:6144].rearrange("p (k n) -> p k n", n=768), I['w_in'][:, h * 768:(h + 1) * 768],
                          WB['w_in_b'][:, h * 768:(h + 1) * 768], 'wb:' + k, [key, keyb], h)
            elif k == 'out':
                ring_load(p, gu[s][:, 0:8192].rearrange("p (k n) -> p k n", n=1024), I['w_out'], WB['w_out_b'], 'wb:out',
                          [key, keyb], 0)
            else:
                sg = int(k[2:])
                cs = slice(sg * 512, (sg + 1) * 512)
                ring_load(p, gu[s][:, 0:4096].rearrange("p (k n) -> p k n", n=512), I['w_ff_gate'][:, cs], WB['w_gate_b'][:, cs],
                          'wb:g%d' % sg, [key], sg % 2)
                ring_load(p, gu[s][:, 4096:8192].rearrange("p (k n) -> p k n", n=512), I['w_ff_up'][:, cs], WB['w_up_b'][:, cs],
                          'wb:u%d' % sg, [keyb], sg % 2)

        def d_prefetch():
            i = d_next[0]
            if i >= len(d_seq):
                return
            d_next[0] += 1
            s = i % 2
            item = d_seq[i]
            d_slot_of[item] = s
            p, sg = item
            rs = slice(sg * 512, (sg + 1) * 512)
            ring_load(p, dn[s][:, :].rearrange("p (k n) -> p k n", n=1024), I['w_ff_down'][rs, :], WB['w_down_b'][rs, :],
                      'wb:d%d' % sg, ['dn:%d' % s], sg % 2)

        gu_prefetch(); gu_prefetch()

        mset('pool', identf[:], 1.0, ['identf'])
        P.op('pool', lambda e: e.affine_select(out=identf[:], in_=identf[:], pattern=[[-1, 128]],
                                               compare_op=ALU.is_equal, fill=0.0, base=0, channel_multiplier=1),
             ['identf'], ['identf'])
        cp('pool', ident[:], identf[:], ['identf'], ['ident'])
        mset('pool', ones_b[:], 1.0, ['ones_b'])
        mset('pool', onesz[:], 0.0, ['onesz'])
        mset('pool', onesz[:, 0, 0:64], 1.0, ['onesz'])
        mset('pool', onesz[:, 1, 64:128], 1.0, ['onesz'])
        mset('pool', Vz[:], 0.0, ['Vz'])
        mset('pool', gcar[:], 0.0, ['gcar:%d' % c for c in range(24)])
        mset('pool', hcar[:], 0.0, ['hcar'])
        for c in range(2):
            for hh in range(2):
                w = (2, 4, 8, 16)[2 * c + hh]
                pr = slice(hh * 64, hh * 64 + 64)
                mset('pool', rc15[pr, c, :], 1.0 / w, ['rc15'])
                for t in range(w - 1):
                    mset('pool', rc15[pr, c, t:t + 1], 1.0 / (t + 1), ['rc15'])
        for j_ in range(4):
            st_ap, st_keys = [(gg[:, 0:2, :].rearrange("p a b -> p (a b)"), ['gg:0', 'gg:1']),
                              (gg[:, 2:4, :].rearrange("p a b -> p (a b)"), ['gg:2', 'gg:3']),
                              (o_pool[:, :, :].rearrange("p a b -> p (a b)"), ['o_pool:0', 'o_pool:1']),
                              (o_xa[:, :, :].rearrange("p a b -> p (a b)"), ['o_xa:0', 'o_xa:1'])][j_]
            P.dma('sp', st_ap, I['xp'][j_ * 128:(j_ + 1) * 128, :], (), st_keys)
        P.dma('sp', gfin[:], I['g_final'][0].partition_broadcast(128), (), ['gfin'])
        stA = xtok[:, 1, 512:640]
        stB = xtok[:, 1, 640:768]
        mset('dve', xtok[:, 1, 512:768], 0.0, ['xtok:1'])
        pst_keys = []
        for n in ['g_mix_norm', 'pool_scale', 'rnn_conv_w', 'rnn_conv_b', 'b_rg_a', 'b_rg_x', 'rg_lambda',
                  'g_mem_norm', 'g_mix_out', 'g_ffn_norm', 'ffn_conv_b']:
            r0 = PV[n]
            nr = I[n].shape[0]
            P.dma('sp', xtok[r0:r0 + nr, 1, 512:640], I[n], (), ['pst:' + n], after=['xtok:1'])
            pst_keys.append('pst:' + n)
        P.dma('sp', xtok[0:72, 1, 640:768], I['ffn_conv_w'], (), ['pst:fcw'], after=['xtok:1'])
        pst_keys.append('pst:fcw')
        tr(psb[0][:, 0:128], stA, identf[:], ['xtok:1', 'identf'] + pst_keys, ['ps:0'])
        tr(psb[0][:, 128:256], stB, identf[:], ['xtok:1', 'identf'] + pst_keys, ['ps:0'])
        cp('dve', pv[:, 0:256 - 56], psb[0][:, 0:200], ['ps:0'], ['pv'])
        tsc('dve', pv[:, PV_HBA:PV_HBA + 4], pv[:, PV['b_rg_a']:PV['b_rg_a'] + 4], 0.5, None, ALU.mult, None, ['pv'], ['pv'])
        tsc('dve', pv[:, PV_HBX:PV_HBX + 4], pv[:, PV['b_rg_x']:PV['b_rg_x'] + 4], 0.5, None, ALU.mult, None, ['pv'], ['pv'])
        act(pv[:, PV_HC:PV_HC + 4], pv[:, PV['rg_lambda']:PV['rg_lambda'] + 4], AF.Exp, ['pv'], ['pv'], scale=-1.0)
        act(pv[:, PV_HC:PV_HC + 4], pv[:, PV_HC:PV_HC + 4], AF.Ln, ['pv'], ['pv'], bias=1.0)
        tsc('dve', pv[:, PV_C2:PV_C2 + 4], pv[:, PV_HC:PV_HC + 4], -8.0, None, ALU.mult, None, ['pv'], ['pv'])
        tsc('dve', pv[:, PV_HC:PV_HC + 4], pv[:, PV_HC:PV_HC + 4], -4.0, None, ALU.mult, None, ['pv'], ['pv'])
        bst0 = xtok[:, 0, :]
        bst1 = xtok[:, 1, 0:256]
        mset('dve', xtok[:, 0, :], 0.0, ['xtok:0'])
        mset('dve', xtok[:, 1, 0:256], 0.0, ['xtok:1'])

        bds_keys = []

        def bd_dst(bi, hh):
            base = bst0 if bi < 8 else bst1
            b = bi if bi < 8 else bi - 8
            return base[hh * 64:hh * 64 + 64, b * 128 + hh * 64: b * 128 + hh * 64 + 64]

        for c in range(2):
            for hh in range(2):
                P.dma('sp', bd_dst(c, hh), I['w_pool'][2 * c + hh], (), ['bds:p%d%d' % (c, hh)], after=['xtok:0', 'xtok:1'])
                bds_keys.append('bds:p%d%d' % (c, hh))
        for c in range(4):
            for hh in range(2):
                P.dma('sp', bd_dst(2 + c, hh), I['w_rg_a'][2 * c + hh], (), ['bds:a%d%d' % (c, hh)], after=['xtok:0', 'xtok:1'])
                P.dma('sp', bd_dst(6 + c, hh), I['w_rg_x'][2 * c + hh], (), ['bds:x%d%d' % (c, hh)], after=['xtok:0', 'xtok:1'])
                bds_keys += ['bds:a%d%d' % (c, hh), 'bds:x%d%d' % (c, hh)]
        cp('dve', bdw[:, 0:8, :], xtok[:, 0, :].rearrange("p (b n) -> p b n", n=128), ['xtok:0'] + bds_keys, ['bdw'])
        cp('dve', bdw[:, 8:10, :], xtok[:, 1, 0:256].rearrange("p (b n) -> p b n", n=128), ['xtok:1'] + bds_keys, ['bdw'])


        def norm_units(ntt, tp, gname, srcs=None, dst=None, dkey='xnT'):
            if dst is None:
                dst = xnT
            if srcs is None:
                srcs = [(xtok[:, j, :], ['xtok:%d' % j]) for j in range(ntt)]
            for j in range(ntt):
                act(xnb[j % 2][:tp, :], srcs[j][0][:tp, :], AF.Square, srcs[j][1], ['xnb:%d' % (j % 2), 'ss'], accum=ss[:tp, j:j + 1])
            tsc('dve', rstd[:tp, 0:ntt], ss[:tp, 0:ntt], 1.0 / 1024, EPS, ALU.mult, ALU.add, ['ss'], ['rstd'])
            rsqrt_inplace(rstd[:tp, 0:ntt], 'rstd')
            yield

            def scale_(j):
                tsc('dve', xnb[j % 2][:tp, :], srcs[j][0][:tp, :], rstd[:tp, j:j + 1], None, ALU.mult, None,
                    list(srcs[j][1]) + ['rstd'], ['xnb:%d' % (j % 2)])

            def tr_(j):
                for k in range(8):
                    tr(ptrs[j % 2][:, k * 128:k * 128 + tp], xnb[j % 2][:tp, k * 128:(k + 1) * 128], ident[:tp, :tp],
                       ['xnb:%d' % (j % 2), 'ident'], ['ptr%d' % (j % 2)])

            def evac_(j):
                for k in range(8):
                    o = dst[:, k, j * 128:(j + 1) * 128]
                    i_ = ptrs[j % 2][:, k * 128:(k + 1) * 128]
                    if j % 2 == 0:
                        act(o, i_, AF.Identity, ['ptr0', 'pv'], ['%s:%d' % (dkey, k)], scale=pvc(gname, k))
                    else:
                        tsc('dve', o, i_, pvc(gname, k), None, ALU.mult, None, ['ptr1', 'pv'], ['%s:%d' % (dkey, k)])

            for step in range(ntt + 2):
                if step - 2 >= 0:
                    evac_(step - 2)
                if 0 <= step - 1 < ntt:
                    tr_(step - 1)
                if step < ntt:
                    scale_(step)
                yield

        def norm_and_transpose(ntt, tp, gname, srcs=None):
            for _ in norm_units(ntt, tp, gname, srcs):
                pass

        norm1_done = set()

        wmk = hT[0][:, :, :].rearrange("p a b -> p (a b)")[:, 0:2048].rearrange("p (k n) -> p k n", n=256)
        wmv = hT[1][:, :, :].rearrange("p a b -> p (a b)")[:, 0:2048].rearrange("p (k n) -> p k n", n=256)
        P.dma('pool', wmk, I['w_mem_k'].rearrange("(k p) n -> p k n", p=128), (), ['hT:0'])
        P.dma('pool', wmv, I['w_mem_v'].rearrange("(k p) n -> p k n", p=128), (), ['hT:1'])
        for j in range(2):
            P.dma('sp', xtok[:, 2 + j, :], I['mem'][j * 128:(j + 1) * 128, :], (), ['xtok:%d' % (2 + j)])
        memT = xnT
        for j in range(2):
            jj = 2 + j
            xb = xnb[j]
            xk = 'xnb:%d' % j
            act(xb[:, :], xtok[:, jj, :], AF.Square, ['xtok:%d' % jj], [xk, 'ss'], accum=ss[:, j:j + 1])
            tsc('dve', rstd[:, j:j + 1], ss[:, j:j + 1], 1.0 / 1024, EPS, ALU.mult, ALU.add, ['ss'], ['rstd'])
            rsqrt_inplace(rstd[:, j:j + 1], 'rstd')
            tsc('dve', xb[:, :], xtok[:, jj, :], rstd[:, j:j + 1], None, ALU.mult, None, ['xtok:%d' % jj, 'rstd'], [xk])
            for k in range(8):
                tr(ptr[:, k * 128:(k + 1) * 128], xb[:, k * 128:(k + 1) * 128], ident[:], [xk, 'ident'], ['ptr0'])
            for k in range(8):
                tsc('dve', memT[:, k, j * 128:(j + 1) * 128], ptr[:, k * 128:(k + 1) * 128], pvc('g_mem_norm', k), None,
                    ALU.mult, None, ['ptr0', 'pv'], ['xnT:%d' % k])
        def ffn_cast():
            pass

        for ch in range(2):
            b = bank()
            for k in range(8):
                mm(psb[b][:, 0:256], wmk[:, k, ch * 128:(ch + 1) * 128], memT[:, k, 0:256], k == 0, k == 7,
                   ['hT:0', 'xnT:%d' % k], ['ps:%d' % b])
            cp('dve', KT[:, ch, :], psb[b][:, 0:256], ['ps:%d' % b], ['KT'])
        for (wsrc, wkey, oname) in [(wmk, 'hT:0', 'mk_p'), (wmv, 'hT:1', 'mv_p')]:
            for mt in range(2):
                b = bank()
                for k in range(8):
                    mm(psb[b][:, 0:256], memT[:, k, mt * 128:(mt + 1) * 128], wsrc[:, k, :], k == 0, k == 7,
                       [wkey, 'xnT:%d' % k], ['ps:%d' % b])
                skey = 'S%d_xc' % mt
                cp('dve', S[mt]['xc'][:, 0:256], psb[b][:, 0:256], ['ps:%d' % b], [skey])
                P.dma('sp', O[oname][mt * 128:(mt + 1) * 128, :], S[mt]['xc'][:, 0:256], [skey], ())
                if oname == 'mv_p':
                    for h in range(4):
                        cp('pool', Vz[:, h, mt, (h % 2) * 64:(h % 2) * 64 + 64], S[mt]['xc'][:, h * 64:(h + 1) * 64],
                           [skey], ['Vz'])

        xstage = [(gg[:, 0:2, :].rearrange("p a b -> p (a b)"), ['gg:0', 'gg:1']),
                  (gg[:, 2:4, :].rearrange("p a b -> p (a b)"), ['gg:2', 'gg:3']),
                  (o_pool[:, :, :].rearrange("p a b -> p (a b)"), ['o_pool:0', 'o_pool:1']),
                  (o_xa[:, :, :].rearrange("p a b -> p (a b)"), ['o_xa:0', 'o_xa:1'])]

        def run_pass(pidx, kind, sti):
            if kind == 's':
                nseq, T, N, ntt, tp = 16, 4, 64, 1, 64
                xsrc = [I['xs']]
                ydst = [O['y_s']]
            else:
                nseq, T, N, ntt, tp = 1, 512, 512, 4, 128
                xsrc = [I['xp'][sti * 512 + j * 128: sti * 512 + (j + 1) * 128, :] for j in range(4)]
                ydst = [O['y_p'][sti * 512 + j * 128: sti * 512 + (j + 1) * 128, :] for j in range(4)]
            last = (kind == 'p' and sti == 3)
            WU = 15 + T
            WX = 3 + T
            WG = 2 + T

            def fv(ap2d):
                return ap2d.rearrange("p (s t) -> p s t", t=T)

            def hv(buf, c, W, lo, hi, pr=slice(0, 128)):
                b2 = buf[pr, c, 0:nseq * W] if c is not None else buf[pr, 0:nseq * W]
                return b2.rearrange("p (s w) -> p s w", w=W)[:, :, lo:hi]

            if defer_store[0]:
                defer_store[0] = False
                for fn_ in pending_store:
                    fn_()
            if kind == 'p' and sti == 0:
                mset('pool', upool[:, :, 0:15], 0.0, ['upool:0', 'upool:1'])
                mset('pool', xrnn[:, :, 0:3], 0.0, ['xrnn:%d' % c for c in range(4)])
            for j in range(ntt):
                P.dma('sp', xtok[:tp, j, :], xsrc[j], (), ['xtok:%d' % j])
            if kind == 's':
                for n_ in range(8):
                    if n_ < 2:
                        ap_, wk_ = kvf[n_][:, :, :], ['kvf:%d' % n_]
                    else:
                        t_ = 1 + (n_ - 2) // 2
                        h_ = (n_ - 2) % 2
                        ap_ = xtok[:, t_, h_ * 512:(h_ + 1) * 512].rearrange("p (a b) -> p a b", b=256)
                        wk_ = ['kvf:%d' % n_, 'xtok:%d' % t_]
                    P.dma('sp', ap_, I['ck'][n_].rearrange("(mt p) n -> p mt n", p=128), (), wk_)
            if pidx in norm1_done:
                xin_buf, xin_key = mixT, 'mixT'
            else:
                xin_buf, xin_key = xnT, 'xnT'
                if kind == 'p':
                    norm_and_transpose(ntt, tp, 'g_mix_norm', srcs=xstage)
                else:
                    norm_and_transpose(ntt, tp, 'g_mix_norm')
            for half in range(2):
                s = gu_slot_of[(pidx, 'in%d' % half)]
                wv = gu[s][:, 0:6144].rearrange("p (k n) -> p k n", n=768)
                for ml in range(6):
                    m = half * 6 + ml
                    b = bank()
                    for k in range(8):
                        mm(psb[b][:, 0:N], wv[:, k, ml * 128:(ml + 1) * 128], xin_buf[:, k, 0:N], k == 0, k == 7,
                           ['gu:%d' % s, 'gub:%d' % s, '%s:%d' % (xin_key, k)], ['ps:%d' % b])
                    pk = ['ps:%d' % b]
                    if m < 2:
                        act(hv(upool, m, WU, 15, 15 + T), fv(psb[b][:, 0:N]), AF.Copy, pk, ['upool:%d' % m])
                    elif m < 6:
                        act(hv(xrnn, m - 2, WX, 3, 3 + T), fv(psb[b][:, 0:N]), AF.Copy, pk, ['xrnn:%d' % (m - 2)])
                    elif m < 10:
                        act(gg[:, m - 6, 0:N], psb[b][:, 0:N], AF.Gelu_apprx_tanh, pk, ['gg:%d' % (m - 6)])
                    else:
                        cp('dve', qb[:, m - 10, 0:N], psb[b][:, 0:N], pk, ['qb:%d' % (m - 10)])
                if half == 0:
                    ffn_cast()
                gu_prefetch()

            groups = {0: ([(o_pool, 0, 'o_pool:0'), (o_pool, 1, 'o_pool:1')], 256, 0),
                      1: ([(gg, c, 'gg:%d' % c) for c in range(4)], 512, 2),
                      2: ([(o_xa, 0, 'o_xa:0'), (o_xa, 1, 'o_xa:1')], 256, 6)}
            sqi_ = [0]

            def rms_group(g):
                chunks, nfe, mi = groups[g]
                b = bank()
                for i, (buf, c, key) in enumerate(chunks):
                    sqb = sq[sqi_[0] % 2]
                    sqk = 'sq:%d' % (sqi_[0] % 2)
                    sqi_[0] += 1
                    act(sqb[:, 0:N], buf[:, c, 0:N], AF.Square, [key], [sqk])
                    mm(psb[b][:, 0:N], ones_b[:, :], sqb[:, 0:N], i == 0, i == len(chunks) - 1, ['ones_b', sqk], ['ps:%d' % b])
                tsc('dve', rbc[:, g, 0:N], psb[b][:, 0:N], 1.0 / nfe, EPS, ALU.mult, ALU.add, ['ps:%d' % b], ['rbc:%d' % g])
                rsqrt_inplace(rbc[:, g, 0:N], 'rbc:%d' % g)
                for (buf, c, key) in chunks:
                    stt('dve', mixT[:, mi, 0:N], buf[:, c, 0:N], pvc('g_mix_out', mi), rbc[:, g, 0:N], ALU.mult, ALU.mult,
                        [key, 'pv', 'rbc:%d' % g], ['mixT:%d' % mi])
                    mi += 1

            if kind == 'p':
                att_pending = []
                for ch in range(2):
                    bo = bank()
                    held.add(bo)
                    bd_ = bank()
                    held.add(bd_)
                    att_pending.append((ch, bo, bd_))
                    for e_ in range(2):
                        h = 2 * ch + e_
                        pr = slice(e_ * 64, e_ * 64 + 64)
                        pt = PT[h % 2]
                        ptk = 'PT:%d' % (h % 2)
                        for mt in range(2):
                            b = bank()
                            mm(psb[b][:, 0:N], KT[pr, ch, mt * 128:(mt + 1) * 128], qb[pr, ch, 0:N], True, True,
                               ['KT', 'qb:%d' % ch], ['ps:%d' % b])
                            act(pt[:, mt, 0:N], psb[b][:, 0:N], AF.Exp, ['ps:%d' % b], [ptk], scale=0.125)
                        for mt in range(2):
                            first = (e_ == 0 and mt == 0)
                            lastm = (e_ == 1 and mt == 1)
                            mm(psb[bo][:, 0:N], Vz[:, h, mt, :], pt[:, mt, 0:N], first, lastm, ['Vz', ptk], ['ps:%d' % bo])
                            mm(psb[bd_][:, 0:N], onesz[:, e_, :], pt[:, mt, 0:N], first, lastm, ['onesz', ptk], ['ps:%d' % bd_])

                def att_partB():
                    for (ch, bo, bd_) in att_pending:
                        P.op('dve', (lambda bd_=bd_: lambda e: e.reciprocal(out=rd[:, 0:N], in_=psb[bd_][:, 0:N]))(), ['ps:%d' % bd_], ['rd'])
                        tt('dve', o_xa[:, ch, 0:N], psb[bo][:, 0:N], rd[:, 0:N], ALU.mult, ['ps:%d' % bo, 'rd'], ['o_xa:%d' % ch])
                        held.discard(bo)
                        held.discard(bd_)
                    rms_group(2)
            else:
                bs = [bank(), bank()]
                NSLOT = 8

                def kvslot(n):
                    i = n % NSLOT
                    if i < 2:
                        return kvf[i][:, :, :], 'kvf:%d' % i, None
                    t_ = 1 + (i - 2) // 2
                    h_ = (i - 2) % 2
                    return (xtok[:, t_, h_ * 512:(h_ + 1) * 512].rearrange("p (a b) -> p a b", b=256), 'kvf:%d' % i, 'xtok:%d' % t_)

                def kvload(n):
                    ap, key, xk_ = kvslot(n)
                    src = I['ck'][n] if n < 16 else I['cv'][n - 16]
                    wk = [key] + ([xk_] if (xk_ is not None and n < NSLOT) else [])
                    P.dma('sp', ap, src.rearrange("(mt p) n -> p mt n", p=128), (), wk)

                def kprep(bb):
                    ap, kk, _ = kvslot(bb)
                    kb16 = Vzs[bb % 2][:, :, :, :].rearrange("p a b c -> p (a b c)")[:, 0:512].rearrange("p (a b) -> p a b", b=256)
                    vzk = 'Vzs:%d' % (bb % 2)
                    cp('pool', kb16, ap, [kk], [vzk])
                    pb = ptrs[bb % 2]
                    pk = 'ptr%d' % (bb % 2)
                    for ch in range(2):
                        for mt in range(2):
                            tr(pb[:, (ch * 2 + mt) * 128:(ch * 2 + mt + 1) * 128], kb16[:, mt, ch * 128:(ch + 1) * 128],
                               ident[:], [vzk, 'ident'], [pk])
                    dst = KTs[bb % 2][:, :, :].rearrange("p a b -> p (a b)")
                    if bb % 2 == 0:
                        act(dst, pb[:, 0:512], AF.Copy, [pk], ['KTs:%d' % (bb % 2)])
                    else:
                        cp('dve', dst, pb[:, 0:512], [pk], ['KTs:%d' % (bb % 2)])
                    kvload(bb + NSLOT)

                def kscores(bb):
                    cols = slice(bb * 4, bb * 4 + 4)
                    for e_ in range(2):
                        pr = slice(e_ * 64, e_ * 64 + 64)
                        for ch in range(2):
                            for mt in range(2):
                                c0 = (mt * 2 + ch) * 64 + bb * 4
                                mm(psb[bs[e_]][:, c0:c0 + 4], KTs[bb % 2][pr, ch, mt * 128:(mt + 1) * 128], qb[pr, ch, cols], True, True,
                                   ['KTs:%d' % (bb % 2), 'qb:%d' % ch], ['ps:%d' % bs[e_]])

                kprep(0)
                for bb in range(16):
                    if bb + 1 < 16:
                        kprep(bb + 1)
                    kscores(bb)
                for e_ in range(2):
                    act(PT[1][:, e_, 0:256], psb[bs[e_]][:, 0:256], AF.Exp, ['ps:%d' % bs[e_]], ['PT:1'], scale=0.125)
                bo = [bank(), bank()]
                bdn = [bank(), bank()]

                def vb16(bb):
                    return Vzs[bb % 2][:, :, :, :].rearrange("p a b c -> p (a b c)")[:, 0:512].rearrange("p (a b) -> p a b", b=256)

                def vprep(bb):
                    ap, kk, _ = kvslot(16 + bb)
                    if bb % 2 == 0:
                        act(vb16(bb), ap, AF.Copy, [kk], ['Vzs:%d' % (bb % 2)])
                    else:
                        cp('dve', vb16(bb), ap, [kk], ['Vzs:%d' % (bb % 2)])
                    if 16 + bb + NSLOT < 32:
                        kvload(16 + bb + NSLOT)

                def pvmm(bb):
                    vk = 'Vzs:%d' % (bb % 2)
                    vb = vb16(bb)
                    for ch in range(2):
                        for e_ in range(2):
                            oc = slice(e_ * 64 + bb * 4, e_ * 64 + bb * 4 + 4)
                            for mt in range(2):
                                c0 = (mt * 2 + ch) * 64 + bb * 4
                                mm(psb[bo[ch]][:, oc], vb[:, mt, ch * 128:(ch + 1) * 128], PT[1][:, e_, c0:c0 + 4], mt == 0, mt == 1,
                                   [vk, 'PT:1'], ['ps:%d' % bo[ch]])
                                mm(psb[bdn[ch]][:, oc], ones_b[:, :], PT[1][:, e_, c0:c0 + 4], mt == 0, mt == 1,
                                   ['ones_b', 'PT:1'], ['ps:%d' % bdn[ch]])

                vprep(0)
                for bb in range(16):
                    if bb + 1 < 16:
                        vprep(bb + 1)
                    pvmm(bb)
                for ch in range(2):
                    for e_ in range(2):
                        pr = slice(e_ * 64, e_ * 64 + 64)
                        P.op('dve', (lambda ch=ch, e_=e_, pr=pr: lambda e: e.reciprocal(
                            out=rd[pr, e_ * 64:(e_ + 1) * 64], in_=psb[bdn[ch]][pr, e_ * 64:(e_ + 1) * 64]))(), ['ps:%d' % bdn[ch]], ['rd'])
                        tt('dve', o_xa[pr, ch, 0:N], psb[bo[ch]][pr, e_ * 64:(e_ + 1) * 64], rd[pr, e_ * 64:(e_ + 1) * 64], ALU.mult,
                           ['ps:%d' % bo[ch], 'rd'], ['o_xa:%d' % ch])

                def att_partB():
                    rms_group(2)

            for c in range(2):
                uk = 'upool:%d' % c
                tt('dve', hv(sA, c, WU, 1, WU), hv(upool, c, WU, 1, WU), hv(upool, c, WU, 0, WU - 1), ALU.add, [uk], ['sA:%d' % c])
            hi = slice(64, 128)
            lo = slice(0, 64)
            tt('dve', hv(sB, 0, WU, 3, WU, hi), hv(sA, 0, WU, 3, WU, hi), hv(sA, 0, WU, 1, WU - 2, hi), ALU.add, ['sA:0'], ['sB:0'])
            tt('dve', hv(sB, 1, WU, 3, WU), hv(sA, 1, WU, 3, WU), hv(sA, 1, WU, 1, WU - 2), ALU.add, ['sA:1'], ['sB:1'])
            tt('dve', hv(sC, None, WU, 7, WU), hv(sB, 1, WU, 7, WU), hv(sB, 1, WU, 3, WU - 4), ALU.add, ['sB:1'], ['sC'])
            tt('dve', hv(sA, 1, WU, 15, WU, hi), hv(sC, None, WU, 15, WU, hi), hv(sC, None, WU, 7, WU - 8, hi), ALU.add, ['sC'], ['sA:1'])
            fin_src = [(0, lo, sA, 0, 'sA:0', 2), (0, hi, sB, 0, 'sB:0', 4), (1, lo, sC, None, 'sC', 8), (1, hi, sA, 1, 'sA:1', 16)]
            for (c, pr, buf, bc, bkey, w) in fin_src:
                stt('dve', fv(diff[pr, c, 0:N]), hv(buf, bc, WU, 15, WU, pr), 1.0 / w, hv(upool, c, WU, 15, WU, pr),
                    ALU.mult, ALU.subtract, [bkey, 'upool:%d' % c], ['diff:%d' % c])
            if kind == 'p' and sti == 0:
                for (c, pr, buf, bc, bkey, w) in fin_src:
                    src = buf[pr, bc, 15:30] if bc is not None else buf[pr, 15:30]
                    tt('dve', t16[pr, 0:15], src, rc15[pr, c, :], ALU.mult, [bkey, 'rc15'], ['t16'])
                    tt('dve', diff[pr, c, 0:15], t16[pr, 0:15], upool[pr, c, 15:30], ALU.subtract, ['t16', 'upool:%d' % c], ['diff:%d' % c])
            for c in range(2):
                b = bank()
                mm(psb[b][:, 0:N], bdw[:, c, :], diff[:, c, 0:N], True, True, ['bdw', 'diff:%d' % c], ['ps:%d' % b])
                act(o_pool[:, c, 0:N], psb[b][:, 0:N], AF.Identity, ['ps:%d' % b, 'pv'], ['o_pool:%d' % c], scale=pvc('pool_scale', c))
            if kind == 's':
                for c in range(2):
                    cp('pool', stg[:, 0:64].rearrange("p (s t) -> p s t", t=4), hv(upool, c, WU, 15, 19), ['upool:%d' % c], ['stg'])
                    rows_out(stg[:, 0:64], 64, ['stg'],
                             [(O['pool_s'][:, 11 + t, c * 128:(c + 1) * 128], t, 4, 16) for t in range(4)])
            else:
                if last:
                    cp('pool', stg[:, 0:30].rearrange("p (c r) -> p c r", r=15), upool[:, :, 512:527], ['upool:0', 'upool:1'], ['stg'])
                    rows_out(stg[:, 0:30], 30, ['stg'],
                             [(O['pool_p'][:, c * 128:(c + 1) * 128], c * 15, 1, 15) for c in range(2)])
                else:
                    for c in range(2):
                        cp('pool', upool[:, c, 0:15], upool[:, c, 512:527], ['upool:%d' % c], ['upool:%d' % c])

            att_partB()
            rms_group(0)
            ffn_cast()
            RGB = {
                0: dict(xc=(S[0]['xc'][:, 0:512], 'S0_xc'), xcb=(S[0]['xcb'][:, 0:512], 'S0_xcb'), thr=(S[0]['thr'][:, 0:512], 'S0_thr'),
                        thi=(S[0]['thi'][:, 0:512], 'S0_thi'), a2=(S[0]['a2'][:, 0:512], 'S0_a2')),
                1: dict(xc=(S[1]['xc'][:, 0:512], 'S1_xc'), xcb=(S[1]['xcb'][:, 0:512], 'S1_xcb'), thr=(S[1]['thr'][:, 0:512], 'S1_thr'),
                        thi=(S[1]['thi'][:, 0:512], 'S1_thi'), a2=(S[1]['a2'][:, 0:512], 'S1_a2')),
                2: dict(xc=(rd[:, 0:512], 'rd'), xcb=(diff[:, 0, :], 'diff:0'), thr=(sA[:, 0, 0:512], 'sA:0'),
                        thi=(sA[:, 1, 0:512], 'sA:1'), a2=(o_pool[:, 0, :], 'o_pool:0')),
                3: dict(xc=(sC[:, 0:512], 'sC'), xcb=(diff[:, 1, :], 'diff:1'), thr=(sB[:, 0, 0:512], 'sB:0'),
                        thi=(sB[:, 1, 0:512], 'sB:1'), a2=(o_pool[:, 1, :], 'o_pool:1')),
            }

            def rb(c, n):
                return RGB[c][n][0][:, 0:N]

            def rk(c, n):
                return RGB[c][n][1]

            C4 = range(4)
            for c in C4:
                act(fv(rb(c, 'xc')), hv(xrnn, c, WX, 0, T), AF.Identity, ['xrnn:%d' % c, 'pv'], [rk(c, 'xc')],
                    scale=pvc('rnn_conv_w', 0 * 4 + c), bias=pvc('rnn_conv_b', c))
            for c in C4:
                xcv = fv(rb(c, 'xc'))
                for tap in range(1, 4):
                    stt('dve', xcv, hv(xrnn, c, WX, tap, tap + T), pvc('rnn_conv_w', tap * 4 + c), xcv, ALU.mult, ALU.add,
                        ['xrnn:%d' % c, 'pv', rk(c, 'xc')], [rk(c, 'xc')])
            for c in C4:
                cp('pool', rb(c, 'xcb'), rb(c, 'xc'), [rk(c, 'xc')], [rk(c, 'xcb')])
                ffn_cast()
                if kind == 'p' and not last:
                    cp('pool', xrnn[:, c, 0:3], xrnn[:, c, 512:515], ['xrnn:%d' % c], ['xrnn:%d' % c])
            for pair in range(2):
                cs = [2 * pair, 2 * pair + 1]
                gb = {}
                for c in cs:
                    ba = bank()
                    mm(psb[ba][:, 0:N], bdw[:, 2 + c, :], rb(c, 'xcb'), True, True, ['bdw', rk(c, 'xcb')], ['ps:%d' % ba])
                    bx = bank()
                    mm(psb[bx][:, 0:N], bdw[:, 6 + c, :], rb(c, 'xcb'), True, True, ['bdw', rk(c, 'xcb')], ['ps:%d' % bx])
                    gb[c] = (ba, bx)
                for c in cs:
                    ba, bx = gb[c]
                    act(rb(c, 'thr'), psb[ba][:, 0:N], AF.Tanh, ['ps:%d' % ba, 'pv'], [rk(c, 'thr')], scale=0.5, bias=pv[:, PV_HBA + c:PV_HBA + c + 1])
                    act(rb(c, 'thi'), psb[bx][:, 0:N], AF.Tanh, ['ps:%d' % bx, 'pv'], [rk(c, 'thi')], scale=0.5, bias=pv[:, PV_HBX + c:PV_HBX + c + 1])
            for c in C4:
                hc_ = pv[:, PV_HC + c:PV_HC + c + 1]
                c2_ = pv[:, PV_C2 + c:PV_C2 + c + 1]
                act(rb(c, 'a2'), rb(c, 'thr'), AF.Exp, [rk(c, 'thr'), 'pv'], [rk(c, 'a2')], scale=c2_, bias=c2_)
                act(rb(c, 'thr'), rb(c, 'thr'), AF.Exp, [rk(c, 'thr'), 'pv'], [rk(c, 'thr')], scale=hc_, bias=hc_)
            for c in C4:
                tsc('dve', rb(c, 'a2'), rb(c, 'a2'), -1.0, 1.0 + 1.2e-7, ALU.mult, ALU.add, [rk(c, 'a2')], [rk(c, 'a2')])
                stt('dve', rb(c, 'thi'), rb(c, 'thi'), 1.0, rb(c, 'xc'), ALU.add, ALU.mult, [rk(c, 'thi'), rk(c, 'xc')], [rk(c, 'thi')])
            for c in C4:
                act(rb(c, 'a2'), rb(c, 'a2'), AF.Ln, [rk(c, 'a2')], [rk(c, 'a2')])
                act(rb(c, 'a2'), rb(c, 'a2'), AF.Exp, [rk(c, 'a2')], [rk(c, 'a2')], scale=0.5)
            for c in C4:
                stt('dve', rb(c, 'thi'), rb(c, 'thi'), 0.5, rb(c, 'a2'), ALU.mult, ALU.mult, [rk(c, 'thi'), rk(c, 'a2')], [rk(c, 'thi')])
                if kind == 's':
                    a0 = fv(rb(c, 'thr'))[:, :, 0]
                    b0 = fv(rb(c, 'thi'))[:, :, 0]
                    tt('dve', t16[:, 0:16], a0, h0T[:, c, :], ALU.mult, [rk(c, 'thr'), 'h0T'], ['t16'])
                    tt('dve', b0, b0, t16[:, 0:16], ALU.add, [rk(c, 'thi'), 't16'], [rk(c, 'thi')])
                    mset('dve', a0, 0.0, [rk(c, 'thr')])
                    init = 0.0
                else:
                    init = hcar[:, c:c + 1]
                P.op('dve', (lambda c=c, init=init: lambda e: e.tensor_tensor_scan(
                    out=rb(c, 'a2'), data0=rb(c, 'thr'), data1=rb(c, 'thi'), initial=init,
                    op0=ALU.mult, op1=ALU.add))(), [rk(c, 'thr'), rk(c, 'thi'), 'hcar'], [rk(c, 'a2')])
                if kind == 'p':
                    cp('dve', hcar[:, c:c + 1], rb(c, 'a2')[:, N - 1:N], [rk(c, 'a2')], ['hcar'])
                else:
                    cp('dve', hls[:, c, :], fv(rb(c, 'a2'))[:, :, 3], [rk(c, 'a2')], ['hls'])
                tt('dve', gg[:, c, 0:N], gg[:, c, 0:N], rb(c, 'a2'), ALU.mult, ['gg:%d' % c, rk(c, 'a2')], ['gg:%d' % c])
            if kind == 's':
                for half in range(2):
                    for ci in range(2):
                        c = half * 2 + ci
                        cp('pool', stg[:, ci * 48:(ci + 1) * 48].rearrange("p (s r) -> p s r", r=3), hv(xrnn, c, WX, 4, 7), ['xrnn:%d' % c], ['stg'])
                    dl = []
                    for ci in range(2):
                        c = half * 2 + ci
                        for r in range(3):
                            dl.append((O['conv_s'][:, r, c * 128:(c + 1) * 128], ci * 48 + r, 3, 16))
                    rows_out(stg[:, 0:96], 96, ['stg'], dl)
                cp('pool', stg[:, 0:64].rearrange("p (c s) -> p c s", s=16), hls[:, :, :], ['hls'], ['stg'])
                rows_out(stg[:, 0:64], 64, ['stg'], [(O['h_s'][:, c * 128:(c + 1) * 128], c * 16, 1, 16) for c in range(4)])
            elif last:
                cp('pool', stg[:, 0:12].rearrange("p (c r) -> p c r", r=3), xrnn[:, :, 512:515], ['xrnn:%d' % c for c in range(4)], ['stg'])
                cp('pool', stg[:, 12:16], hcar[:, :], ['hcar'], ['stg'])
                rows_out(stg[:, 0:16], 16, ['stg'],
                         [(O['conv_p'][:, c * 128:(c + 1) * 128], c * 3, 1, 3) for c in range(4)] +
                         [(O['h_p'][:, c * 128:(c + 1) * 128], 12 + c, 1, 1) for c in range(4)])

            rms_group(1)

            s = gu_slot_of[(pidx, 'out')]
            wv = gu[s][:, 0:8192].rearrange("p (k n) -> p k n", n=1024)
            for j in range(ntt):
                for half in range(2):
                    b = bank()
                    for k in range(8):
                        mm(psb[b][:tp, 0:512], mixT[:, k, j * 128:j * 128 + tp], wv[:, k, half * 512:(half + 1) * 512], k == 0, k == 7,
                           ['gu:%d' % s, 'gub:%d' % s, 'mixT:%d' % k], ['ps:%d' % b])
                    xs_ = xtok[:tp, j, half * 512:(half + 1) * 512]
                    tt('dve', xs_, xs_, psb[b][:tp, 0:512], ALU.add, ['xtok:%d' % j, 'ps:%d' % b], ['xtok:%d' % j])
            gu_prefetch()
            if d_next[0] == 0:
                d_prefetch(); d_prefetch()

            if kind == 'p' and sti < 3:
                for j in range(4):
                    P.dma('sp', xstage[j][0], I['xp'][(sti + 1) * 512 + j * 128:(sti + 1) * 512 + (j + 1) * 128, :], (), xstage[j][1])
            if last:
                state_loads()
                st_gen[0] = state_units()
            nu = [None]
            if kind == 'p' and sti < 3:
                nu[0] = norm_units(4, 128, 'g_mix_norm', xstage, mixT, 'mixT')
                norm1_done.add(pidx + 1)
            norm_and_transpose(ntt, tp, 'g_ffn_norm')

            def down_part(sg):
                ds = d_slot_of[(pidx, sg)]
                dv = dn[ds][:, :].rearrange("p (k n) -> p k n", n=1024)
                hb = hT[sg % 2]
                for j in range(ntt):
                    for half in range(2):
                        b = bank()
                        for cc in range(4):
                            mm(psb[b][:tp, 0:512], hb[:, cc, j * 128:j * 128 + tp], dv[:, cc, half * 512:(half + 1) * 512], cc == 0, cc == 3,
                               ['dn:%d' % ds, 'hT:%d' % (sg % 2)], ['ps:%d' % b])
                        xs_ = xtok[:tp, j, half * 512:(half + 1) * 512]
                        tt('dve', xs_, xs_, psb[b][:tp, 0:512], ALU.add, ['xtok:%d' % j, 'ps:%d' % b], ['xtok:%d' % j])
                d_prefetch()

            for sg in range(6):
                s = gu_slot_of[(pidx, 'gu%d' % sg)]
                gv = gu[s][:, 0:4096].rearrange("p (k n) -> p k n", n=512)
                uv = gu[s][:, 4096:8192].rearrange("p (k n) -> p k n", n=512)
                for cc in range(4):
                    c = sg * 4 + cc
                    par = c % 2
                    gk = 'gbuf:%d' % par
                    fk = 'ftmp:%d' % par
                    bg = bank()
                    for k in range(8):
                        mm(psb[bg][:, 0:N], gv[:, k, cc * 128:(cc + 1) * 128], xnT[:, k, 0:N], k == 0, k == 7,
                           ['gu:%d' % s, 'gub:%d' % s, 'xnT:%d' % k], ['ps:%d' % bg])
                    bu = bank()
                    for k in range(8):
                        mm(psb[bu][:, 0:N], uv[:, k, cc * 128:(cc + 1) * 128], xnT[:, k, 0:N], k == 0, k == 7,
                           ['gu:%d' % s, 'gub:%d' % s, 'xnT:%d' % k], ['ps:%d' % bu])
                    gb3 = gbuf[par][:, 0:nseq * WG].rearrange("p (s w) -> p s w", w=WG)
                    if kind == 's':
                        cp('pool', gb3[:, :, 0:2], gcar_s[:, c, :, :], ['gcs:%d' % c], [gk])
                    else:
                        cp('pool', gb3[:, :, 0:2], gcar[:, c:c + 1, :], ['gcar:%d' % c], [gk])
                    act(gb3[:, :, 2:2 + T], fv(psb[bg][:, 0:N]), AF.Copy, ['ps:%d' % bg], [gk])
                    fw_ = PV['ffn_conv_w']
                    ft3 = fv(ftmp[par][:, 0:N])
                    act(ft3, fv(psb[bg][:, 0:N]), AF.Identity, ['ps:%d' % bg, 'pv'], [fk],
                        scale=pv[:, fw_ + 48 + c:fw_ + 48 + c + 1], bias=pvc('ffn_conv_b', c))
                    stt('dve', ft3, gb3[:, :, 1:1 + T], pv[:, fw_ + 24 + c:fw_ + 24 + c + 1], ft3, ALU.mult, ALU.add, [gk, 'pv', fk], [fk])
                    stt('dve', ft3, gb3[:, :, 0:T], pv[:, fw_ + c:fw_ + c + 1], ft3, ALU.mult, ALU.add, [gk, 'pv', fk], [fk])
                    if kind == 's':
                        cp('pool', gcar_s[:, c, :, :], gb3[:, :, T:T + 2], [gk], ['gcs:%d' % c])
                    else:
                        cp('pool', gcar[:, c:c + 1, :], gb3[:, :, T:T + 2], [gk], ['gcar:%d' % c])
                    act(ftmp[par][:, 0:N], ftmp[par][:, 0:N], AF.Gelu_apprx_tanh, [fk], [fk])
                    tt('dve', hT[sg % 2][:, cc, 0:N], ftmp[par][:, 0:N], psb[bu][:, 0:N], ALU.mult, [fk, 'ps:%d' % bu], ['hT:%d' % (sg % 2)])
                    if cc == 1 and sg > 0:
                        down_part(sg - 1)
                    if last and sg >= 1:
                        for _ in range(2):
                            next(st_gen[0], None)
                    if nu[0] is not None and ((sg == 3 and cc >= 1) or sg == 4):
                        next(nu[0], None)
                gu_prefetch()
                if kind == 's':
                    src = gcar_s[:, sg * 4:(sg + 1) * 4, :, :].rearrange("p c s r -> p (c s r)")
                    dl = []
                    for ci in range(4):
                        c = sg * 4 + ci
                        for r in range(2):
                            dl.append((O['ffn_s'][:, r, c * 128:(c + 1) * 128], ci * 32 + r, 2, 16))
                    rows_out(src, 128, ['gcs:%d' % (sg * 4 + ci) for ci in range(4)], dl)
            if nu[0] is not None:
                for _ in nu[0]:
                    pass
            if last:
                for _ in st_gen[0]:
                    pass
            down_part(5)

            if kind == 's':
                pass
            elif last:
                src = gcar[:, :, :].rearrange("p c r -> p (c r)")
                dl = []
                for c in range(24):
                    for r in range(2):
                        dl.append((O['ffn_p'][r:r + 1, c * 128:(c + 1) * 128], c * 2 + r, 1, 1))
                rows_out(src, 48, ['gcar:%d' % c for c in range(24)], dl)

            for j in range(ntt):
                act(xnb[j % 2][:tp, :], xtok[:tp, j, :], AF.Square, ['xtok:%d' % j], ['xnb:%d' % (j % 2), 'ss'], accum=ss[:tp, 4 + j:5 + j])
            tsc('dve', rstd[:tp, 4:4 + ntt], ss[:tp, 4:4 + ntt], 1.0 / 1024, EPS, ALU.mult, ALU.add, ['ss'], ['rstd'])
            rsqrt_inplace(rstd[:tp, 4:4 + ntt], 'rstd')
            for j in range(ntt):
                stt('dve', xtok[:tp, j, :], xtok[:tp, j, :], rstd[:tp, 4 + j:5 + j], gfin[:tp, :], ALU.mult, ALU.mult,
                    ['xtok:%d' % j, 'rstd', 'gfin'], ['xtok:%d' % j])
                P.dma('sp', ydst[j], xtok[:tp, j, :], ['xtok:%d' % j], ())

        st_gen = [None]

        def state_loads():
            stf_a = gg[0:32, :, :].rearrange("p a b -> p (a b)")
            stf_b = rbc[0:32, 0:2, :].rearrange("p a b -> p (a b)")
            P.dma('sp', stf_a, I['st_ffn'][:, 0:2048], (), ['gg:0', 'gg:1', 'gg:2', 'gg:3'])
            P.dma('sp', stf_b, I['st_ffn'][:, 2048:3072], (), ['rbc:0', 'rbc:1'])
            P.dma('sp', o_pool[:, 0, 0:256], I['st_pool'][0:128, :], (), ['o_pool:0'])
            P.dma('sp', o_pool[0:112, 1, 0:256], I['st_pool'][128:240, :], (), ['o_pool:1'])
            P.dma('sp', o_xa[0:48, 0, :], I['st_conv'], (), ['o_xa:0'])
            P.dma('sp', o_xa[0:16, 1, :], I['st_h'], (), ['o_xa:1'])
            P.dma('sp', O['pool_s'][:, 0:11, :], I['st_pool'].rearrange("(b r) c -> b r c", r=15)[:, 4:15, :], (), ())

        def state_units():
            stf_a = gg[0:32, :, :].rearrange("p a b -> p (a b)")
            stf_b = rbc[0:32, 0:2, :].rearrange("p a b -> p (a b)")

            def st_tr(src_ap, nr, src_keys):
                b = bank()
                tr(psb[b][:, 0:nr], src_ap, identf[0:nr, 0:nr], list(src_keys) + ['identf'], ['ps:%d' % b])
                return b

            for c in range(2):
                flat = upool[:, c, 0:16 * 19].rearrange("p (s w) -> p s w", w=19)
                b = st_tr(o_pool[0:128, 0, c * 128:(c + 1) * 128], 128, ['o_pool:0'])
                cp('dve', flat[:, 0:8, 0:15], psb[b][:, 0:120].rearrange("p (s r) -> p s r", r=15), ['ps:%d' % b], ['upool:%d' % c])
                cp('dve', flat[:, 8, 0:8], psb[b][:, 120:128], ['ps:%d' % b], ['upool:%d' % c])
                yield
                b = st_tr(o_pool[0:112, 1, c * 128:(c + 1) * 128], 112, ['o_pool:1'])
                cp('dve', flat[:, 8, 8:15], psb[b][:, 0:7], ['ps:%d' % b], ['upool:%d' % c])
                cp('dve', flat[:, 9:16, 0:15], psb[b][:, 7:112].rearrange("p (s r) -> p s r", r=15), ['ps:%d' % b], ['upool:%d' % c])
                yield
            for c in range(4):
                b = st_tr(o_xa[0:48, 0, c * 128:(c + 1) * 128], 48, ['o_xa:0'])
                cp('dve', xrnn[:, c, 0:16 * 7].rearrange("p (s w) -> p s w", w=7)[:, :, 0:3],
                   psb[b][:, 0:48].rearrange("p (s r) -> p s r", r=3), ['ps:%d' % b], ['xrnn:%d' % c])
                yield
            for c in range(4):
                b = st_tr(o_xa[0:16, 1, c * 128:(c + 1) * 128], 16, ['o_xa:1'])
                cp('dve', h0T[:, c, :], psb[b][:, 0:16], ['ps:%d' % b], ['h0T'])
                yield
            for c in range(24):
                if c < 16:
                    src, sk_ = stf_a[:, c * 128:(c + 1) * 128], ['gg:%d' % (c // 4)]
                else:
                    src, sk_ = stf_b[:, (c - 16) * 128:(c - 15) * 128], ['rbc:%d' % ((c - 16) // 4)]
                b = st_tr(src, 32, sk_)
                act(gcar_s[:, c, :, :], psb[b][:, 0:32].rearrange("p (s r) -> p s r", r=2), AF.Copy, ['ps:%d' % b], ['gcs:%d' % c])
                yield

        for sti in range(4):
            run_pass(sti, 'p', sti)
        run_pass(4, 's', 0)
        P.emit(st)
    return nc


_NC_CACHE = {}


def kernel(**inp):
    f = lambda a: np.ascontiguousarray(np.asarray(a, dtype=np.float32))
    shared = {
        'g_mix_norm': f(inp['g_mix_norm'][0]).reshape(8, 128),
        'w_in': f(inp['w_in'][0]),
        'w_pool': f(inp['w_pool'][0]),
        'pool_scale': f(inp['pool_scale'][0]).reshape(2, 128),
        'rnn_conv_w': f(inp['rnn_conv_w'][0]).reshape(16, 128),
        'rnn_conv_b': f(inp['rnn_conv_b'][0]).reshape(4, 128),
        'w_rg_a': f(inp['w_rg_a'][0]),
        'b_rg_a': f(inp['b_rg_a'][0]).reshape(4, 128),
        'w_rg_x': f(inp['w_rg_x'][0]),
        'b_rg_x': f(inp['b_rg_x'][0]).reshape(4, 128),
        'rg_lambda': f(inp['rg_lambda'][0]).reshape(4, 128),
        'g_mem_norm': f(inp['g_mem_norm'][0]).reshape(8, 128),
        'w_mem_k': f(inp['w_mem_k'][0]),
        'w_mem_v': f(inp['w_mem_v'][0]),
        'g_mix_out': f(inp['g_mix_out'][0]).reshape(8, 128),
        'w_out': f(inp['w_out'][0]),
        'g_ffn_norm': f(inp['g_ffn_norm'][0]).reshape(8, 128),
        'w_ff_gate': f(inp['w_ff_gate'][0]),
        'w_ff_up': f(inp['w_ff_up'][0]),
        'ffn_conv_w': f(inp['ffn_conv_w'][0]).reshape(72, 128),
        'ffn_conv_b': f(inp['ffn_conv_b'][0]).reshape(24, 128),
        'w_ff_down': f(inp['w_ff_down'][0]),
        'g_final': f(inp['g_final']).reshape(1, 1024),
    }
    in_maps = []
    for c in range(NCORES):
        b = slice(16 * c, 16 * c + 16)
        m = dict(shared)
        m['xp'] = f(inp['x_prompt'][c])
        m['xs'] = f(inp['x_sample'][b]).reshape(64, 1024)
        m['mem'] = f(inp['mem_prompt'][c])
        m['st_pool'] = f(inp['state_pool'][0, b]).reshape(240, 256)
        m['st_conv'] = f(inp['state_rnn_conv'][0, b]).reshape(48, 512)
        m['st_h'] = f(inp['state_rnn_h'][0, b])
        m['st_ffn'] = f(inp['state_ffn_conv'][0, b]).reshape(32, 3072)
        m['ck'] = f(inp['cache_mem_k'][0, b]).reshape(16, 256, 256)
        m['cv'] = f(inp['cache_mem_v'][0, b]).reshape(16, 256, 256)
        in_maps.append(m)
    if 'nc' not in _NC_CACHE:
        _NC_CACHE['nc'] = build()
    nc = _NC_CACHE['nc']
    res = run_bass_kernel_spmd(nc, in_maps, core_ids=list(range(NCORES)))
    R = res.results

    def cat(name, shape):
        return np.stack([np.asarray(R[c][name], dtype=np.float32).reshape(shape) for c in range(NCORES)], axis=0)

    y_p = cat('y_p', (2048, 1024))
    y_s = cat('y_s', (16, 4, 1024)).reshape(128, 4, 1024)
    pool_p = cat('pool_p', (15, 256))[None]
    conv_p = cat('conv_p', (3, 512))[None]
    h_p = cat('h_p', (512,))[None]
    ffn_p = cat('ffn_p', (2, 3072))[None]
    mk_p = cat('mk_p', (256, 4, 64))[None]
    mv_p = cat('mv_p', (256, 4, 64))[None]
    pool_s = cat('pool_s', (16, 15, 256)).reshape(1, 128, 15, 256)
    conv_s = cat('conv_s', (16, 3, 512)).reshape(1, 128, 3, 512)
    h_s = cat('h_s', (16, 512)).reshape(1, 128, 512)
    ffn_s = cat('ffn_s', (16, 2, 3072)).reshape(1, 128, 2, 3072)
    return (y_p, y_s, pool_p, conv_p, h_p, ffn_p, mk_p, mv_p, pool_s, conv_s, h_s, ffn_s)
```
